# Optimizing a Trainium2 kernel written in Bass

```python
import math
import jax, jax.numpy as jnp
from jax import lax
import numpy as np

D_MODEL = 1024
BATCH = 8
SEQ = 2048
DEPTH = 1
DEC_BATCH = 8
DEC_SEQ = 8192
PAST_LEN = 128

D_SSM = D_MODEL // 2
SSM_GROUP = 16
N_GROUPS = D_SSM // SSM_GROUP
STATE = 64
D_ATTN = D_MODEL // 2
N_HEADS = 4
HEAD_DIM = D_ATTN // (2 * N_HEADS)
Q_BLOCK = 128
ROPE_THETA = 10000.0
NORM_EPS = 1e-6
SUBLN_EPS = 1e-5
DT_MIN = 1e-3
DT_MAX = 1e-1
N_IN = 2 * D_SSM + 4 * D_ATTN + 2 * D_MODEL

kernel_name = "hybrid_s5_diffattn_gated_encoder"


def _rmsnorm(x, g, eps=NORM_EPS):
    xf = x.astype(jnp.float32)
    return xf * lax.rsqrt(jnp.mean(xf * xf, axis=-1, keepdims=True) + eps) * g.astype(jnp.float32)


def _rotary(x):
    L = x.shape[1]
    half = HEAD_DIM // 2
    inv_freq = 1.0 / (ROPE_THETA ** (jnp.arange(0, half, dtype=jnp.float32) * 2.0 / HEAD_DIM))
    ang = jnp.arange(L, dtype=jnp.float32)[:, None] * inv_freq[None, :]
    cos = jnp.cos(ang)[None, :, None, None, :]
    sin = jnp.sin(ang)[None, :, None, None, :]
    x1, x2 = x[..., :half], x[..., half:]
    return jnp.concatenate([x1 * cos - x2 * sin, x2 * cos + x1 * sin], axis=-1)


def _scan_combine(e1, e2):
    a1, b1 = e1
    a2, b2 = e2
    return (a2 * a1, a2 * b1 + b2)


def _s5_bidirectional(u, a_re, a_im, log_dt, b_re, b_im, c_re, c_im, d_skip):
    f32 = jnp.float32
    Bsz, L, _ = u.shape
    A = lax.complex(a_re.astype(f32), a_im.astype(f32))
    dt = jnp.exp(log_dt.astype(f32))[..., None]
    A_bar = jnp.exp(A * dt)
    B_c = lax.complex(b_re.astype(f32), b_im.astype(f32))
    B_bar = ((A_bar - 1.0) / A)[..., None] * B_c
    C_c = lax.complex(c_re.astype(f32), c_im.astype(f32))
    ug = u.reshape(Bsz, L, N_GROUPS, SSM_GROUP)

    def per_sequence(us):
        bu = jnp.einsum('ngpc,lgc->nlgp', B_bar, us.astype(jnp.complex64))
        a = jnp.broadcast_to(A_bar[:, None], bu.shape)
        _, h_f = lax.associative_scan(_scan_combine, (a[0], bu[0]), axis=0)
        _, h_b = lax.associative_scan(_scan_combine, (a[1], bu[1]), axis=0, reverse=True)
        y = jnp.einsum('gcp,lgp->lgc', C_c[0], h_f) + jnp.einsum('gcp,lgp->lgc', C_c[1], h_b)
        return jnp.real(y)

    y = lax.map(per_sequence, ug).reshape(Bsz, L, D_SSM)
    return y + d_skip.astype(f32) * u


def _diff_attention(q, k, v, lam):
    Bsz, L = q.shape[0], q.shape[1]
    nb = L // Q_BLOCK
    scale = 1.0 / math.sqrt(HEAD_DIM)
    qb = q.reshape(Bsz, nb, Q_BLOCK, N_HEADS, 2, HEAD_DIM).transpose(1, 0, 2, 3, 4, 5)

    def block(qi):
        s = jnp.einsum('bqhtd,bkhtd->bthqk', qi, k).astype(jnp.float32) * scale
        p = jax.nn.softmax(s, axis=-1)
        w = p[:, 0] - lam * p[:, 1]
        return jnp.einsum('bhqk,bkhe->bqhe', w, v)

    o = lax.map(block, qb)
    return o.transpose(1, 0, 2, 3, 4).reshape(Bsz, L, N_HEADS, 2 * HEAD_DIM)


def _layer(x, li, norm_g, w_in, ssm_a_re, ssm_a_im, ssm_log_dt, ssm_b_re, ssm_b_im,
           ssm_c_re, ssm_c_im, ssm_d, w_glu, b_glu, lambda_q1, lambda_k1, lambda_q2,
           lambda_k2, subln_g, w_branch, w_out):
    f32 = jnp.float32
    Bsz, L, _ = x.shape
    h = _rmsnorm(x, norm_g)
    proj = jnp.einsum('bld,dn->bln', h, w_in.astype(f32))
    o = 0
    xs = proj[..., o:o + D_SSM]; o += D_SSM
    zs = proj[..., o:o + D_SSM]; o += D_SSM
    q = proj[..., o:o + D_ATTN]; o += D_ATTN
    k = proj[..., o:o + D_ATTN]; o += D_ATTN
    v = proj[..., o:o + D_ATTN]; o += D_ATTN
    za = proj[..., o:o + D_ATTN]; o += D_ATTN
    gs = proj[..., o:o + D_MODEL]; o += D_MODEL
    ga = proj[..., o:o + D_MODEL]

    ys = _s5_bidirectional(xs, ssm_a_re, ssm_a_im, ssm_log_dt, ssm_b_re, ssm_b_im,
                           ssm_c_re, ssm_c_im, ssm_d)
    ys = jax.nn.gelu(ys)
    ys = ys * jax.nn.sigmoid(jnp.einsum('bld,de->ble', ys, w_glu.astype(f32)) + b_glu.astype(f32))
    ys = ys * jax.nn.silu(zs)

    q = _rotary(q.reshape(Bsz, L, N_HEADS, 2, HEAD_DIM))
    k = _rotary(k.reshape(Bsz, L, N_HEADS, 2, HEAD_DIM))
    v = v.reshape(Bsz, L, N_HEADS, 2 * HEAD_DIM)
    lam_init = 0.8 - 0.6 * math.exp(-0.3 * li)
    lam = (jnp.exp(jnp.sum(lambda_q1.astype(f32) * lambda_k1.astype(f32)))
           - jnp.exp(jnp.sum(lambda_q2.astype(f32) * lambda_k2.astype(f32))) + lam_init)
    ya = _diff_attention(q, k, v, lam)
    ya = _rmsnorm(ya, subln_g, eps=SUBLN_EPS) * (1.0 - lam_init)
    ya = ya.reshape(Bsz, L, D_ATTN) * jax.nn.silu(za)

    wb = w_branch.astype(f32)
    ps = jnp.einsum('bld,de->ble', ys, wb[0])
    pa = jnp.einsum('bld,de->ble', ya, wb[1])
    merged = jax.nn.sigmoid(gs) * ps + jax.nn.sigmoid(ga) * pa
    return x.astype(f32) + jnp.einsum('bld,de->ble', merged, w_out.astype(f32))


def _trunk(x, norm_g, w_in, ssm_a_re, ssm_a_im, ssm_log_dt, ssm_b_re, ssm_b_im,
           ssm_c_re, ssm_c_im, ssm_d, w_glu, b_glu, lambda_q1, lambda_k1, lambda_q2,
           lambda_k2, subln_g, w_branch, w_out, final_g):
    dtype = x.dtype
    h = x.astype(jnp.float32)
    for li in range(DEPTH):
        h = _layer(h, li, norm_g[li], w_in[li], ssm_a_re[li], ssm_a_im[li], ssm_log_dt[li],
                   ssm_b_re[li], ssm_b_im[li], ssm_c_re[li], ssm_c_im[li], ssm_d[li],
                   w_glu[li], b_glu[li], lambda_q1[li], lambda_k1[li], lambda_q2[li],
                   lambda_k2[li], subln_g[li], w_branch[li], w_out[li])
    return _rmsnorm(h, final_g).astype(dtype)


def setup_inputs(seed: int = 0) -> dict:
    key = jax.random.key(seed)
    ks = jax.random.split(key, 24)
    f32 = jnp.float32
    nrm = lambda k, s: jax.random.normal(k, s, f32)
    n_idx = jnp.arange(STATE, dtype=f32)
    a_re = -0.5 + 0.01 * nrm(ks[4], (DEPTH, 2, N_GROUPS, STATE))
    a_im = jnp.pi * n_idx + 0.01 * nrm(ks[5], (DEPTH, 2, N_GROUPS, STATE))
    log_dt = jax.random.uniform(ks[6], (DEPTH, 2, N_GROUPS), f32,
                                math.log(DT_MIN), math.log(DT_MAX))
    b_scale = (2.0 * SSM_GROUP) ** -0.5
    c_scale = (2.0 * STATE) ** -0.5
    return {
        "x_prompt": nrm(ks[0], (BATCH, SEQ, D_MODEL)),
        "x_sample": nrm(ks[1], (DEC_BATCH, DEC_SEQ, D_MODEL)),
        "norm_g": 1.0 + 0.01 * nrm(ks[2], (DEPTH, D_MODEL)),
        "w_in": nrm(ks[3], (DEPTH, D_MODEL, N_IN)) * D_MODEL ** -0.5,
        "ssm_a_re": a_re,
        "ssm_a_im": a_im,
        "ssm_log_dt": log_dt,
        "ssm_b_re": nrm(ks[7], (DEPTH, 2, N_GROUPS, STATE, SSM_GROUP)) * b_scale,
        "ssm_b_im": nrm(ks[8], (DEPTH, 2, N_GROUPS, STATE, SSM_GROUP)) * b_scale,
        "ssm_c_re": nrm(ks[9], (DEPTH, 2, N_GROUPS, SSM_GROUP, STATE)) * c_scale,
        "ssm_c_im": nrm(ks[10], (DEPTH, 2, N_GROUPS, SSM_GROUP, STATE)) * c_scale,
        "ssm_d": nrm(ks[11], (DEPTH, D_SSM)),
        "w_glu": nrm(ks[12], (DEPTH, D_SSM, D_SSM)) * D_SSM ** -0.5,
        "b_glu": 0.01 * nrm(ks[13], (DEPTH, D_SSM)),
        "lambda_q1": 0.1 * nrm(ks[14], (DEPTH, HEAD_DIM)),
        "lambda_k1": 0.1 * nrm(ks[15], (DEPTH, HEAD_DIM)),
        "lambda_q2": 0.1 * nrm(ks[16], (DEPTH, HEAD_DIM)),
        "lambda_k2": 0.1 * nrm(ks[17], (DEPTH, HEAD_DIM)),
        "subln_g": 1.0 + 0.01 * nrm(ks[18], (DEPTH, 2 * HEAD_DIM)),
        "w_branch": nrm(ks[19], (DEPTH, 2, D_SSM, D_MODEL)) * D_SSM ** -0.5,
        "w_out": nrm(ks[20], (DEPTH, D_MODEL, D_MODEL)) * D_MODEL ** -0.5,
        "final_g": 1.0 + 0.01 * nrm(ks[21], (D_MODEL,)),
    }


def reference(x_prompt, x_sample, norm_g, w_in, ssm_a_re, ssm_a_im, ssm_log_dt, ssm_b_re,
              ssm_b_im, ssm_c_re, ssm_c_im, ssm_d, w_glu, b_glu, lambda_q1, lambda_k1,
              lambda_q2, lambda_k2, subln_g, w_branch, w_out, final_g):
    y_prompt = _trunk(x_prompt, norm_g, w_in, ssm_a_re, ssm_a_im, ssm_log_dt, ssm_b_re,
                      ssm_b_im, ssm_c_re, ssm_c_im, ssm_d, w_glu, b_glu, lambda_q1,
                      lambda_k1, lambda_q2, lambda_k2, subln_g, w_branch, w_out, final_g)
    y_sample = _trunk(x_sample, norm_g, w_in, ssm_a_re, ssm_a_im, ssm_log_dt, ssm_b_re,
                      ssm_b_im, ssm_c_re, ssm_c_im, ssm_d, w_glu, b_glu, lambda_q1,
                      lambda_k1, lambda_q2, lambda_k2, subln_g, w_branch, w_out, final_g)
    return (y_prompt, y_sample)
```

```python
from contextlib import ExitStack
import math
import numpy as np
import ml_dtypes
import concourse.bass as bass
import concourse.mybir as mybir
from concourse.bass_utils import run_bass_kernel_spmd

F32 = mybir.dt.float32
BF16 = mybir.dt.bfloat16
I32 = mybir.dt.int32
AF = mybir.ActivationFunctionType
ALU = mybir.AluOpType

NDMA = 24
ENGS = ("pe", "act", "dve", "pool", "sp")
N_CORES = 8
D = 1024
NEXT = 5120
ARENA_WORDS = 49152


class Res:
    __slots__ = ("name", "w", "r")

    def __init__(self, name=""):
        self.name = name
        self.w = None
        self.r = {}


class Prog:
    def __init__(self):
        self.ops = {e: [] for e in ENGS}
        self.cnt = {e: 0 for e in ENGS}
        self.waited = {e: {} for e in ENGS}
        self.dma_val = [0] * NDMA
        self.dma_next = 0

    def _deps(self, reads, writes):
        deps = {}
        for r in reads:
            if r.w is not None and deps.get(r.w[0], 0) < r.w[1]:
                deps[r.w[0]] = r.w[1]
        for w in writes:
            if w.w is not None and deps.get(w.w[0], 0) < w.w[1]:
                deps[w.w[0]] = w.w[1]
            for s, v in w.r.items():
                if deps.get(s, 0) < v:
                    deps[s] = v
        return deps

    def _waits(self, eng, deps):
        waits = []
        wd = self.waited[eng]
        for s, v in deps.items():
            if s == "pe" and eng == "pe":
                continue
            if wd.get(s, 0) >= v:
                continue
            wd[s] = v
            waits.append((s, v))
        return waits

    def _mark(self, tok, reads, writes):
        for r in reads:
            if r.r.get(tok[0], 0) < tok[1]:
                r.r[tok[0]] = tok[1]
        for w in writes:
            w.w = tok
            w.r = {}

    def op(self, eng, fn, reads=(), writes=()):
        waits = self._waits(eng, self._deps(reads, writes))
        n = self.cnt[eng] + 1
        self.cnt[eng] = n
        self._mark((eng, n), reads, writes)
        self.ops[eng].append((waits, fn, (eng, 1)))

    def dma(self, q, fn, reads=(), writes=()):
        s = self.dma_next
        self.dma_next = (s + 1) % NDMA
        key = ("dma", s)
        deps = self._deps(reads, writes)
        if self.dma_val[s] > 0:
            deps[key] = max(deps.get(key, 0), self.dma_val[s])
        waits = self._waits(q, deps)
        self.dma_val[s] += 16
        self._mark((key, self.dma_val[s]), reads, writes)
        self.ops[q].append((waits, fn, (key, 16)))

    def barrier(self):
        deps = {("dma", s): v for s, v in enumerate(self.dma_val) if v > 0}
        for e in ENGS:
            if e != "sp" and self.cnt[e] > 0:
                deps[e] = self.cnt[e]
        for q in ENGS:
            d = {k: v for k, v in deps.items() if k != q}
            waits = self._waits(q, d)
            if waits:
                self.ops[q].append((waits, None, None))

    def finish(self):
        for q in ("sp",):
            deps = {("dma", s): v for s, v in enumerate(self.dma_val) if v > 0}
            for e in ENGS:
                if e != q and e != "sp" and self.cnt[e] > 0:
                    deps[e] = self.cnt[e]
            waits = self._waits(q, deps)
            self.ops[q].append((waits, None, None))

    def emit(self, nc, stack):
        sems = {}
        for e in ENGS:
            sems[e] = stack.enter_context(nc.semaphore("s_" + e))
        for s in range(NDMA):
            sems[("dma", s)] = stack.enter_context(nc.semaphore("s_dma%d" % s))
        block = stack.enter_context(nc.Block())
        meth = {"pe": block.tensor, "act": block.scalar, "dve": block.vector,
                "pool": block.gpsimd, "sp": block.sync}
        for e in ENGS:
            ops = self.ops[e]

            def body(eng, ops=ops):
                for waits, fn, inc in ops:
                    for s, v in waits:
                        eng.wait_ge(sems[s], v)
                    if fn is not None:
                        ins = fn(eng)
                        if inc is not None:
                            ins.then_inc(sems[inc[0]], inc[1])
            meth[e](body)


class Arena:
    def __init__(self, ap, cap):
        self.ap = ap
        self.cap = cap
        self.off = 0

    def f32(self, n):
        off = self.off
        self.off += n
        assert self.off <= self.cap, ("arena overflow", self.off, self.cap)
        return self.ap[:, off:off + n]

    def bf16(self, n):
        w = (n + 1) // 2
        return self.f32(w).bitcast(BF16)[:, :n]


class KB:
    def __init__(self, P):
        self.P = P

    def mm(self, out, lhsT, rhs, start, stop, R, W):
        self.P.op("pe", lambda e: e.matmul(out, lhsT, rhs, start=start, stop=stop), R, W)

    def tr(self, out, in_, ident, R, W):
        self.P.op("pe", lambda e: e.transpose(out, in_, ident), R, W)

    def act(self, out, in_, func, R, W, bias=None, scale=None, accum_out=None):
        kw = {}
        if bias is not None:
            kw["bias"] = bias
        if scale is not None:
            kw["scale"] = scale
        if accum_out is not None:
            kw["accum_out"] = accum_out
        self.P.op("act", lambda e: e.activation(out=out, in_=in_, func=func, **kw), R, W)

    def ts(self, eng, out, in0, s1, s2, op0, op1, R, W):
        if op1 is None:
            self.P.op(eng, lambda e: e.tensor_scalar(out=out, in0=in0, scalar1=s1, scalar2=None, op0=op0), R, W)
        else:
            self.P.op(eng, lambda e: e.tensor_scalar(out=out, in0=in0, scalar1=s1, scalar2=s2, op0=op0, op1=op1), R, W)

    def tt(self, eng, out, in0, in1, op, R, W):
        self.P.op(eng, lambda e: e.tensor_tensor(out=out, in0=in0, in1=in1, op=op), R, W)

    def stt(self, out, in0, scalar, in1, op0, op1, R, W, accum_out=None):
        if accum_out is None:
            self.P.op("dve", lambda e: e.scalar_tensor_tensor(out=out, in0=in0, scalar=scalar, in1=in1, op0=op0, op1=op1), R, W)
        else:
            self.P.op("dve", lambda e: e.scalar_tensor_tensor(out=out, in0=in0, scalar=scalar, in1=in1, op0=op0, op1=op1,
                                                             accum_out=accum_out), R, W)

    def copy(self, eng, out, in_, R, W):
        if eng == "act":
            self.P.op("act", lambda e: e.activation(out=out, in_=in_, func=AF.Copy), R, W)
        else:
            self.P.op(eng, lambda e: e.tensor_copy(out=out, in_=in_), R, W)

    def memset(self, eng, ap, val, W):
        self.P.op(eng, lambda e: e.memset(ap, val), (), W)

    def recip(self, out, in_, R, W):
        self.P.op("dve", lambda e: e.reciprocal(out=out, in_=in_), R, W)

    def dma(self, q, out, in_, R, W, slow=False):
        q = "sp"
        if slow:
            self.P.dma(q, lambda e: e.dma_start(out=out, in_=in_, allow_slow_non_contiguous=True), R, W)
        else:
            self.P.dma(q, lambda e: e.dma_start(out=out, in_=in_), R, W)


class _Stop(Exception):
    pass


def build(L0, L1, debug=False, phases="ABTC", stop=0):
    Ls = [L0, L1]
    NTOK = L0 + L1
    LMAX = max(Ls)
    seq_off = [0, L0]
    nc = bass.Bass("TRN2", target_bir_lowering=False)

    def din(name, shape, dt=F32):
        return nc.dram_tensor(name, list(shape), dt, kind="ExternalInput").ap()

    skind = "ExternalOutput" if debug else "Internal"

    def dscr(name, shape, dt=BF16):
        return nc.dram_tensor(name, list(shape), dt, kind=skind).ap()

    x_in = din("x", [NTOK, D])
    w_in = din("w_in", [D, NEXT])
    ng_in = din("norm_g", [128, 8])
    ropeC = din("ropeC", [128, LMAX])
    ropeS = din("ropeS", [128, LMAX])
    ident_in = din("ident", [128, 128])
    jswap_in = din("jswap", [128, 128])
    pswap_in = din("pswap", [128, 128], BF16)
    esel_in = din("esel", [128, 64 * 128], BF16)
    maskf_in = din("maskf", [128, 128])
    maskb_in = din("maskb", [128, 128])
    a_re_in = din("a_re", [128, 64])
    a_im_in = din("a_im", [128, 64])
    ldt_in = din("log_dt", [128, 64])
    b_re_in = din("b_re", [128, 64 * 16])
    b_im_in = din("b_im", [128, 64 * 16])
    c_re_in = din("c_re", [128, 64 * 16])
    c_im_in = din("c_im", [128, 64 * 16])
    dsk_in = din("dskip", [128, 32])
    wglu_in = din("w_glu", [512, 512])
    bglu_in = din("b_glu", [128, 4])
    wbr_in = din("w_branch", [2, 512, 1024])
    wout_in = din("w_out", [1024, 1024])
    fg_in = din("final_g", [128, 1024])
    sg_in = din("subln_g", [128, 1])
    lam_in = din("lam", [128, 4 * 64])
    y_out = nc.dram_tensor("y", [NTOK, D], F32, kind="ExternalOutput").ap()

    xsT = dscr("xsT", [512, 8, NTOK // 8])
    szsT = dscr("szsT", [512, NTOK])
    qT = dscr("qT", [512, NTOK])
    kT = dscr("kT", [512, NTOK])
    vS = dscr("vS", [NTOK, 512])
    szaT = dscr("szaT", [512, NTOK])
    sgsT = dscr("sgsT", [1024, NTOK])
    sgaT = dscr("sgaT", [1024, NTOK])
    yT = dscr("yT", [512, NTOK])
    yaT_d = dscr("yaT", [512, NTOK])

    P = Prog()
    K = KB(P)
    NT = NTOK // 512
    R_xsT = [Res() for _ in range(NT)]
    R_szsT = [Res() for _ in range(NT)]
    R_qT = [Res() for _ in range(NT)]
    R_kT = [Res() for _ in range(NT)]
    R_v = [Res() for _ in range(NT)]
    R_sza = [Res() for _ in range(NT)]
    R_sgs = [Res() for _ in range(NT)]
    R_sga = [Res() for _ in range(NT)]
    R_yT = [Res() for _ in range(2)]
    R_ya = [Res() for _ in range(NT)]

    with ExitStack() as st:
        arena_t = st.enter_context(nc.sbuf_tensor("arena", [128, ARENA_WORDS], F32))
        psum_t = st.enter_context(nc.psum_tensor("psum", [128, 4096], F32))
        PS = psum_t[:, :]

        def bank(b, n=512):
            return PS[:, b * 512:b * 512 + n]

        R_bank = [Res("bank%d" % b) for b in range(8)]

        A = Arena(arena_t[:, :], ARENA_WORDS)
        ident32 = A.f32(128)
        identb = A.bf16(128)
        R_const = Res("const")
        K.dma("sp", ident32, ident_in, (), [R_const])
        K.copy("dve", identb, ident32, [R_const], [R_const])
        pswap = A.bf16(128)
        K.dma("sp", pswap, pswap_in, (), [R_const])
        base_off = A.off

        def ckpt(n):
            if stop == n:
                raise _Stop()

        try:
          if "A" in phases:
              A.off = base_off
              ng = A.f32(8)
              Wp = A.bf16(8 * NEXT).rearrange("p (a n) -> p a n", a=8)
              R_W = Res("W")
              R_ng = Res("ng")
              K.dma("sp", ng, ng_in, (), [R_ng])
              xt_off = A.off
              xt = A.f32(4096).rearrange("p (s d) -> p s d", s=4)
              R_xt = Res("xt")
              xn = A.bf16(4096).rearrange("p (s d) -> p s d", s=4)
              R_xn = Res("xn")
              junk = A.f32(1024)
              R_junk = Res("junk")
              xnT = [A.bf16(4096).rearrange("p (a t) -> p a t", a=8) for _ in range(2)]
              R_xnT = [Res("xnT0"), Res("xnT1")]
              cosT = [A.f32(512) for _ in range(2)]
              sinT = [A.f32(512) for _ in range(2)]
              R_tab = [Res("tab0"), Res("tab1")]
              t1 = [A.f32(512) for _ in range(2)]
              t2 = [A.f32(512) for _ in range(2)]
              R_t1 = [Res("t10"), Res("t11")]
              R_t2 = [Res("t20"), Res("t21")]
              sig = [A.f32(512) for _ in range(2)]
              qsb = [A.bf16(512) for _ in range(2)]
              R_qsb = [Res("qsb0"), Res("qsb1")]
              R_sig = [Res("sig0"), Res("sig1")]
              NSTG = 16
              stg = [A.bf16(512) for _ in range(NSTG)]
              R_stg = [Res("stg%d" % i) for i in range(NSTG)]
              ss = A.f32(4)
              ms = A.f32(4)
              rstd = A.f32(4)
              neghalf = A.f32(4)
              R_ss = Res("ss")
              R_rstd = Res("rstd")
              K.memset("dve", neghalf, -0.5, [R_rstd])
              wst = [arena_t[:, xt_off + h2 * 2048: xt_off + (h2 + 1) * 2048] for h2 in range(2)]
              R_wst = [Res("wst0"), Res("wst1")]
              w_view = w_in.rearrange("(a p) n -> p a n", p=128)
              i = 0
              for dc in range(8):
                  for cb in range(3):
                      sl = i % 2
                      c0_ = cb * 2048
                      cn_ = min(2048, NEXT - c0_)
                      K.dma("sp", wst[sl][:, 0:cn_], w_view[:, dc, c0_:c0_ + cn_], (), [R_wst[sl]])
                      eng = ("dve", "pool")[i % 2]
                      K.ts(eng, Wp[:, dc, c0_:c0_ + cn_], wst[sl][:, 0:cn_], ng[:, dc:dc + 1], None, ALU.mult, None,
                           [R_wst[sl], R_ng], [R_W])
                      i += 1

              for rw in R_wst:
                  for kk, vv in list(rw.r.items()) + ([rw.w] if rw.w else []):
                      if R_xt.r.get(kk, 0) < vv:
                          R_xt.r[kk] = vv
              ckpt(1)
              stg_i = [0]

              def stage_out(dst_ap, R_dst, produce, view3=False):
                  sl = stg_i[0] % NSTG
                  stg_i[0] += 1
                  produce(stg[sl], R_stg[sl])
                  src = stg[sl].rearrange("p (j k) -> p j k", j=8) if view3 else stg[sl]
                  K.dma("pool", dst_ap, src, [R_stg[sl]], [R_dst])

              pb_i = [0]

              def next_bank():
                  b = 4 + (pb_i[0] % 4)
                  pb_i[0] += 1
                  return b

              def load_tile(T):
                  K.dma("sp", xt, x_in[T * 512:(T + 1) * 512, :].rearrange("(s p) d -> p s d", p=128), (), [R_xt])

              def tile_pos(T):
                  tok0 = T * 512
                  sq = 0 if tok0 < L0 else 1
                  return sq, tok0 - seq_off[sq]

              def load_tabs(T):
                  sq, pos0 = tile_pos(T)
                  K.dma("sp", cosT[T % 2], ropeC[:, pos0:pos0 + 512], (), [R_tab[T % 2]])
                  K.dma("sp", sinT[T % 2], ropeS[:, pos0:pos0 + 512], (), [R_tab[T % 2]])

              load_tile(0)
              load_tabs(0)
              for T in range(NT):
                  sl = T % 2
                  for s in range(4):
                      K.stt(junk, xt[:, s, :], 1.0, xt[:, s, :], ALU.mult, ALU.mult, [R_xt], [R_junk, R_ss],
                            accum_out=ss[:, s:s + 1])
                  K.ts("dve", ms, ss, 1.0 / D, 1e-6, ALU.mult, ALU.add, [R_ss], [R_ss])
                  K.tt("pool", rstd, ms, neghalf, ALU.pow, [R_ss, R_rstd], [R_rstd])
                  for s in range(4):
                      K.act(xn[:, s, :], xt[:, s, :], AF.Copy, [R_xt, R_rstd], [R_xn], scale=rstd[:, s:s + 1])
                  if T + 1 < NT:
                      load_tile(T + 1)
                      load_tabs(T + 1)
                  for h in range(2):
                      pst = PS[:, h * 1024:(h + 1) * 1024].bitcast(BF16).rearrange("p (a t) -> p a t", a=4)
                      Rb = [R_bank[2 * h], R_bank[2 * h + 1]]
                      for a in range(4):
                          dc = h * 4 + a
                          for s in range(4):
                              K.tr(pst[:, a, s * 128:(s + 1) * 128], xn[:, s, dc * 128:(dc + 1) * 128], identb,
                                   [R_xn, R_const], Rb)
                      K.copy(("dve", "act")[h], xnT[sl][:, h * 4:(h + 1) * 4, :], pst, Rb, [R_xnT[sl]])
                  ckpt(3)
                  X = xnT[sl]
                  RX = R_xnT[sl]
                  tsl = slice(T * 512, (T + 1) * 512)

                  def fm_proj(blk):
                      b = next_bank()
                      for dc in range(8):
                          K.mm(bank(b), Wp[:, dc, blk * 128:(blk + 1) * 128], X[:, dc, :], dc == 0, dc == 7,
                               [R_W, RX], [R_bank[b]])
                      return b

                  for j in range(4):
                      b = fm_proj(j)
                      stage_out(xsT[j * 128:(j + 1) * 128, :, T * 64:(T + 1) * 64], R_xsT[T],
                                lambda s_ap, s_r, b=b: K.copy("dve", s_ap.rearrange("p (j k) -> p k j", j=8),
                                                              bank(b).rearrange("p (k j) -> p k j", j=8), [R_bank[b]], [s_r]),
                                view3=True)
                  ckpt(4)
                  for j in range(4):
                      b = fm_proj(4 + j)
                      g = j % 2
                      K.act(sig[g], bank(b), AF.Sigmoid, [R_bank[b]], [R_sig[g]])
                      stage_out(szsT[j * 128:(j + 1) * 128, tsl], R_szsT[T],
                                lambda s_ap, s_r, b=b, g=g: K.tt("dve", s_ap, bank(b), sig[g], ALU.mult,
                                                                 [R_bank[b], R_sig[g]], [s_r]))
                  ckpt(5)
                  for (base, dstT, Rd) in ((8, qT, R_qT), (12, kT, R_kT)):
                      for j in range(4):
                          b0 = fm_proj(base + j)
                          g = j % 2
                          K.copy("dve", qsb[g], bank(b0), [R_bank[b0]], [R_qsb[g]])
                          b1 = next_bank()
                          K.mm(bank(b1), pswap, qsb[g], True, True, [R_const, R_qsb[g]], [R_bank[b1]])
                          K.tt("dve", t1[g], bank(b0), cosT[sl], ALU.mult, [R_bank[b0], R_tab[sl]], [R_t1[g]])
                          K.tt("dve", t2[g], bank(b1), sinT[sl], ALU.mult, [R_bank[b1], R_tab[sl]], [R_t2[g]])
                          stage_out(dstT[j * 128:(j + 1) * 128, tsl], Rd[T],
                                    lambda s_ap, s_r, g=g: K.tt("pool", s_ap, t1[g], t2[g], ALU.add, [R_t1[g], R_t2[g]], [s_r]))
                  ckpt(6)
                  for (base, dstT, Rd) in ((16, sgsT, R_sgs), (24, sgaT, R_sga)):
                      for j in range(8):
                          b = fm_proj(base + j)
                          stage_out(dstT[j * 128:(j + 1) * 128, tsl], Rd[T],
                                    lambda s_ap, s_r, b=b: K.act(s_ap, bank(b), AF.Sigmoid, [R_bank[b]], [s_r]))
                  ckpt(7)
                  for j in range(4):
                      b = fm_proj(32 + j)
                      g = j % 2
                      K.act(sig[g], bank(b), AF.Sigmoid, [R_bank[b]], [R_sig[g]])
                      stage_out(szaT[j * 128:(j + 1) * 128, tsl], R_sza[T],
                                lambda s_ap, s_r, b=b, g=g: K.tt("dve", s_ap, bank(b), sig[g], ALU.mult,
                                                                 [R_bank[b], R_sig[g]], [s_r]))
                  for s in range(4):
                      rows = slice(T * 512 + s * 128, T * 512 + (s + 1) * 128)
                      b = next_bank()
                      for dc in range(8):
                          K.mm(bank(b), X[:, dc, s * 128:(s + 1) * 128], Wp[:, dc, 4608:5120], dc == 0, dc == 7,
                               [R_W, RX], [R_bank[b]])
                      stage_out(vS[rows, :], R_v[T],
                                lambda s_ap, s_r, b=b: K.copy("dve", s_ap, bank(b), [R_bank[b]], [s_r]))

          if "B" in phases:
              P.barrier()
              A.off = base_off
              TWO_PI = 2.0 * math.pi
              jsw = A.f32(128)
              mkf = A.f32(128)
              mkb = A.f32(128)
              dsk = A.f32(32)
              ES = A.bf16(64 * 128).rearrange("p (e n) -> p e n", e=64)
              R_cB = Res("constB")
              K.dma("sp", jsw, jswap_in, (), [R_cB])
              K.dma("sp", mkf, maskf_in, (), [R_cB])
              K.dma("sp", mkb, maskb_in, (), [R_cB])
              K.dma("sp", dsk, dsk_in, (), [R_cB])
              K.dma("sp", ES, esel_in.rearrange("p (e n) -> p e n", e=64), (), [R_cB])
              Cst_bf = A.bf16(64 * 128).rearrange("p (e n) -> p e n", e=64)
              Bst_bf = A.bf16(64 * 128).rearrange("p (e n) -> p e n", e=64)
              M_bf = A.bf16(32 * 128).rearrange("p (e n) -> p e n", e=32)
              S1tab = A.f32(10 * 64).rearrange("p (m e) -> p m e", m=10)
              S2tab = A.f32(10 * 64).rearrange("p (m e) -> p m e", m=10)
              R_fin = Res("finB")
              main_off = A.off
              R_s = Res("setupB")
              are = A.f32(64); aim = A.f32(64); ldt = A.f32(64)
              K.dma("sp", are, a_re_in, (), [R_s])
              K.dma("sp", aim, a_im_in, (), [R_s])
              K.dma("sp", ldt, ldt_in, (), [R_s])
              Bre = A.f32(1024).rearrange("p (e c) -> p e c", e=64)
              Bim = A.f32(1024).rearrange("p (e c) -> p e c", e=64)
              Cre = A.f32(1024).rearrange("p (e c) -> p e c", e=64)
              Cim = A.f32(1024).rearrange("p (e c) -> p e c", e=64)
              R_raw = Res("rawB")
              K.dma("sp", Bre, b_re_in.rearrange("p (e c) -> p e c", e=64), (), [R_raw])
              K.dma("sp", Bim, b_im_in.rearrange("p (e c) -> p e c", e=64), (), [R_raw])
              K.dma("sp", Cre, c_re_in.rearrange("p (e c) -> p e c", e=64), (), [R_raw])
              K.dma("sp", Cim, c_im_in.rearrange("p (e c) -> p e c", e=64), (), [R_raw])
              sc = lambda: A.f32(64)
              dtv = sc(); lr = sc(); ph = sc(); mag = sc(); kk = sc(); tmp = sc(); tmp2 = sc()
              r1 = sc(); r2 = sc(); sn = sc(); cs = sc(); minv = sc()
              PWre = A.f32(17 * 64).rearrange("p (n e) -> p n e", n=17)
              PWim = A.f32(17 * 64).rearrange("p (n e) -> p n e", n=17)
              w0re = sc(); w0im = sc(); den = sc()
              Rs, Ws = [R_s], [R_s]

              def v(out, in0, in1, op):
                  K.tt("dve", out, in0, in1, op, Rs, Ws)

              def vs(out, in0, s1, s2, op0, op1=None):
                  K.ts("dve", out, in0, s1, s2, op0, op1, Rs, Ws)

              K.act(dtv, ldt, AF.Exp, Rs, Ws)
              v(lr, are, dtv, ALU.mult)
              v(ph, aim, dtv, ALU.mult)
              K.act(mag, lr, AF.Exp, Rs, Ws)
              vs(kk, ph, math.pi, None, ALU.is_gt)
              for i in range(1, 9):
                  vs(tmp, ph, (2 * i + 1) * math.pi, None, ALU.is_gt)
                  v(kk, kk, tmp, ALU.add)
              vs(tmp, kk, -TWO_PI, None, ALU.mult)
              v(r1, ph, tmp, ALU.add)
              vs(r2, r1, math.pi / 2, None, ALU.add)
              vs(tmp, r2, math.pi, -TWO_PI, ALU.is_gt, ALU.mult)
              v(r2, r2, tmp, ALU.add)
              K.act(sn, r1, AF.Sin, Rs, Ws)
              K.act(cs, r2, AF.Sin, Rs, Ws)
              K.memset("dve", PWre[:, 8, :], 1.0, Ws)
              K.memset("dve", PWim[:, 8, :], 0.0, Ws)
              v(PWre[:, 9, :], mag, cs, ALU.mult)
              v(PWim[:, 9, :], mag, sn, ALU.mult)

              def cmul(ore, oim, are_, aim_, bre_, bim_):
                  v(tmp, are_, bre_, ALU.mult)
                  v(tmp2, aim_, bim_, ALU.mult)
                  v(kk, are_, bim_, ALU.mult)
                  v(den, aim_, bre_, ALU.mult)
                  v(ore, tmp, tmp2, ALU.subtract)
                  v(oim, kk, den, ALU.add)

              for n in range(2, 9):
                  cmul(PWre[:, 8 + n, :], PWim[:, 8 + n, :], PWre[:, 7 + n, :], PWim[:, 7 + n, :], PWre[:, 9, :], PWim[:, 9, :])
              for n in range(1, 9):
                  K.act(minv, lr, AF.Exp, Rs, Ws, scale=-2.0 * n)
                  v(PWre[:, 8 - n, :], PWre[:, 8 + n, :], minv, ALU.mult)
                  vs(tmp, PWim[:, 8 + n, :], -1.0, None, ALU.mult)
                  v(PWim[:, 8 - n, :], tmp, minv, ALU.mult)
              K.copy("dve", S1tab[:, 0, :], PWre[:, 16, :], Rs, Ws)
              K.copy("dve", S2tab[:, 0, :], PWim[:, 16, :], Rs, Ws)
              for m in range(1, 10):
                  v(tmp, S1tab[:, m - 1, :], S1tab[:, m - 1, :], ALU.mult)
                  v(tmp2, S2tab[:, m - 1, :], S2tab[:, m - 1, :], ALU.mult)
                  v(S1tab[:, m, :], tmp, tmp2, ALU.subtract)
                  K.stt(S2tab[:, m, :], S1tab[:, m - 1, :], 2.0, S2tab[:, m - 1, :], ALU.mult, ALU.mult, Rs, Ws)
              vs(tmp, PWre[:, 9, :], -1.0, None, ALU.add)
              v(den, are, are, ALU.mult)
              v(tmp2, aim, aim, ALU.mult)
              v(den, den, tmp2, ALU.add)
              K.recip(den, den, Rs, Ws)
              v(w0re, tmp, are, ALU.mult)
              v(tmp2, PWim[:, 9, :], aim, ALU.mult)
              v(w0re, w0re, tmp2, ALU.add)
              v(w0re, w0re, den, ALU.mult)
              v(w0im, PWim[:, 9, :], are, ALU.mult)
              v(tmp2, tmp, aim, ALU.mult)
              v(w0im, w0im, tmp2, ALU.subtract)
              v(w0im, w0im, den, ALU.mult)
              K.ts("dve", S2tab[64:128], S2tab[64:128], -1.0, None, ALU.mult, None, Rs, [R_s, R_fin])
              Tb = A.f32(1024).rearrange("p (e c) -> p e c", e=64)
              Tb2 = A.f32(1024).rearrange("p (e c) -> p e c", e=64)
              Bpre = A.f32(1024).rearrange("p (e c) -> p e c", e=64)
              Bpim = A.f32(1024).rearrange("p (e c) -> p e c", e=64)
              bc = lambda s_: s_.unsqueeze(2).to_broadcast([128, 64, 16])
              RW = [R_s, R_raw]
              K.tt("dve", Tb, Bre, bc(w0re), ALU.mult, RW, [R_raw])
              K.tt("dve", Tb2, Bim, bc(w0im), ALU.mult, RW, [R_raw])
              K.tt("dve", Bpre, Tb, Tb2, ALU.subtract, RW, [R_raw])
              K.tt("dve", Tb, Bim, bc(w0re), ALU.mult, RW, [R_raw])
              K.tt("dve", Tb2, Bre, bc(w0im), ALU.mult, RW, [R_raw])
              K.tt("dve", Bpim, Tb, Tb2, ALU.add, RW, [R_raw])
              Cst32 = A.f32(4096).rearrange("p (g j c) -> p g j c", g=32, j=8)
              Bneg32 = A.f32(4096).rearrange("p (g j c) -> p g j c", g=32, j=8)
              Bpos32 = A.f32(4096).rearrange("p (g j c) -> p g j c", g=32, j=8)
              M32 = A.f32(4096).rearrange("p (g n) -> p g n", g=32)
              T1 = A.f32(512).rearrange("p (g c) -> p g c", g=32)
              T2 = A.f32(512).rearrange("p (g c) -> p g c", g=32)
              R_bigh = [Res("bigB0"), Res("bigB1")]
              R_big = None
              R_Th = [Res("TB0"), Res("TB1")]
              R_M32 = Res("M32")
              R_T = Res("TB")

              def cscale(out4, j, MRE, MIM, n, d, ctype):
                  gs_ = slice(d * 32, (d + 1) * 32)
                  wre = PWre[:, 8 + n, gs_]
                  wim = PWim[:, 8 + n, gs_]
                  for half in range(2):
                      ps_ = slice(half * 64, (half + 1) * 64)
                      bcw = lambda w_: w_[ps_].unsqueeze(2).to_broadcast([64, 32, 16])
                      o = out4[ps_, :, j, :]
                      eng = "dve"
                      R_T = R_Th[half]
                      R_big = R_bigh[half]
                      if half == 0:
                          K.tt(eng, T1[ps_], MRE[ps_, gs_, :], bcw(wre), ALU.mult, [R_s, R_raw], [R_T])
                          K.tt(eng, T2[ps_], MIM[ps_, gs_, :], bcw(wim), ALU.mult, [R_s, R_raw], [R_T])
                          K.tt(eng, o, T1[ps_], T2[ps_], ALU.subtract, [R_T], [R_big])
                      else:
                          K.tt(eng, T1[ps_], MRE[ps_, gs_, :], bcw(wim), ALU.mult, [R_s, R_raw], [R_T])
                          K.tt(eng, T2[ps_], MIM[ps_, gs_, :], bcw(wre), ALU.mult, [R_s, R_raw], [R_T])
                          K.tt(eng, o, T1[ps_], T2[ps_], ALU.add, [R_T], [R_big])
                          if ctype:
                              K.ts(eng, o, o, -1.0, None, ALU.mult, None, [R_big], [R_big])

              rb_i = [0]

              def ringA():
                  b = rb_i[0] % 4
                  rb_i[0] += 1
                  return b

              for d in range(2):
                  for j in range(8):
                      cscale(Cst32, j, Cre, Cim, (j + 1) if d == 0 else (8 - j), d, True)
                      cscale(Bneg32, j, Bpre, Bpim, -(j + 1) if d == 0 else (j - 8), d, False)
                      cscale(Bpos32, j, Bpre, Bpim, (7 - j) if d == 0 else j, d, False)
                  K.copy("act", Cst_bf[:, d * 32:(d + 1) * 32, :], Cst32.rearrange("p g j c -> p g (j c)"), R_bigh, [R_fin])
                  Bpos_f = Bpos32.rearrange("p g j c -> p g (j c)")
                  Bneg_f = Bneg32.rearrange("p g j c -> p g (j c)")
                  Cst_f = Cst32.rearrange("p g j c -> p g (j c)")
                  for g4 in range(8):
                      b = ringA()
                      for q in range(4):
                          g = g4 * 4 + q
                          K.tr(bank(b)[:, q * 128:(q + 1) * 128], Bpos_f[:, g, :], ident32, R_bigh + [R_const], [R_bank[b]])
                      K.copy("dve", Bst_bf[:, d * 32 + g4 * 4:d * 32 + g4 * 4 + 4, :],
                             bank(b).rearrange("p (q n) -> p q n", q=4), [R_bank[b]], [R_fin])
                  for g4 in range(8):
                      b = ringA()
                      for q in range(4):
                          g = g4 * 4 + q
                          K.mm(bank(b)[:, q * 128:(q + 1) * 128], Bneg_f[:, g, :], Cst_f[:, g, :], True, True,
                               R_bigh, [R_bank[b]])
                      pv = bank(b).rearrange("p (q n) -> p q n", q=4)
                      mk = (mkf if d == 0 else mkb).unsqueeze(1).to_broadcast([128, 4, 128])
                      if d == 0:
                          K.tt("dve", M32[:, g4 * 4:g4 * 4 + 4, :], pv, mk, ALU.mult, [R_bank[b], R_cB], [R_M32])
                      else:
                          Tm = Tb.rearrange("p e c -> p (e c)")[:, 0:512].rearrange("p (q n) -> p q n", q=4)
                          K.tt("dve", Tm, pv, mk, ALU.mult, [R_bank[b], R_cB], [R_raw])
                          K.tt("dve", M32[:, g4 * 4:g4 * 4 + 4, :], M32[:, g4 * 4:g4 * 4 + 4, :], Tm, ALU.add,
                               [R_raw], [R_M32])
              for g in range(32):
                  K.stt(M32[:, g, :], ident32, dsk[:, g:g + 1], M32[:, g, :], ALU.mult, ALU.add, [R_const, R_cB], [R_M32])
              K.copy("act", M_bf, M32, [R_M32], [R_fin])

              ckpt(30)
              P.barrier()
              A.off = main_off
              KMAX = LMAX // 8
              XS = A.bf16(LMAX)
              R_XS = Res("XS")
              U = [A.bf16(KMAX) for _ in range(8)]
              R_U = [Res("U%d" % i) for i in range(8)]
              YU = [A.bf16(KMAX) for _ in range(8)]
              R_YU = [Res("YU%d" % i) for i in range(8)]
              YT = A.bf16(LMAX)
              R_YT = Res("YT")
              NS = 4
              Xs = [A.f32(KMAX) for _ in range(NS)]
              R_X = [Res("X%d" % i) for i in range(NS)]
              Hb = [A.bf16(KMAX + 2) for _ in range(NS)]
              R_H = [Res("H%d" % i) for i in range(NS)]
              NR = 8
              Rm = [A.f32(128) for _ in range(NR)]
              R_Rm = [Res("R%d" % i) for i in range(NR)]
              Rh = [A.bf16(128) for _ in range(NR)]
              Rl = [A.bf16(128) for _ in range(NR)]
              R_Rh = [Res("Rh%d" % i) for i in range(NR)]
              R_Rl = [Res("Rl%d" % i) for i in range(NR)]
              Xb = [A.bf16(KMAX) for _ in range(NS)]
              R_Xc = [[Res("X%d_%d" % (i, c_)) for c_ in range(2)] for i in range(NS)]
              R_Xbc = [[Res("Xb%d_%d" % (i, c_)) for c_ in range(2)] for i in range(NS)]
              rm_i = [0]
              rbB_i = [0]

              def ringB():
                  b = 4 + rbB_i[0] % 4
                  rbB_i[0] += 1
                  return b

              for sq in range(2):
                  L = Ls[sq]
                  Kc = L // 8
                  nstep = int(round(math.log2(Kc)))
                  tok0 = seq_off[sq]
                  CW = min(512, Kc)
                  nch = Kc // CW
                  for blk in range(4):
                      XSJ = XS[:, 0:L].rearrange("p (j k) -> p j k", j=8)
                      K.dma("sp", XSJ, xsT[blk * 128:(blk + 1) * 128, :, tok0 // 8:tok0 // 8 + Kc],
                            [R_xsT[(tok0 + t_) // 512] for t_ in range(0, L, 512)], [R_XS])
                      for gl in range(8):
                          for ch in range(nch):
                              b = ringA()
                              for j in range(8):
                                  K.mm(bank(b)[:, 0:CW], ES[:, gl * 8 + j, :], XSJ[:, j, ch * CW:(ch + 1) * CW], j == 0, j == 7,
                                       [R_cB, R_XS], [R_bank[b]])
                              K.copy(("dve", "act")[ch % 2], U[gl][:, ch * CW:(ch + 1) * CW], bank(b)[:, 0:CW], [R_bank[b]], [R_U[gl]])
                      for gp in range(4):
                          streams = [(gp * 2 + (s_ // 2), s_ % 2) for s_ in range(4)]
                          for s_, (gl, d) in enumerate(streams):
                              g = blk * 8 + gl
                              for ch in range(nch):
                                  b = ringA()
                                  K.mm(bank(b)[:, 0:CW], Bst_bf[:, d * 32 + g, :], U[gl][:, ch * CW:(ch + 1) * CW], True, True,
                                       [R_fin, R_U[gl]], [R_bank[b]])
                                  K.copy(("act", "dve")[ch % 2], Xs[s_][:, ch * CW:(ch + 1) * CW], bank(b)[:, 0:CW], [R_bank[b]], [R_Xc[s_][ch]])
                                  K.copy(("dve", "act")[ch % 2], Xb[s_][:, ch * CW:(ch + 1) * CW], Xs[s_][:, ch * CW:(ch + 1) * CW],
                                         [R_Xc[s_][ch]], [R_Xbc[s_][ch]])
                          ckpt(31)
                          def rbuild(m):
                              out = []
                              for s_, (gl, d) in enumerate(streams):
                                  g = blk * 8 + gl
                                  ri = rm_i[0] % NR
                                  rm_i[0] += 1
                                  K.ts("dve", Rm[ri], ident32, S1tab[:, m, d * 32 + g:d * 32 + g + 1], None, ALU.mult, None,
                                       [R_const, R_fin], [R_Rm[ri]])
                                  K.stt(Rm[ri], jsw, S2tab[:, m, d * 32 + g:d * 32 + g + 1], Rm[ri], ALU.mult, ALU.add,
                                        [R_cB, R_fin], [R_Rm[ri]])
                                  K.copy("dve", Rh[ri], Rm[ri], [R_Rm[ri]], [R_Rh[ri]])
                                  K.tt("pool", Rl[ri], Rm[ri], Rh[ri], ALU.subtract, [R_Rm[ri], R_Rh[ri]], [R_Rl[ri]])
                                  out.append(ri)
                              return out

                          ris = rbuild(0)
                          for m in range(nstep):
                              sft = 1 << m
                              work = []
                              ris_next = rbuild(m + 1) if m + 1 < nstep else None
                              for s_, (gl, d) in enumerate(streams):
                                  ri = ris[s_]
                                  for ch in range(nch):
                                      if d == 0:
                                          lo = max(ch * CW, sft); hi = (ch + 1) * CW
                                          src = (lo - sft, hi - sft)
                                      else:
                                          lo = ch * CW; hi = min((ch + 1) * CW, Kc - sft)
                                          src = (lo + sft, hi + sft)
                                      n = hi - lo
                                      if n <= 0:
                                          continue
                                      b = ringB()
                                      srcR = [R_Xbc[s_][c_] for c_ in range(src[0] // CW, (src[1] - 1) // CW + 1)]
                                      K.mm(bank(b)[:, 0:n], Rh[ri], Xb[s_][:, src[0]:src[1]], True, False,
                                           [R_Rh[ri]] + srcR, [R_bank[b]])
                                      K.mm(bank(b)[:, 0:n], Rl[ri], Xb[s_][:, src[0]:src[1]], False, True,
                                           [R_Rl[ri]] + srcR, [R_bank[b]])
                                      work.append((s_, ch, lo, hi, n, b))
                                      if len(work) % 4 == 0:
                                          for (s2, ch2, lo2, hi2, n2, b2) in work[-4:]:
                                              K.tt("dve", Xs[s2][:, lo2:hi2], Xs[s2][:, lo2:hi2], bank(b2)[:, 0:n2], ALU.add,
                                                   [R_bank[b2]], [R_Xc[s2][ch2]])
                              rem = len(work) % 4
                              for (s2, ch2, lo2, hi2, n2, b2) in (work[-rem:] if rem else []):
                                  K.tt("dve", Xs[s2][:, lo2:hi2], Xs[s2][:, lo2:hi2], bank(b2)[:, 0:n2], ALU.add,
                                       [R_bank[b2]], [R_Xc[s2][ch2]])
                              if m + 1 < nstep:
                                  for (s2, ch2, lo2, hi2, n2, b2) in work:
                                      K.copy("act", Xb[s2][:, lo2:hi2], Xs[s2][:, lo2:hi2], [R_Xc[s2][ch2]], [R_Xbc[s2][ch2]])
                                  ris = ris_next
                          for s_, (gl, d) in enumerate(streams):
                              if d == 0:
                                  K.memset("pool", Hb[s_][:, 0:1], 0.0, [R_H[s_]])
                                  K.copy("act", Hb[s_][:, 1:Kc + 1], Xs[s_][:, 0:Kc], R_Xc[s_], [R_H[s_]])
                              else:
                                  K.memset("pool", Hb[s_][:, Kc:Kc + 1], 0.0, [R_H[s_]])
                                  K.copy("act", Hb[s_][:, 0:Kc], Xs[s_][:, 0:Kc], R_Xc[s_], [R_H[s_]])
                          for q in range(2):
                              gl = gp * 2 + q
                              g = blk * 8 + gl
                              sf, sb = q * 2, q * 2 + 1
                              for ch in range(nch):
                                  b = ringA()
                                  cs_ = slice(ch * CW, (ch + 1) * CW)
                                  K.mm(bank(b)[:, 0:CW], M_bf[:, g, :], U[gl][:, cs_], True, False, [R_fin, R_U[gl]], [R_bank[b]])
                                  K.mm(bank(b)[:, 0:CW], Cst_bf[:, g, :], Hb[sf][:, ch * CW:ch * CW + CW], False, False,
                                       [R_fin, R_H[sf]], [R_bank[b]])
                                  K.mm(bank(b)[:, 0:CW], Cst_bf[:, 32 + g, :], Hb[sb][:, ch * CW + 1:ch * CW + CW + 1], False, True,
                                       [R_fin, R_H[sb]], [R_bank[b]])
                                  K.copy(("dve", "act")[ch % 2], YU[gl][:, cs_], bank(b)[:, 0:CW], [R_bank[b]], [R_YU[gl]])
                      YT3 = YT[:, 0:L].rearrange("p (k j) -> p k j", j=8)
                      for j in range(8):
                          for ch in range(nch):
                              b = ringA()
                              for gl in range(8):
                                  K.mm(bank(b)[:, 0:CW], ES[:, j * 8 + gl, :], YU[gl][:, ch * CW:(ch + 1) * CW], gl == 0, gl == 7,
                                       [R_cB, R_YU[gl]], [R_bank[b]])
                              K.copy(("dve", "act")[(j + ch) % 2], YT3[:, ch * CW:(ch + 1) * CW, j], bank(b)[:, 0:CW], [R_bank[b]], [R_YT])
                      K.dma("sp", yT[blk * 128:(blk + 1) * 128, tok0:tok0 + L], YT[:, 0:L], [R_YT], [R_yT[sq]])

          if "T" in phases:
              P.barrier()
              A.off = base_off
              lamt = A.f32(256)
              sgcol = A.f32(1)
              R_cT = Res("constT")
              K.dma("sp", lamt, lam_in, (), [R_cT])
              K.dma("sp", sgcol, sg_in, (), [R_cT])
              jk = A.f32(64)
              s12 = A.f32(2)
              e12 = A.f32(2)
              lamneg = A.f32(1)
              ones32 = A.f32(128)
              onesb = A.bf16(128)
              K.memset("dve", ones32, 1.0, [R_cT])
              K.memset("dve", onesb, 1.0, [R_cT])
              selt = A.f32(128)
              K.memset("dve", selt, 0.0, [R_cT])
              K.memset("dve", selt[0:1, :], 1.0, [R_cT])
              K.memset("dve", selt[32:33, :], 1.0, [R_cT])
              lsb = A.f32(512)
              lsb1 = A.f32(512)
              R_lc = Res("lc")
              oc0 = A.f32(512)
              oc1 = A.f32(512)
              K.stt(jk, lamt[:, 0:64], 1.0, lamt[:, 64:128], ALU.mult, ALU.mult, [R_cT], [R_cT], accum_out=s12[:, 0:1])
              K.stt(jk, lamt[:, 128:192], 1.0, lamt[:, 192:256], ALU.mult, ALU.mult, [R_cT], [R_cT], accum_out=s12[:, 1:2])
              K.act(e12, s12, AF.Exp, [R_cT], [R_cT])
              K.tt("dve", lamneg, e12[:, 1:2], e12[:, 0:1], ALU.subtract, [R_cT], [R_cT])
              K.ts("dve", lamneg, lamneg, -0.2, None, ALU.add, None, [R_cT], [R_cT])
              K.ts("dve", sgcol, sgcol, 0.8, None, ALU.mult, None, [R_cT], [R_cT])
              NCK = LMAX // 128
              KTb = [A.bf16(LMAX) for _ in range(2)]
              QTb = [A.bf16(LMAX) for _ in range(2)]
              Vb = [A.bf16(NCK * 128).rearrange("p (c e) -> p c e", e=128) for _ in range(2)]
              R_KT = [Res("KT0"), Res("KT1")]
              R_QT = [Res("QT0"), Res("QT1")]
              R_V = [Res("V0"), Res("V1")]
              NPT = 4
              PT = [A.bf16(1024) for _ in range(NPT)]
              R_PT = [Res("PT%d" % i) for i in range(NPT)]
              PS2 = [A.bf16(1024) for _ in range(2)]
              R_PS2 = [Res("PS2_0"), Res("PS2_1")]
              p2_i = [0]
              sums_q = []
              szat = [A.bf16(512) for _ in range(2)]
              R_szat = [Res("szat0"), Res("szat1")]
              yst = [A.bf16(512) for _ in range(2)]
              R_yst = [Res("yst0"), Res("yst1")]
              rl0 = A.f32(512); rl1 = A.f32(512); o0 = A.f32(512); o1 = A.f32(512); oo = A.f32(512); sqt = A.f32(512)
              Dm = [A.f32(128) for _ in range(4)]
              R_Dm = [Res("Dm%d" % i_) for i_ in range(4)]
              sm = A.f32(16)
              nh = A.f32(4)
              R_fz = Res("finz")
              R_sq = Res("sq")
              K.memset("dve", nh, -0.5, [R_fz])
              QW = 512
              heads = [(sq, h) for sq in range(2) for h in range(4)]
              ckpt(20)

              def load_head(i):
                  sq, h = heads[i]
                  L = Ls[sq]; tok0 = seq_off[sq]; sl = i % 2
                  tiles = [(tok0 + t_) // 512 for t_ in range(0, L, 512)]
                  K.dma("sp", KTb[sl][:, 0:L], kT[h * 128:(h + 1) * 128, tok0:tok0 + L], [R_kT[t_] for t_ in tiles], [R_KT[sl]])
                  K.dma("sp", QTb[sl][:, 0:L], qT[h * 128:(h + 1) * 128, tok0:tok0 + L], [R_qT[t_] for t_ in tiles], [R_QT[sl]])
                  K.dma("sp", Vb[sl][:, 0:L // 128, :],
                        vS[tok0:tok0 + L, h * 128:(h + 1) * 128].rearrange("(c p) e -> p c e", p=128),
                        [R_v[t_] for t_ in tiles], [R_V[sl]])

              sb_i = [0]
              pt_i = [0]
              qt_i = [0]
              dm_i = [0]
              load_head(0)
              pending = []
              for hi, (sq, h) in enumerate(heads):
                  L = Ls[sq]; tok0 = seq_off[sq]; sl = hi % 2
                  if hi + 1 < len(heads):
                      load_head(hi + 1)
                  KT = KTb[sl]; QT = QTb[sl]; V = Vb[sl]
                  nck = L // 128
                  for q0 in range(0, L, QW):
                      qs = qt_i[0] % 2
                      qt_i[0] += 1
                      trow = tok0 + q0
                      K.dma("sp", szat[qs], szaT[h * 128:(h + 1) * 128, trow:trow + QW], [R_sza[trow // 512]], [R_szat[qs]])

                      def qk(c):
                          b = 2 * (sb_i[0] % 2)
                          sb_i[0] += 1
                          for t in range(2):
                              K.mm(bank(b + t), KT[t * 64:(t + 1) * 64, c * 128:(c + 1) * 128],
                                   QT[t * 64:(t + 1) * 64, q0:q0 + QW], True, True, [R_KT[sl], R_QT[sl]], [R_bank[b + t]])
                          return b

                      if sb_i[0] % 2 == 1:
                          sb_i[0] += 1
                      slot_b = [qk(0), qk(1) if nck > 1 else 0]
                      for c in range(nck):
                          b = slot_b[c % 2]
                          pi = pt_i[0] % NPT
                          pt_i[0] += 1
                          K.act(PT[pi], PS[:, b * 512:b * 512 + 1024], AF.Exp, [R_bank[b], R_bank[b + 1]], [R_PT[pi]], scale=0.125)
                          for (cc, fn_) in pending:
                              if cc == c:
                                  fn_()
                          pending = [pf for pf in pending if pf[0] > c]
                          if c + 2 < nck:
                              slot_b[c % 2] = qk(c + 2)
                          for t in range(2):
                              K.mm(bank(4 + t), V[:, c, :], PT[pi][:, t * 512:(t + 1) * 512], c == 0, c == nck - 1,
                                   [R_PT[pi], R_V[sl]], [R_bank[4 + t]])
                          if sums_q:
                              sums_q.pop(0)()
                          if c % 2 == 1:
                              k2 = p2_i[0] % 2
                              p2_i[0] += 1
                              pprev = (pt_i[0] - 2) % NPT
                              K.tt("dve", PS2[k2], PT[pprev], PT[pi], ALU.add, [R_PT[pprev], R_PT[pi]], [R_PS2[k2]])

                              def do_sums(k2=k2, first=(c == 1), last=(c == nck - 1)):
                                  for t in range(2):
                                      K.mm(bank(6 + t), onesb, PS2[k2][:, t * 512:(t + 1) * 512], first, last,
                                           [R_PS2[k2], R_cT], [R_bank[6 + t]])
                              sums_q.append(do_sums)
                      while sums_q:
                          sums_q.pop(0)()
                      for (cc, fn_) in pending:
                          fn_()
                      pending = []
                      K.copy("dve", oc0, bank(4), [R_bank[4]], [R_fz])
                      K.copy("dve", oc1, bank(5), [R_bank[5]], [R_fz])
                      K.copy("act", lsb, bank(6), [R_bank[6]], [R_lc])
                      K.copy("act", lsb1, bank(7), [R_bank[7]], [R_lc])

                      def mk_stages(qs=qs, h=h, trow=trow):
                          def st0a():
                              K.recip(rl0, lsb, [R_lc], [R_fz])

                          def st0b():
                              K.recip(rl1, lsb1, [R_lc], [R_fz])

                          def st1():
                              K.tt("dve", o0, oc0, rl0, ALU.mult, [R_fz], [R_fz])
                              K.tt("dve", o1, oc1, rl1, ALU.mult, [R_fz], [R_fz])
                              K.stt(oo, o1, lamneg[:, 0:1], o0, ALU.mult, ALU.add, [R_fz, R_cT], [R_fz])
                              K.tt("dve", sqt, oo, oo, ALU.mult, [R_fz], [R_sq])

                          def st2():
                              b = 2 * (sb_i[0] % 2)
                              for s4 in range(4):
                                  K.mm(bank(b)[:, s4:s4 + 1], sqt[:, s4 * 128:(s4 + 1) * 128], ones32[:, 0:1], True, True,
                                       [R_sq, R_cT], [R_bank[b]])
                              K.ts("dve", sm[:, 0:4], bank(b)[:, 0:4], 1.0 / 128, 1e-5, ALU.mult, ALU.add, [R_bank[b]], [R_fz])
                              K.tt("pool", sm[:, 4:8], sm[:, 0:4], nh, ALU.pow, [R_fz], [R_fz])

                          def st3():
                              for s4 in range(4):
                                  di = dm_i[0] % 4
                                  dm_i[0] += 1
                                  K.ts("dve", Dm[di], ident32, sm[:, 4 + s4:5 + s4], None, ALU.mult, None, [R_const, R_fz], [R_Dm[di]])

                          def st4():
                              b = 2 * (sb_i[0] % 2)
                              for s4 in range(4):
                                  di = (dm_i[0] - 4 + s4) % 4
                                  K.mm(bank(b)[:, s4 * 128:(s4 + 1) * 128], ones32, Dm[di], True, True, [R_Dm[di], R_cT], [R_bank[b]])
                              K.tt("dve", o0, oo, bank(b), ALU.mult, [R_bank[b], R_fz], [R_fz])
                              K.stt(yst[qs], o0, sgcol[:, 0:1], szat[qs], ALU.mult, ALU.mult, [R_fz, R_cT, R_szat[qs]], [R_yst[qs]])
                              K.dma("sp", yaT_d[h * 128:(h + 1) * 128, trow:trow + QW], yst[qs], [R_yst[qs]], [R_ya[trow // 512]])
                          return [(1, st0a), (3, st0b), (5, st1), (9, st2), (10, st3), (13, st4)]

                      pending = mk_stages()
                      ckpt(25)
                      ckpt(100 + qt_i[0])
              for (cc, fn_) in pending:
                  fn_()
              pending = []

          if "C" in phases:
              P.barrier()
              A.off = base_off
              wglu = A.bf16(4 * 512).rearrange("p (a n) -> p a n", a=4)
              wbr = A.bf16(8 * 1024).rearrange("p (a n) -> p a n", a=8)
              wout = A.bf16(8 * 1024).rearrange("p (a n) -> p a n", a=8)
              bglu = A.f32(4)
              fgt = A.f32(1024)
              R_wC = Res("wC")
              K.dma("sp", bglu, bglu_in, (), [R_wC])
              K.dma("sp", fgt, fg_in, (), [R_wC])
              nhc = A.f32(4)
              K.memset("dve", nhc, -0.5, [R_wC])
              ytb = [A.bf16(2048).rearrange("p (a t) -> p a t", a=4) for _ in range(2)]
              szb = [A.bf16(2048).rearrange("p (a t) -> p a t", a=4) for _ in range(2)]
              sgsb = [A.bf16(4096).rearrange("p (a t) -> p a t", a=8) for _ in range(2)]
              sgab = [A.bf16(4096).rearrange("p (a t) -> p a t", a=8) for _ in range(2)]
              yab = [A.bf16(2048).rearrange("p (a t) -> p a t", a=4) for _ in range(2)]
              xtb = [A.f32(4096).rearrange("p (s d) -> p s d", s=4) for _ in range(2)]
              R_ld = [Res("ldC0"), Res("ldC1")]
              gbf = A.bf16(2048).rearrange("p (a t) -> p a t", a=4)
              ys3 = A.bf16(2048).rearrange("p (a t) -> p a t", a=4)
              yaT = A.bf16(2048).rearrange("p (a t) -> p a t", a=4)
              mT = A.bf16(4096).rearrange("p (a t) -> p a t", a=8)
              R_g = Res("g"); R_ys3 = Res("ys3"); R_yaT = Res("yaT"); R_mT = Res("mT")
              tA = [A.f32(512) for _ in range(2)]
              tB = [A.f32(512) for _ in range(2)]
              R_tA = [Res("tA0"), Res("tA1")]
              R_tB = [Res("tB0"), Res("tB1")]
              resb = [A.f32(1024) for _ in range(2)]
              outb = [A.f32(1024) for _ in range(2)]
              R_res = [Res("res0"), Res("res1")]
              R_outb = [Res("out0"), Res("out1")]
              junkc = A.f32(1024)
              R_jc = Res("junkc")
              smc = A.f32(8)
              R_smc = Res("smc")
              wstg = [xtb[1].rearrange("p s d -> p (s d)")[:, i * 2048:(i + 1) * 2048] for i in range(2)]
              R_wstg = [Res("wstgC0"), Res("wstgC1")]
              wi = [0]

              def wload(dst, src_ap):
                  n = dst.shape[-1]
                  sl = wi[0] % 2
                  wi[0] += 1
                  K.dma("sp", wstg[sl][:, 0:n], src_ap, (), [R_wstg[sl]])
                  K.copy(("dve", "pool")[sl], dst, wstg[sl][:, 0:n], [R_wstg[sl]], [R_wC])

              wg_v = wglu_in.rearrange("(a p) n -> p a n", p=128)
              for a in range(4):
                  wload(wglu[:, a, :], wg_v[:, a, :])
              wb_v = wbr_in.rearrange("b (a p) n -> p b a n", p=128)
              for br in range(2):
                  for a in range(4):
                      wload(wbr[:, br * 4 + a, :], wb_v[:, br, a, :])
              wo_v = wout_in.rearrange("(a p) n -> p a n", p=128)
              for a in range(8):
                  wload(wout[:, a, :], wo_v[:, a, :])
              for rw in R_wstg:
                  for kk_, vv_ in list(rw.r.items()) + ([rw.w] if rw.w else []):
                      if R_ld[1].r.get(kk_, 0) < vv_:
                          R_ld[1].r[kk_] = vv_

              def loadC(T):
                  sl = T % 2
                  ts_ = slice(T * 512, (T + 1) * 512)
                  W_ = [R_ld[sl]]
                  K.dma("sp", ytb[sl], yT[:, ts_].rearrange("(a p) t -> p a t", p=128), [R_yT[0], R_yT[1]], W_)
                  K.dma("sp", szb[sl], szsT[:, ts_].rearrange("(a p) t -> p a t", p=128), [R_szsT[T]], W_)
                  K.dma("sp", sgsb[sl], sgsT[:, ts_].rearrange("(a p) t -> p a t", p=128), [R_sgs[T]], W_)
                  K.dma("sp", sgab[sl], sgaT[:, ts_].rearrange("(a p) t -> p a t", p=128), [R_sga[T]], W_)
                  K.dma("sp", yab[sl], yaT_d[:, ts_].rearrange("(a p) t -> p a t", p=128), [R_ya[T]], W_)
                  K.dma("sp", xtb[sl], x_in[ts_, :].rearrange("(s p) d -> p s d", p=128), (), W_)

              rc_i = [0]

              def ringC():
                  b = rc_i[0] % 8
                  rc_i[0] += 1
                  return b

              loadC(0)
              for T in range(NT):
                  sl = T % 2
                  if T + 1 < NT:
                      loadC(T + 1)
                  RL = [R_ld[sl]]
                  yt = ytb[sl]
                  for a in range(4):
                      g_ = a % 2
                      K.tt("dve", tA[g_], yt[:, a, :], yt[:, a, :], ALU.mult, RL, [R_tA[g_]])
                      K.ts("dve", tA[g_], tA[g_], 0.044715, 1.0, ALU.mult, ALU.add, [R_tA[g_]], [R_tA[g_]])
                      K.tt("pool", tA[g_], tA[g_], yt[:, a, :], ALU.mult, RL + [R_tA[g_]], [R_tA[g_]])
                      K.act(tB[g_], tA[g_], AF.Sigmoid, [R_tA[g_]], [R_tB[g_]], scale=1.5957691216057308)
                      K.tt("dve", gbf[:, a, :], yt[:, a, :], tB[g_], ALU.mult, RL + [R_tB[g_]], [R_g])
                  for e_ in range(4):
                      b = ringC()
                      for a in range(4):
                          K.mm(bank(b), wglu[:, a, e_ * 128:(e_ + 1) * 128], gbf[:, a, :], a == 0, a == 3, [R_wC, R_g], [R_bank[b]])
                      g_ = e_ % 2
                      K.act(tB[g_], bank(b), AF.Sigmoid, [R_bank[b], R_wC], [R_tB[g_]], bias=bglu[:, e_:e_ + 1])
                      K.tt("dve", tA[g_], gbf[:, e_, :], tB[g_], ALU.mult, [R_g, R_tB[g_]], [R_tA[g_]])
                      K.tt("pool", ys3[:, e_, :], tA[g_], szb[sl][:, e_, :], ALU.mult, RL + [R_tA[g_]], [R_ys3])
                  for e_ in range(8):
                      bs = ringC()
                      for a in range(4):
                          K.mm(bank(bs), wbr[:, a, e_ * 128:(e_ + 1) * 128], ys3[:, a, :], a == 0, a == 3, [R_wC, R_ys3], [R_bank[bs]])
                      ba = ringC()
                      for a in range(4):
                          K.mm(bank(ba), wbr[:, 4 + a, e_ * 128:(e_ + 1) * 128], yab[sl][:, a, :], a == 0, a == 3, [R_wC] + RL, [R_bank[ba]])
                      g_ = e_ % 2
                      K.tt("dve", tA[g_], bank(bs), sgsb[sl][:, e_, :], ALU.mult, RL + [R_bank[bs]], [R_tA[g_]])
                      K.tt("dve", tB[g_], bank(ba), sgab[sl][:, e_, :], ALU.mult, RL + [R_bank[ba]], [R_tB[g_]])
                      K.tt("pool", mT[:, e_, :], tA[g_], tB[g_], ALU.add, [R_tA[g_], R_tB[g_]], [R_mT])
                  for s in range(4):
                      rs = s % 2
                      for hf in range(2):
                          b = ringC()
                          for a in range(8):
                              K.mm(bank(b), mT[:, a, s * 128:(s + 1) * 128], wout[:, a, hf * 512:(hf + 1) * 512], a == 0, a == 7,
                                   [R_wC, R_mT], [R_bank[b]])
                          K.tt("dve", resb[rs][:, hf * 512:(hf + 1) * 512], bank(b), xtb[sl][:, s, hf * 512:(hf + 1) * 512], ALU.add,
                               RL + [R_bank[b]], [R_res[rs]])
                      K.act(junkc, resb[rs], AF.Square, [R_res[rs]], [R_jc, R_smc], accum_out=smc[:, 0:1])
                      K.ts("dve", smc[:, 1:2], smc[:, 0:1], 1.0 / D, 1e-6, ALU.mult, ALU.add, [R_smc], [R_smc])
                      K.tt("pool", smc[:, 2:3], smc[:, 1:2], nhc[:, 0:1], ALU.pow, [R_smc, R_wC], [R_smc])
                      K.act(junkc, resb[rs], AF.Copy, [R_res[rs], R_smc], [R_jc], scale=smc[:, 2:3])
                      K.tt("pool", outb[rs], junkc, fgt, ALU.mult, [R_jc, R_wC], [R_outb[rs]])
                      rows = slice(T * 512 + s * 128, T * 512 + (s + 1) * 128)
                      K.dma("sp", y_out[rows, :], outb[rs], [R_outb[rs]], ())

        except _Stop:
            pass
        P.finish()
        P.emit(nc, st)
    return nc


def _rope_tables(lmax):
    half = 32
    inv = (1.0 / (10000.0 ** (np.arange(half, dtype=np.float32) * 2.0 / 64))).astype(np.float32)
    ang = (np.arange(lmax, dtype=np.float32)[:, None] * inv[None, :]).astype(np.float32)
    c = np.cos(ang.astype(np.float64)); s = np.sin(ang.astype(np.float64))
    C = np.zeros((128, lmax), np.float32); S = np.zeros((128, lmax), np.float32)
    for t in range(2):
        for d in range(64):
            C[t * 64 + d] = c[:, d % 32]
            S[t * 64 + d] = (-1.0 if d < 32 else 1.0) * s[:, d % 32]
    return C, S


def _host_consts(lmax):
    C, S = _rope_tables(lmax)
    ident = np.eye(128, dtype=np.float32)
    jswap = np.zeros((128, 128), np.float32)
    for k in range(128):
        jswap[k, (k + 64) % 128] = 1.0
    esel = np.zeros((8, 8, 128, 128), np.float32)
    for a in range(8):
        for b in range(8):
            for c in range(16):
                esel[a, b, a * 16 + c, b * 16 + c] = 1.0
    esel = esel.reshape(64, 128, 128).transpose(1, 0, 2).reshape(128, 64 * 128).astype(ml_dtypes.bfloat16)
    jj = np.arange(128) // 16
    maskf = (jj[:, None] <= jj[None, :]).astype(np.float32)
    maskb = (jj[:, None] >= jj[None, :]).astype(np.float32)
    pswap = np.zeros((128, 128), np.float32)
    for m in range(128):
        pswap[(m // 64) * 64 + ((m % 64) + 32) % 64, m] = 1.0
    pswap = pswap.astype(ml_dtypes.bfloat16)
    return dict(ropeC=C, ropeS=S, ident=ident, jswap=jswap, pswap=pswap, esel=np.ascontiguousarray(esel), maskf=maskf, maskb=maskb)


def _layout_weights(inp):
    w = np.asarray(inp["w_in"][0], np.float32)
    xs, zs, q, k, v, za = [w[:, i * 512:(i + 1) * 512] for i in range(6)]
    gs = w[:, 3072:4096]; ga = w[:, 4096:5120]
    f = np.arange(512)
    swap = (f // 64) * 64 + ((f % 64) + 32) % 64
    w_ext = np.concatenate([xs, zs, q, k, gs, ga, za, v], axis=1)
    o = {"w_in": np.ascontiguousarray(w_ext)}
    o["norm_g"] = np.ascontiguousarray(np.asarray(inp["norm_g"][0], np.float32).reshape(8, 128).T)

    def pdup(a):
        a = np.asarray(a, np.float32)
        rest = a.shape[3:]
        t = np.moveaxis(a, 2, 0).reshape(64, 64, *rest)
        t = np.concatenate([t, t], 0)
        return np.ascontiguousarray(t.reshape(128, -1))

    o["a_re"] = pdup(inp["ssm_a_re"][0])
    o["a_im"] = pdup(inp["ssm_a_im"][0])
    o["log_dt"] = np.ascontiguousarray(np.broadcast_to(np.asarray(inp["ssm_log_dt"][0], np.float32).reshape(1, 64), (128, 64)))
    o["b_re"] = pdup(inp["ssm_b_re"][0])
    o["b_im"] = pdup(inp["ssm_b_im"][0])
    o["c_re"] = pdup(np.swapaxes(np.asarray(inp["ssm_c_re"][0]), 2, 3))
    o["c_im"] = pdup(np.swapaxes(np.asarray(inp["ssm_c_im"][0]), 2, 3))
    dsk = np.asarray(inp["ssm_d"][0], np.float32).reshape(32, 16)
    o["dskip"] = np.ascontiguousarray(np.tile(dsk.T, (8, 1)))
    o["w_glu"] = np.ascontiguousarray(np.asarray(inp["w_glu"][0], np.float32))
    o["b_glu"] = np.ascontiguousarray(np.asarray(inp["b_glu"][0], np.float32).reshape(4, 128).T)
    o["w_branch"] = np.ascontiguousarray(np.asarray(inp["w_branch"][0], np.float32))
    o["w_out"] = np.ascontiguousarray(np.asarray(inp["w_out"][0], np.float32))
    o["final_g"] = np.ascontiguousarray(np.broadcast_to(np.asarray(inp["final_g"], np.float32).reshape(1, 1024), (128, 1024)))
    o["subln_g"] = np.ascontiguousarray(np.asarray(inp["subln_g"][0], np.float32).reshape(128, 1))
    lam = np.concatenate([np.asarray(inp[n][0], np.float32) for n in ("lambda_q1", "lambda_k1", "lambda_q2", "lambda_k2")])
    o["lam"] = np.ascontiguousarray(np.broadcast_to(lam.reshape(1, 256), (128, 256)))
    return o


def make_in_maps(inp, L0, L1, n_cores):
    shared = _host_consts(max(L0, L1))
    shared.update(_layout_weights(inp))
    maps = []
    for c in range(n_cores):
        m = dict(shared)
        m["x"] = np.ascontiguousarray(np.concatenate(
            [np.asarray(inp["x_prompt"][c, :L0], np.float32), np.asarray(inp["x_sample"][c, :L1], np.float32)], 0))
        maps.append(m)
    return maps


def kernel(**inputs):
    L0, L1 = 2048, 8192
    nc = build(L0, L1)
    maps = make_in_maps(inputs, L0, L1, N_CORES)
    res = run_bass_kernel_spmd(nc, maps, core_ids=list(range(N_CORES)))
    ys = [np.asarray(r["y"], np.float32) for r in res.results]
    y_prompt = np.stack([y[:L0] for y in ys], 0)
    y_sample = np.stack([y[L0:] for y in ys], 0)
    return (y_prompt, y_sample)
```

```python
from contextlib import ExitStack
import math
import numpy as np
import ml_dtypes
import concourse.bass as bass
import concourse.mybir as mybir
from concourse.bass_utils import run_bass_kernel_spmd

F32 = mybir.dt.float32
BF16 = mybir.dt.bfloat16
I32 = mybir.dt.int32
AF = mybir.ActivationFunctionType
ALU = mybir.AluOpType

NDMA = 24
ENGS = ("pe", "act", "dve", "pool", "sp")
N_CORES = 8
D = 1024
NEXT = 5120
ARENA_WORDS = 49152


class Res:
    __slots__ = ("name", "w", "r")

    def __init__(self, name=""):
        self.name = name
        self.w = None
        self.r = {}


class Prog:
    def __init__(self):
        self.ops = {e: [] for e in ENGS}
        self.cnt = {e: 0 for e in ENGS}
        self.waited = {e: {} for e in ENGS}
        self.dma_val = [0] * NDMA
        self.dma_next = 0

    def _deps(self, reads, writes):
        deps = {}
        for r in reads:
            if r.w is not None and deps.get(r.w[0], 0) < r.w[1]:
                deps[r.w[0]] = r.w[1]
        for w in writes:
            if w.w is not None and deps.get(w.w[0], 0) < w.w[1]:
                deps[w.w[0]] = w.w[1]
            for s, v in w.r.items():
                if deps.get(s, 0) < v:
                    deps[s] = v
        return deps

    def _waits(self, eng, deps):
        waits = []
        wd = self.waited[eng]
        for s, v in deps.items():
            if s == "pe" and eng == "pe":
                continue
            if wd.get(s, 0) >= v:
                continue
            wd[s] = v
            waits.append((s, v))
        return waits

    def _mark(self, tok, reads, writes):
        for r in reads:
            if r.r.get(tok[0], 0) < tok[1]:
                r.r[tok[0]] = tok[1]
        for w in writes:
            w.w = tok
            w.r = {}

    def op(self, eng, fn, reads=(), writes=()):
        waits = self._waits(eng, self._deps(reads, writes))
        n = self.cnt[eng] + 1
        self.cnt[eng] = n
        self._mark((eng, n), reads, writes)
        self.ops[eng].append((waits, fn, (eng, 1)))

    def dma(self, q, fn, reads=(), writes=()):
        s = self.dma_next
        self.dma_next = (s + 1) % NDMA
        key = ("dma", s)
        deps = self._deps(reads, writes)
        if self.dma_val[s] > 0:
            deps[key] = max(deps.get(key, 0), self.dma_val[s])
        waits = self._waits(q, deps)
        self.dma_val[s] += 16
        self._mark((key, self.dma_val[s]), reads, writes)
        self.ops[q].append((waits, fn, (key, 16)))

    def barrier(self):
        deps = {("dma", s): v for s, v in enumerate(self.dma_val) if v > 0}
        for e in ENGS:
            if e != "sp" and self.cnt[e] > 0:
                deps[e] = self.cnt[e]
        for q in ENGS:
            d = {k: v for k, v in deps.items() if k != q}
            waits = self._waits(q, d)
            if waits:
                self.ops[q].append((waits, None, None))

    def finish(self):
        for q in ("sp",):
            deps = {("dma", s): v for s, v in enumerate(self.dma_val) if v > 0}
            for e in ENGS:
                if e != q and e != "sp" and self.cnt[e] > 0:
                    deps[e] = self.cnt[e]
            waits = self._waits(q, deps)
            self.ops[q].append((waits, None, None))

    def emit(self, nc, stack):
        sems = {}
        for e in ENGS:
            sems[e] = stack.enter_context(nc.semaphore("s_" + e))
        for s in range(NDMA):
            sems[("dma", s)] = stack.enter_context(nc.semaphore("s_dma%d" % s))
        block = stack.enter_context(nc.Block())
        meth = {"pe": block.tensor, "act": block.scalar, "dve": block.vector,
                "pool": block.gpsimd, "sp": block.sync}
        for e in ENGS:
            ops = self.ops[e]

            def body(eng, ops=ops):
                for waits, fn, inc in ops:
                    for s, v in waits:
                        eng.wait_ge(sems[s], v)
                    if fn is not None:
                        ins = fn(eng)
                        if inc is not None:
                            ins.then_inc(sems[inc[0]], inc[1])
            meth[e](body)


class Arena:
    def __init__(self, ap, cap):
        self.ap = ap
        self.cap = cap
        self.off = 0

    def f32(self, n):
        off = self.off
        self.off += n
        assert self.off <= self.cap, ("arena overflow", self.off, self.cap)
        return self.ap[:, off:off + n]

    def bf16(self, n):
        w = (n + 1) // 2
        return self.f32(w).bitcast(BF16)[:, :n]


class KB:
    def __init__(self, P):
        self.P = P

    def mm(self, out, lhsT, rhs, start, stop, R, W):
        self.P.op("pe", lambda e: e.matmul(out, lhsT, rhs, start=start, stop=stop), R, W)

    def tr(self, out, in_, ident, R, W):
        self.P.op("pe", lambda e: e.transpose(out, in_, ident), R, W)

    def act(self, out, in_, func, R, W, bias=None, scale=None, accum_out=None):
        kw = {}
        if bias is not None:
            kw["bias"] = bias
        if scale is not None:
            kw["scale"] = scale
        if accum_out is not None:
            kw["accum_out"] = accum_out
        self.P.op("act", lambda e: e.activation(out=out, in_=in_, func=func, **kw), R, W)

    def ts(self, eng, out, in0, s1, s2, op0, op1, R, W):
        if op1 is None:
            self.P.op(eng, lambda e: e.tensor_scalar(out=out, in0=in0, scalar1=s1, scalar2=None, op0=op0), R, W)
        else:
            self.P.op(eng, lambda e: e.tensor_scalar(out=out, in0=in0, scalar1=s1, scalar2=s2, op0=op0, op1=op1), R, W)

    def tt(self, eng, out, in0, in1, op, R, W):
        self.P.op(eng, lambda e: e.tensor_tensor(out=out, in0=in0, in1=in1, op=op), R, W)

    def stt(self, out, in0, scalar, in1, op0, op1, R, W, accum_out=None):
        if accum_out is None:
            self.P.op("dve", lambda e: e.scalar_tensor_tensor(out=out, in0=in0, scalar=scalar, in1=in1, op0=op0, op1=op1), R, W)
        else:
            self.P.op("dve", lambda e: e.scalar_tensor_tensor(out=out, in0=in0, scalar=scalar, in1=in1, op0=op0, op1=op1,
                                                             accum_out=accum_out), R, W)

    def copy(self, eng, out, in_, R, W):
        if eng == "act":
            self.P.op("act", lambda e: e.activation(out=out, in_=in_, func=AF.Copy), R, W)
        else:
            self.P.op(eng, lambda e: e.tensor_copy(out=out, in_=in_), R, W)

    def memset(self, eng, ap, val, W):
        self.P.op(eng, lambda e: e.memset(ap, val), (), W)

    def recip(self, out, in_, R, W):
        self.P.op("dve", lambda e: e.reciprocal(out=out, in_=in_), R, W)

    def dma(self, q, out, in_, R, W, slow=False):
        q = "sp"
        if slow:
            self.P.dma(q, lambda e: e.dma_start(out=out, in_=in_, allow_slow_non_contiguous=True), R, W)
        else:
            self.P.dma(q, lambda e: e.dma_start(out=out, in_=in_), R, W)


class _Stop(Exception):
    pass


def build(L0, L1, debug=False, phases="ABTC", stop=0):
    Ls = [L0, L1]
    NTOK = L0 + L1
    LMAX = max(Ls)
    seq_off = [0, L0]
    nc = bass.Bass("TRN2", target_bir_lowering=False)

    def din(name, shape, dt=F32):
        return nc.dram_tensor(name, list(shape), dt, kind="ExternalInput").ap()

    skind = "ExternalOutput" if debug else "Internal"

    def dscr(name, shape, dt=BF16):
        return nc.dram_tensor(name, list(shape), dt, kind=skind).ap()

    x_in = din("x", [NTOK, D])
    w_in = din("w_in", [D, NEXT])
    ng_in = din("norm_g", [128, 8])
    ropeC = din("ropeC", [128, LMAX])
    ropeS = din("ropeS", [128, LMAX])
    ident_in = din("ident", [128, 128])
    jswap_in = din("jswap", [128, 128])
    pswap_in = din("pswap", [128, 128], BF16)
    esel_in = din("esel", [128, 64 * 128], BF16)
    maskf_in = din("maskf", [128, 128])
    maskb_in = din("maskb", [128, 128])
    a_re_in = din("a_re", [128, 64])
    a_im_in = din("a_im", [128, 64])
    ldt_in = din("log_dt", [128, 64])
    b_re_in = din("b_re", [128, 64 * 16])
    b_im_in = din("b_im", [128, 64 * 16])
    c_re_in = din("c_re", [128, 64 * 16])
    c_im_in = din("c_im", [128, 64 * 16])
    dsk_in = din("dskip", [128, 32])
    wglu_in = din("w_glu", [512, 512])
    bglu_in = din("b_glu", [128, 4])
    wbr_in = din("w_branch", [2, 512, 1024])
    wout_in = din("w_out", [1024, 1024])
    fg_in = din("final_g", [128, 1024])
    sg_in = din("subln_g", [128, 1])
    lam_in = din("lam", [128, 4 * 64])
    y_out = nc.dram_tensor("y", [NTOK, D], F32, kind="ExternalOutput").ap()

    xsT = dscr("xsT", [512, 8, NTOK // 8])
    szsT = dscr("szsT", [512, NTOK])
    qT = dscr("qT", [512, NTOK])
    kT = dscr("kT", [512, NTOK])
    vS = dscr("vS", [NTOK, 512])
    szaT = dscr("szaT", [512, NTOK])
    sgsT = dscr("sgsT", [1024, NTOK])
    sgaT = dscr("sgaT", [1024, NTOK])
    yT = dscr("yT", [512, NTOK])
    yaT_d = dscr("yaT", [512, NTOK])

    P = Prog()
    K = KB(P)
    NT = NTOK // 512
    R_xsT = [Res() for _ in range(NT)]
    R_szsT = [Res() for _ in range(NT)]
    R_qT = [Res() for _ in range(NT)]
    R_kT = [Res() for _ in range(NT)]
    R_v = [Res() for _ in range(NT)]
    R_sza = [Res() for _ in range(NT)]
    R_sgs = [Res() for _ in range(NT)]
    R_sga = [Res() for _ in range(NT)]
    R_yT = [Res() for _ in range(2)]
    R_ya = [Res() for _ in range(NT)]

    with ExitStack() as st:
        arena_t = st.enter_context(nc.sbuf_tensor("arena", [128, ARENA_WORDS], F32))
        psum_t = st.enter_context(nc.psum_tensor("psum", [128, 4096], F32))
        PS = psum_t[:, :]

        def bank(b, n=512):
            return PS[:, b * 512:b * 512 + n]

        R_bank = [Res("bank%d" % b) for b in range(8)]

        A = Arena(arena_t[:, :], ARENA_WORDS)
        ident32 = A.f32(128)
        identb = A.bf16(128)
        R_const = Res("const")
        K.dma("sp", ident32, ident_in, (), [R_const])
        K.copy("dve", identb, ident32, [R_const], [R_const])
        pswap = A.bf16(128)
        K.dma("sp", pswap, pswap_in, (), [R_const])
        base_off = A.off

        def ckpt(n):
            if stop == n:
                raise _Stop()

        try:
          if "A" in phases:
              A.off = base_off
              ng = A.f32(8)
              Wp = A.bf16(8 * NEXT).rearrange("p (a n) -> p a n", a=8)
              R_W = Res("W")
              R_ng = Res("ng")
              K.dma("sp", ng, ng_in, (), [R_ng])
              xt_off = A.off
              xt = A.f32(4096).rearrange("p (s d) -> p s d", s=4)
              R_xt = Res("xt")
              xn = A.bf16(4096).rearrange("p (s d) -> p s d", s=4)
              R_xn = Res("xn")
              junk = A.f32(1024)
              R_junk = Res("junk")
              xnT = [A.bf16(4096).rearrange("p (a t) -> p a t", a=8) for _ in range(2)]
              R_xnT = [Res("xnT0"), Res("xnT1")]
              cosT = [A.f32(512) for _ in range(2)]
              sinT = [A.f32(512) for _ in range(2)]
              R_tab = [Res("tab0"), Res("tab1")]
              t1 = [A.f32(512) for _ in range(2)]
              t2 = [A.f32(512) for _ in range(2)]
              R_t1 = [Res("t10"), Res("t11")]
              R_t2 = [Res("t20"), Res("t21")]
              sig = [A.f32(512) for _ in range(2)]
              qsb = [A.bf16(512) for _ in range(2)]
              R_qsb = [Res("qsb0"), Res("qsb1")]
              R_sig = [Res("sig0"), Res("sig1")]
              NSTG = 16
              stg = [A.bf16(512) for _ in range(NSTG)]
              R_stg = [Res("stg%d" % i) for i in range(NSTG)]
              ss = A.f32(4)
              ms = A.f32(4)
              rstd = A.f32(4)
              neghalf = A.f32(4)
              R_ss = Res("ss")
              R_rstd = Res("rstd")
              K.memset("dve", neghalf, -0.5, [R_rstd])
              wst = [arena_t[:, xt_off + h2 * 2048: xt_off + (h2 + 1) * 2048] for h2 in range(2)]
              R_wst = [Res("wst0"), Res("wst1")]
              w_view = w_in.rearrange("(a p) n -> p a n", p=128)
              i = 0
              for dc in range(8):
                  for cb in range(3):
                      sl = i % 2
                      c0_ = cb * 2048
                      cn_ = min(2048, NEXT - c0_)
                      K.dma("sp", wst[sl][:, 0:cn_], w_view[:, dc, c0_:c0_ + cn_], (), [R_wst[sl]])
                      eng = ("dve", "pool")[i % 2]
                      K.ts(eng, Wp[:, dc, c0_:c0_ + cn_], wst[sl][:, 0:cn_], ng[:, dc:dc + 1], None, ALU.mult, None,
                           [R_wst[sl], R_ng], [R_W])
                      i += 1

              for rw in R_wst:
                  for kk, vv in list(rw.r.items()) + ([rw.w] if rw.w else []):
                      if R_xt.r.get(kk, 0) < vv:
                          R_xt.r[kk] = vv
              ckpt(1)
              stg_i = [0]

              def stage_out(dst_ap, R_dst, produce, view3=False):
                  sl = stg_i[0] % NSTG
                  stg_i[0] += 1
                  produce(stg[sl], R_stg[sl])
                  src = stg[sl].rearrange("p (j k) -> p j k", j=8) if view3 else stg[sl]
                  K.dma("pool", dst_ap, src, [R_stg[sl]], [R_dst])

              pb_i = [0]

              def next_bank():
                  b = 4 + (pb_i[0] % 4)
                  pb_i[0] += 1
                  return b

              def load_tile(T):
                  K.dma("sp", xt, x_in[T * 512:(T + 1) * 512, :].rearrange("(s p) d -> p s d", p=128), (), [R_xt])

              def tile_pos(T):
                  tok0 = T * 512
                  sq = 0 if tok0 < L0 else 1
                  return sq, tok0 - seq_off[sq]

              def load_tabs(T):
                  sq, pos0 = tile_pos(T)
                  K.dma("sp", cosT[T % 2], ropeC[:, pos0:pos0 + 512], (), [R_tab[T % 2]])
                  K.dma("sp", sinT[T % 2], ropeS[:, pos0:pos0 + 512], (), [R_tab[T % 2]])

              load_tile(0)
              load_tabs(0)
              for T in range(NT):
                  sl = T % 2
                  for s in range(4):
                      K.stt(junk, xt[:, s, :], 1.0, xt[:, s, :], ALU.mult, ALU.mult, [R_xt], [R_junk, R_ss],
                            accum_out=ss[:, s:s + 1])
                  K.ts("dve", ms, ss, 1.0 / D, 1e-6, ALU.mult, ALU.add, [R_ss], [R_ss])
                  K.tt("pool", rstd, ms, neghalf, ALU.pow, [R_ss, R_rstd], [R_rstd])
                  for s in range(4):
                      K.act(xn[:, s, :], xt[:, s, :], AF.Copy, [R_xt, R_rstd], [R_xn], scale=rstd[:, s:s + 1])
                  if T + 1 < NT:
                      load_tile(T + 1)
                      load_tabs(T + 1)
                  for h in range(2):
                      pst = PS[:, h * 1024:(h + 1) * 1024].bitcast(BF16).rearrange("p (a t) -> p a t", a=4)
                      Rb = [R_bank[2 * h], R_bank[2 * h + 1]]
                      for a in range(4):
                          dc = h * 4 + a
                          for s in range(4):
                              K.tr(pst[:, a, s * 128:(s + 1) * 128], xn[:, s, dc * 128:(dc + 1) * 128], identb,
                                   [R_xn, R_const], Rb)
                      K.copy(("dve", "act")[h], xnT[sl][:, h * 4:(h + 1) * 4, :], pst, Rb, [R_xnT[sl]])
                  ckpt(3)
                  X = xnT[sl]
                  RX = R_xnT[sl]
                  tsl = slice(T * 512, (T + 1) * 512)

                  def fm_proj(blk):
                      b = next_bank()
                      for dc in range(8):
                          K.mm(bank(b), Wp[:, dc, blk * 128:(blk + 1) * 128], X[:, dc, :], dc == 0, dc == 7,
                               [R_W, RX], [R_bank[b]])
                      return b

                  for j in range(4):
                      b = fm_proj(j)
                      stage_out(xsT[j * 128:(j + 1) * 128, :, T * 64:(T + 1) * 64], R_xsT[T],
                                lambda s_ap, s_r, b=b: K.copy("dve", s_ap.rearrange("p (j k) -> p k j", j=8),
                                                              bank(b).rearrange("p (k j) -> p k j", j=8), [R_bank[b]], [s_r]),
                                view3=True)
                  ckpt(4)
                  for j in range(4):
                      b = fm_proj(4 + j)
                      g = j % 2
                      K.act(sig[g], bank(b), AF.Sigmoid, [R_bank[b]], [R_sig[g]])
                      stage_out(szsT[j * 128:(j + 1) * 128, tsl], R_szsT[T],
                                lambda s_ap, s_r, b=b, g=g: K.tt("dve", s_ap, bank(b), sig[g], ALU.mult,
                                                                 [R_bank[b], R_sig[g]], [s_r]))
                  ckpt(5)
                  for (base, dstT, Rd) in ((8, qT, R_qT), (12, kT, R_kT)):
                      for j in range(4):
                          b0 = fm_proj(base + j)
                          g = j % 2
                          K.copy("dve", qsb[g], bank(b0), [R_bank[b0]], [R_qsb[g]])
                          b1 = next_bank()
                          K.mm(bank(b1), pswap, qsb[g], True, True, [R_const, R_qsb[g]], [R_bank[b1]])
                          K.tt("dve", t1[g], bank(b0), cosT[sl], ALU.mult, [R_bank[b0], R_tab[sl]], [R_t1[g]])
                          K.tt("dve", t2[g], bank(b1), sinT[sl], ALU.mult, [R_bank[b1], R_tab[sl]], [R_t2[g]])
                          stage_out(dstT[j * 128:(j + 1) * 128, tsl], Rd[T],
                                    lambda s_ap, s_r, g=g: K.tt("pool", s_ap, t1[g], t2[g], ALU.add, [R_t1[g], R_t2[g]], [s_r]))
                  ckpt(6)
                  for (base, dstT, Rd) in ((16, sgsT, R_sgs), (24, sgaT, R_sga)):
                      for j in range(8):
                          b = fm_proj(base + j)
                          stage_out(dstT[j * 128:(j + 1) * 128, tsl], Rd[T],
                                    lambda s_ap, s_r, b=b: K.act(s_ap, bank(b), AF.Sigmoid, [R_bank[b]], [s_r]))
                  ckpt(7)
                  for j in range(4):
                      b = fm_proj(32 + j)
                      g = j % 2
                      K.act(sig[g], bank(b), AF.Sigmoid, [R_bank[b]], [R_sig[g]])
                      stage_out(szaT[j * 128:(j + 1) * 128, tsl], R_sza[T],
                                lambda s_ap, s_r, b=b, g=g: K.tt("dve", s_ap, bank(b), sig[g], ALU.mult,
                                                                 [R_bank[b], R_sig[g]], [s_r]))
                  for s in range(4):
                      rows = slice(T * 512 + s * 128, T * 512 + (s + 1) * 128)
                      b = next_bank()
                      for dc in range(8):
                          K.mm(bank(b), X[:, dc, s * 128:(s + 1) * 128], Wp[:, dc, 4608:5120], dc == 0, dc == 7,
                               [R_W, RX], [R_bank[b]])
                      stage_out(vS[rows, :], R_v[T],
                                lambda s_ap, s_r, b=b: K.copy("dve", s_ap, bank(b), [R_bank[b]], [s_r]))

          if "B" in phases:
              P.barrier()
              A.off = base_off
              TWO_PI = 2.0 * math.pi
              jsw = A.f32(128)
              mkf = A.f32(128)
              mkb = A.f32(128)
              dsk = A.f32(32)
              ES = A.bf16(64 * 128).rearrange("p (e n) -> p e n", e=64)
              R_cB = Res("constB")
              K.dma("sp", jsw, jswap_in, (), [R_cB])
              K.dma("sp", mkf, maskf_in, (), [R_cB])
              K.dma("sp", mkb, maskb_in, (), [R_cB])
              K.dma("sp", dsk, dsk_in, (), [R_cB])
              K.dma("sp", ES, esel_in.rearrange("p (e n) -> p e n", e=64), (), [R_cB])
              Cst_bf = A.bf16(64 * 128).rearrange("p (e n) -> p e n", e=64)
              Bst_bf = A.bf16(64 * 128).rearrange("p (e n) -> p e n", e=64)
              M_bf = A.bf16(32 * 128).rearrange("p (e n) -> p e n", e=32)
              S1tab = A.f32(10 * 64).rearrange("p (m e) -> p m e", m=10)
              S2tab = A.f32(10 * 64).rearrange("p (m e) -> p m e", m=10)
              R_fin = Res("finB")
              main_off = A.off
              R_s = Res("setupB")
              are = A.f32(64); aim = A.f32(64); ldt = A.f32(64)
              K.dma("sp", are, a_re_in, (), [R_s])
              K.dma("sp", aim, a_im_in, (), [R_s])
              K.dma("sp", ldt, ldt_in, (), [R_s])
              Bre = A.f32(1024).rearrange("p (e c) -> p e c", e=64)
              Bim = A.f32(1024).rearrange("p (e c) -> p e c", e=64)
              Cre = A.f32(1024).rearrange("p (e c) -> p e c", e=64)
              Cim = A.f32(1024).rearrange("p (e c) -> p e c", e=64)
              R_raw = Res("rawB")
              K.dma("sp", Bre, b_re_in.rearrange("p (e c) -> p e c", e=64), (), [R_raw])
              K.dma("sp", Bim, b_im_in.rearrange("p (e c) -> p e c", e=64), (), [R_raw])
              K.dma("sp", Cre, c_re_in.rearrange("p (e c) -> p e c", e=64), (), [R_raw])
              K.dma("sp", Cim, c_im_in.rearrange("p (e c) -> p e c", e=64), (), [R_raw])
              sc = lambda: A.f32(64)
              dtv = sc(); lr = sc(); ph = sc(); mag = sc(); kk = sc(); tmp = sc(); tmp2 = sc()
              r1 = sc(); r2 = sc(); sn = sc(); cs = sc(); minv = sc()
              PWre = A.f32(17 * 64).rearrange("p (n e) -> p n e", n=17)
              PWim = A.f32(17 * 64).rearrange("p (n e) -> p n e", n=17)
              w0re = sc(); w0im = sc(); den = sc()
              Rs, Ws = [R_s], [R_s]

              def v(out, in0, in1, op):
                  K.tt("dve", out, in0, in1, op, Rs, Ws)

              def vs(out, in0, s1, s2, op0, op1=None):
                  K.ts("dve", out, in0, s1, s2, op0, op1, Rs, Ws)

              K.act(dtv, ldt, AF.Exp, Rs, Ws)
              v(lr, are, dtv, ALU.mult)
              v(ph, aim, dtv, ALU.mult)
              K.act(mag, lr, AF.Exp, Rs, Ws)
              vs(kk, ph, math.pi, None, ALU.is_gt)
              for i in range(1, 9):
                  vs(tmp, ph, (2 * i + 1) * math.pi, None, ALU.is_gt)
                  v(kk, kk, tmp, ALU.add)
              vs(tmp, kk, -TWO_PI, None, ALU.mult)
              v(r1, ph, tmp, ALU.add)
              vs(r2, r1, math.pi / 2, None, ALU.add)
              vs(tmp, r2, math.pi, -TWO_PI, ALU.is_gt, ALU.mult)
              v(r2, r2, tmp, ALU.add)
              K.act(sn, r1, AF.Sin, Rs, Ws)
              K.act(cs, r2, AF.Sin, Rs, Ws)
              K.memset("dve", PWre[:, 8, :], 1.0, Ws)
              K.memset("dve", PWim[:, 8, :], 0.0, Ws)
              v(PWre[:, 9, :], mag, cs, ALU.mult)
              v(PWim[:, 9, :], mag, sn, ALU.mult)

              def cmul(ore, oim, are_, aim_, bre_, bim_):
                  v(tmp, are_, bre_, ALU.mult)
                  v(tmp2, aim_, bim_, ALU.mult)
                  v(kk, are_, bim_, ALU.mult)
                  v(den, aim_, bre_, ALU.mult)
                  v(ore, tmp, tmp2, ALU.subtract)
                  v(oim, kk, den, ALU.add)

              for n in range(2, 9):
                  cmul(PWre[:, 8 + n, :], PWim[:, 8 + n, :], PWre[:, 7 + n, :], PWim[:, 7 + n, :], PWre[:, 9, :], PWim[:, 9, :])
              for n in range(1, 9):
                  K.act(minv, lr, AF.Exp, Rs, Ws, scale=-2.0 * n)
                  v(PWre[:, 8 - n, :], PWre[:, 8 + n, :], minv, ALU.mult)
                  vs(tmp, PWim[:, 8 + n, :], -1.0, None, ALU.mult)
                  v(PWim[:, 8 - n, :], tmp, minv, ALU.mult)
              K.copy("dve", S1tab[:, 0, :], PWre[:, 16, :], Rs, Ws)
              K.copy("dve", S2tab[:, 0, :], PWim[:, 16, :], Rs, Ws)
              for m in range(1, 10):
                  v(tmp, S1tab[:, m - 1, :], S1tab[:, m - 1, :], ALU.mult)
                  v(tmp2, S2tab[:, m - 1, :], S2tab[:, m - 1, :], ALU.mult)
                  v(S1tab[:, m, :], tmp, tmp2, ALU.subtract)
                  K.stt(S2tab[:, m, :], S1tab[:, m - 1, :], 2.0, S2tab[:, m - 1, :], ALU.mult, ALU.mult, Rs, Ws)
              vs(tmp, PWre[:, 9, :], -1.0, None, ALU.add)
              v(den, are, are, ALU.mult)
              v(tmp2, aim, aim, ALU.mult)
              v(den, den, tmp2, ALU.add)
              K.recip(den, den, Rs, Ws)
              v(w0re, tmp, are, ALU.mult)
              v(tmp2, PWim[:, 9, :], aim, ALU.mult)
              v(w0re, w0re, tmp2, ALU.add)
              v(w0re, w0re, den, ALU.mult)
              v(w0im, PWim[:, 9, :], are, ALU.mult)
              v(tmp2, tmp, aim, ALU.mult)
              v(w0im, w0im, tmp2, ALU.subtract)
              v(w0im, w0im, den, ALU.mult)
              K.ts("dve", S2tab[64:128], S2tab[64:128], -1.0, None, ALU.mult, None, Rs, [R_s, R_fin])
              Tb = A.f32(1024).rearrange("p (e c) -> p e c", e=64)
              Tb2 = A.f32(1024).rearrange("p (e c) -> p e c", e=64)
              Bpre = A.f32(1024).rearrange("p (e c) -> p e c", e=64)
              Bpim = A.f32(1024).rearrange("p (e c) -> p e c", e=64)
              bc = lambda s_: s_.unsqueeze(2).to_broadcast([128, 64, 16])
              RW = [R_s, R_raw]
              K.tt("dve", Tb, Bre, bc(w0re), ALU.mult, RW, [R_raw])
              K.tt("dve", Tb2, Bim, bc(w0im), ALU.mult, RW, [R_raw])
              K.tt("dve", Bpre, Tb, Tb2, ALU.subtract, RW, [R_raw])
              K.tt("dve", Tb, Bim, bc(w0re), ALU.mult, RW, [R_raw])
              K.tt("dve", Tb2, Bre, bc(w0im), ALU.mult, RW, [R_raw])
              K.tt("dve", Bpim, Tb, Tb2, ALU.add, RW, [R_raw])
              Cst32 = A.f32(4096).rearrange("p (g j c) -> p g j c", g=32, j=8)
              Bneg32 = A.f32(4096).rearrange("p (g j c) -> p g j c", g=32, j=8)
              Bpos32 = A.f32(4096).rearrange("p (g j c) -> p g j c", g=32, j=8)
              M32 = A.f32(4096).rearrange("p (g n) -> p g n", g=32)
              T1 = A.f32(512).rearrange("p (g c) -> p g c", g=32)
              T2 = A.f32(512).rearrange("p (g c) -> p g c", g=32)
              R_bigh = [Res("bigB0"), Res("bigB1")]
              R_big = None
              R_Th = [Res("TB0"), Res("TB1")]
              R_M32 = Res("M32")
              R_T = Res("TB")

              def cscale(out4, j, MRE, MIM, n, d, ctype):
                  gs_ = slice(d * 32, (d + 1) * 32)
                  wre = PWre[:, 8 + n, gs_]
                  wim = PWim[:, 8 + n, gs_]
                  for half in range(2):
                      ps_ = slice(half * 64, (half + 1) * 64)
                      bcw = lambda w_: w_[ps_].unsqueeze(2).to_broadcast([64, 32, 16])
                      o = out4[ps_, :, j, :]
                      eng = "dve"
                      R_T = R_Th[half]
                      R_big = R_bigh[half]
                      if half == 0:
                          K.tt(eng, T1[ps_], MRE[ps_, gs_, :], bcw(wre), ALU.mult, [R_s, R_raw], [R_T])
                          K.tt(eng, T2[ps_], MIM[ps_, gs_, :], bcw(wim), ALU.mult, [R_s, R_raw], [R_T])
                          K.tt(eng, o, T1[ps_], T2[ps_], ALU.subtract, [R_T], [R_big])
                      else:
                          K.tt(eng, T1[ps_], MRE[ps_, gs_, :], bcw(wim), ALU.mult, [R_s, R_raw], [R_T])
                          K.tt(eng, T2[ps_], MIM[ps_, gs_, :], bcw(wre), ALU.mult, [R_s, R_raw], [R_T])
                          K.tt(eng, o, T1[ps_], T2[ps_], ALU.add, [R_T], [R_big])
                          if ctype:
                              K.ts(eng, o, o, -1.0, None, ALU.mult, None, [R_big], [R_big])

              rb_i = [0]

              def ringA():
                  b = rb_i[0] % 4
                  rb_i[0] += 1
                  return b

              for d in range(2):
                  for j in range(8):
                      cscale(Cst32, j, Cre, Cim, (j + 1) if d == 0 else (8 - j), d, True)
                      cscale(Bneg32, j, Bpre, Bpim, -(j + 1) if d == 0 else (j - 8), d, False)
                      cscale(Bpos32, j, Bpre, Bpim, (7 - j) if d == 0 else j, d, False)
                  K.copy("act", Cst_bf[:, d * 32:(d + 1) * 32, :], Cst32.rearrange("p g j c -> p g (j c)"), R_bigh, [R_fin])
                  Bpos_f = Bpos32.rearrange("p g j c -> p g (j c)")
                  Bneg_f = Bneg32.rearrange("p g j c -> p g (j c)")
                  Cst_f = Cst32.rearrange("p g j c -> p g (j c)")
                  for g4 in range(8):
                      b = ringA()
                      for q in range(4):
                          g = g4 * 4 + q
                          K.tr(bank(b)[:, q * 128:(q + 1) * 128], Bpos_f[:, g, :], ident32, R_bigh + [R_const], [R_bank[b]])
                      K.copy("dve", Bst_bf[:, d * 32 + g4 * 4:d * 32 + g4 * 4 + 4, :],
                             bank(b).rearrange("p (q n) -> p q n", q=4), [R_bank[b]], [R_fin])
                  for g4 in range(8):
                      b = ringA()
                      for q in range(4):
                          g = g4 * 4 + q
                          K.mm(bank(b)[:, q * 128:(q + 1) * 128], Bneg_f[:, g, :], Cst_f[:, g, :], True, True,
                               R_bigh, [R_bank[b]])
                      pv = bank(b).rearrange("p (q n) -> p q n", q=4)
                      mk = (mkf if d == 0 else mkb).unsqueeze(1).to_broadcast([128, 4, 128])
                      if d == 0:
                          K.tt("dve", M32[:, g4 * 4:g4 * 4 + 4, :], pv, mk, ALU.mult, [R_bank[b], R_cB], [R_M32])
                      else:
                          Tm = Tb.rearrange("p e c -> p (e c)")[:, 0:512].rearrange("p (q n) -> p q n", q=4)
                          K.tt("dve", Tm, pv, mk, ALU.mult, [R_bank[b], R_cB], [R_raw])
                          K.tt("dve", M32[:, g4 * 4:g4 * 4 + 4, :], M32[:, g4 * 4:g4 * 4 + 4, :], Tm, ALU.add,
                               [R_raw], [R_M32])
              for g in range(32):
                  K.stt(M32[:, g, :], ident32, dsk[:, g:g + 1], M32[:, g, :], ALU.mult, ALU.add, [R_const, R_cB], [R_M32])
              K.copy("act", M_bf, M32, [R_M32], [R_fin])

              ckpt(30)
              P.barrier()
              A.off = main_off
              KMAX = LMAX // 8
              XS = A.bf16(LMAX)
              R_XS = Res("XS")
              U = [A.bf16(KMAX) for _ in range(8)]
              R_U = [Res("U%d" % i) for i in range(8)]
              YU = [A.bf16(KMAX) for _ in range(8)]
              R_YU = [Res("YU%d" % i) for i in range(8)]
              YT = A.bf16(LMAX)
              R_YT = Res("YT")
              NS = 4
              Xs = [A.f32(KMAX) for _ in range(NS)]
              R_X = [Res("X%d" % i) for i in range(NS)]
              Hb = [A.bf16(KMAX + 2) for _ in range(NS)]
              R_H = [Res("H%d" % i) for i in range(NS)]
              NR = 8
              Rm = [A.f32(128) for _ in range(NR)]
              R_Rm = [Res("R%d" % i) for i in range(NR)]
              Rh = [A.bf16(128) for _ in range(NR)]
              Rl = [A.bf16(128) for _ in range(NR)]
              R_Rh = [Res("Rh%d" % i) for i in range(NR)]
              R_Rl = [Res("Rl%d" % i) for i in range(NR)]
              Xb = [A.bf16(KMAX) for _ in range(NS)]
              R_Xc = [[Res("X%d_%d" % (i, c_)) for c_ in range(2)] for i in range(NS)]
              R_Xbc = [[Res("Xb%d_%d" % (i, c_)) for c_ in range(2)] for i in range(NS)]
              rm_i = [0]
              rbB_i = [0]

              def ringB():
                  b = 4 + rbB_i[0] % 4
                  rbB_i[0] += 1
                  return b

              for sq in range(2):
                  L = Ls[sq]
                  Kc = L // 8
                  nstep = int(round(math.log2(Kc)))
                  tok0 = seq_off[sq]
                  CW = min(512, Kc)
                  nch = Kc // CW
                  for blk in range(4):
                      XSJ = XS[:, 0:L].rearrange("p (j k) -> p j k", j=8)
                      K.dma("sp", XSJ, xsT[blk * 128:(blk + 1) * 128, :, tok0 // 8:tok0 // 8 + Kc],
                            [R_xsT[(tok0 + t_) // 512] for t_ in range(0, L, 512)], [R_XS])
                      for gl in range(8):
                          for ch in range(nch):
                              b = ringA()
                              for j in range(8):
                                  K.mm(bank(b)[:, 0:CW], ES[:, gl * 8 + j, :], XSJ[:, j, ch * CW:(ch + 1) * CW], j == 0, j == 7,
                                       [R_cB, R_XS], [R_bank[b]])
                              K.copy(("dve", "act")[ch % 2], U[gl][:, ch * CW:(ch + 1) * CW], bank(b)[:, 0:CW], [R_bank[b]], [R_U[gl]])
                      for gp in range(4):
                          streams = [(gp * 2 + (s_ // 2), s_ % 2) for s_ in range(4)]
                          for s_, (gl, d) in enumerate(streams):
                              g = blk * 8 + gl
                              for ch in range(nch):
                                  b = ringA()
                                  K.mm(bank(b)[:, 0:CW], Bst_bf[:, d * 32 + g, :], U[gl][:, ch * CW:(ch + 1) * CW], True, True,
                                       [R_fin, R_U[gl]], [R_bank[b]])
                                  K.copy(("act", "dve")[ch % 2], Xs[s_][:, ch * CW:(ch + 1) * CW], bank(b)[:, 0:CW], [R_bank[b]], [R_Xc[s_][ch]])
                                  K.copy(("dve", "act")[ch % 2], Xb[s_][:, ch * CW:(ch + 1) * CW], Xs[s_][:, ch * CW:(ch + 1) * CW],
                                         [R_Xc[s_][ch]], [R_Xbc[s_][ch]])
                          ckpt(31)
                          def rbuild(m):
                              out = []
                              for s_, (gl, d) in enumerate(streams):
                                  g = blk * 8 + gl
                                  ri = rm_i[0] % NR
                                  rm_i[0] += 1
                                  K.ts("dve", Rm[ri], ident32, S1tab[:, m, d * 32 + g:d * 32 + g + 1], None, ALU.mult, None,
                                       [R_const, R_fin], [R_Rm[ri]])
                                  K.stt(Rm[ri], jsw, S2tab[:, m, d * 32 + g:d * 32 + g + 1], Rm[ri], ALU.mult, ALU.add,
                                        [R_cB, R_fin], [R_Rm[ri]])
                                  K.copy("act", Rh[ri], Rm[ri], [R_Rm[ri]], [R_Rh[ri]])
                                  K.tt("pool", Rl[ri], Rm[ri], Rh[ri], ALU.subtract, [R_Rm[ri], R_Rh[ri]], [R_Rl[ri]])
                                  out.append(ri)
                              return out

                          ris = rbuild(0)
                          for m in range(nstep):
                              sft = 1 << m
                              work = []
                              ris_next = rbuild(m + 1) if m + 1 < nstep else None
                              for s_, (gl, d) in enumerate(streams):
                                  ri = ris[s_]
                                  for ch in range(nch):
                                      if d == 0:
                                          lo = max(ch * CW, sft); hi = (ch + 1) * CW
                                          src = (lo - sft, hi - sft)
                                      else:
                                          lo = ch * CW; hi = min((ch + 1) * CW, Kc - sft)
                                          src = (lo + sft, hi + sft)
                                      n = hi - lo
                                      if n <= 0:
                                          continue
                                      b = ringB()
                                      srcR = [R_Xbc[s_][c_] for c_ in range(src[0] // CW, (src[1] - 1) // CW + 1)]
                                      K.mm(bank(b)[:, 0:n], Rh[ri], Xb[s_][:, src[0]:src[1]], True, False,
                                           [R_Rh[ri]] + srcR, [R_bank[b]])
                                      K.mm(bank(b)[:, 0:n], Rl[ri], Xb[s_][:, src[0]:src[1]], False, True,
                                           [R_Rl[ri]] + srcR, [R_bank[b]])
                                      work.append((s_, ch, lo, hi, n, b))
                                      if len(work) % 4 == 0:
                                          for (s2, ch2, lo2, hi2, n2, b2) in work[-4:]:
                                              K.tt("dve", Xs[s2][:, lo2:hi2], Xs[s2][:, lo2:hi2], bank(b2)[:, 0:n2], ALU.add,
                                                   [R_bank[b2]], [R_Xc[s2][ch2]])
                              rem = len(work) % 4
                              for (s2, ch2, lo2, hi2, n2, b2) in (work[-rem:] if rem else []):
                                  K.tt("dve", Xs[s2][:, lo2:hi2], Xs[s2][:, lo2:hi2], bank(b2)[:, 0:n2], ALU.add,
                                       [R_bank[b2]], [R_Xc[s2][ch2]])
                              if m + 1 < nstep:
                                  for (s2, ch2, lo2, hi2, n2, b2) in work:
                                      K.copy("act", Xb[s2][:, lo2:hi2], Xs[s2][:, lo2:hi2], [R_Xc[s2][ch2]], [R_Xbc[s2][ch2]])
                                  ris = ris_next
                          for s_, (gl, d) in enumerate(streams):
                              if d == 0:
                                  K.memset("pool", Hb[s_][:, 0:1], 0.0, [R_H[s_]])
                                  K.copy("act", Hb[s_][:, 1:Kc + 1], Xs[s_][:, 0:Kc], R_Xc[s_], [R_H[s_]])
                              else:
                                  K.memset("pool", Hb[s_][:, Kc:Kc + 1], 0.0, [R_H[s_]])
                                  K.copy("act", Hb[s_][:, 0:Kc], Xs[s_][:, 0:Kc], R_Xc[s_], [R_H[s_]])
                          for q in range(2):
                              gl = gp * 2 + q
                              g = blk * 8 + gl
                              sf, sb = q * 2, q * 2 + 1
                              for ch in range(nch):
                                  b = ringA()
                                  cs_ = slice(ch * CW, (ch + 1) * CW)
                                  K.mm(bank(b)[:, 0:CW], M_bf[:, g, :], U[gl][:, cs_], True, False, [R_fin, R_U[gl]], [R_bank[b]])
                                  K.mm(bank(b)[:, 0:CW], Cst_bf[:, g, :], Hb[sf][:, ch * CW:ch * CW + CW], False, False,
                                       [R_fin, R_H[sf]], [R_bank[b]])
                                  K.mm(bank(b)[:, 0:CW], Cst_bf[:, 32 + g, :], Hb[sb][:, ch * CW + 1:ch * CW + CW + 1], False, True,
                                       [R_fin, R_H[sb]], [R_bank[b]])
                                  K.copy(("dve", "act")[ch % 2], YU[gl][:, cs_], bank(b)[:, 0:CW], [R_bank[b]], [R_YU[gl]])
                      YT3 = YT[:, 0:L].rearrange("p (k j) -> p k j", j=8)
                      for j in range(8):
                          for ch in range(nch):
                              b = ringA()
                              for gl in range(8):
                                  K.mm(bank(b)[:, 0:CW], ES[:, j * 8 + gl, :], YU[gl][:, ch * CW:(ch + 1) * CW], gl == 0, gl == 7,
                                       [R_cB, R_YU[gl]], [R_bank[b]])
                              K.copy(("dve", "act")[(j + ch) % 2], YT3[:, ch * CW:(ch + 1) * CW, j], bank(b)[:, 0:CW], [R_bank[b]], [R_YT])
                      K.dma("sp", yT[blk * 128:(blk + 1) * 128, tok0:tok0 + L], YT[:, 0:L], [R_YT], [R_yT[sq]])

          if "T" in phases:
              P.barrier()
              A.off = base_off
              lamt = A.f32(256)
              sgcol = A.f32(1)
              R_cT = Res("constT")
              K.dma("sp", lamt, lam_in, (), [R_cT])
              K.dma("sp", sgcol, sg_in, (), [R_cT])
              jk = A.f32(64)
              s12 = A.f32(2)
              e12 = A.f32(2)
              lamneg = A.f32(1)
              ones32 = A.f32(128)
              onesb = A.bf16(128)
              K.memset("dve", ones32, 1.0, [R_cT])
              K.memset("dve", onesb, 1.0, [R_cT])
              selt = A.f32(128)
              K.memset("dve", selt, 0.0, [R_cT])
              K.memset("dve", selt[0:1, :], 1.0, [R_cT])
              K.memset("dve", selt[32:33, :], 1.0, [R_cT])
              lsb = A.f32(512)
              lsb1 = A.f32(512)
              R_lc = Res("lc")
              oc0 = A.f32(512)
              oc1 = A.f32(512)
              K.stt(jk, lamt[:, 0:64], 1.0, lamt[:, 64:128], ALU.mult, ALU.mult, [R_cT], [R_cT], accum_out=s12[:, 0:1])
              K.stt(jk, lamt[:, 128:192], 1.0, lamt[:, 192:256], ALU.mult, ALU.mult, [R_cT], [R_cT], accum_out=s12[:, 1:2])
              K.act(e12, s12, AF.Exp, [R_cT], [R_cT])
              K.tt("dve", lamneg, e12[:, 1:2], e12[:, 0:1], ALU.subtract, [R_cT], [R_cT])
              K.ts("dve", lamneg, lamneg, -0.2, None, ALU.add, None, [R_cT], [R_cT])
              K.ts("dve", sgcol, sgcol, 0.8, None, ALU.mult, None, [R_cT], [R_cT])
              NCK = LMAX // 128
              KTb = [A.bf16(LMAX) for _ in range(2)]
              QTb = [A.bf16(LMAX) for _ in range(2)]
              Vb = [A.bf16(NCK * 128).rearrange("p (c e) -> p c e", e=128) for _ in range(2)]
              R_KT = [Res("KT0"), Res("KT1")]
              R_QT = [Res("QT0"), Res("QT1")]
              R_V = [Res("V0"), Res("V1")]
              NPT = 4
              PT = [A.bf16(1024) for _ in range(NPT)]
              R_PT = [Res("PT%d" % i) for i in range(NPT)]
              PS2 = [A.bf16(1024) for _ in range(2)]
              R_PS2 = [Res("PS2_0"), Res("PS2_1")]
              p2_i = [0]
              sums_q = []
              PS4 = [A.bf16(1024) for _ in range(2)]
              R_PS4 = [Res("PS4_0"), Res("PS4_1")]
              p4_i = [0]
              szat = [A.bf16(512) for _ in range(2)]
              R_szat = [Res("szat0"), Res("szat1")]
              yst = [A.bf16(512) for _ in range(2)]
              R_yst = [Res("yst0"), Res("yst1")]
              rl0 = A.f32(512); rl1 = A.f32(512); o0 = A.f32(512); o1 = A.f32(512); oo = A.f32(512); sqt = A.f32(512)
              Dm = [A.f32(128) for _ in range(4)]
              R_Dm = [Res("Dm%d" % i_) for i_ in range(4)]
              sm = A.f32(16)
              nh = A.f32(4)
              R_fz = Res("finz")
              R_sq = Res("sq")
              K.memset("dve", nh, -0.5, [R_fz])
              QW = 512
              heads = [(sq, h) for sq in range(2) for h in range(4)]
              ckpt(20)

              def load_head(i):
                  sq, h = heads[i]
                  L = Ls[sq]; tok0 = seq_off[sq]; sl = i % 2
                  tiles = [(tok0 + t_) // 512 for t_ in range(0, L, 512)]
                  K.dma("sp", KTb[sl][:, 0:L], kT[h * 128:(h + 1) * 128, tok0:tok0 + L], [R_kT[t_] for t_ in tiles], [R_KT[sl]])
                  K.dma("sp", QTb[sl][:, 0:L], qT[h * 128:(h + 1) * 128, tok0:tok0 + L], [R_qT[t_] for t_ in tiles], [R_QT[sl]])
                  K.dma("sp", Vb[sl][:, 0:L // 128, :],
                        vS[tok0:tok0 + L, h * 128:(h + 1) * 128].rearrange("(c p) e -> p c e", p=128),
                        [R_v[t_] for t_ in tiles], [R_V[sl]])

              sb_i = [0]
              pt_i = [0]
              qt_i = [0]
              dm_i = [0]
              load_head(0)
              pending = []
              for hi, (sq, h) in enumerate(heads):
                  L = Ls[sq]; tok0 = seq_off[sq]; sl = hi % 2
                  if hi + 1 < len(heads):
                      load_head(hi + 1)
                  KT = KTb[sl]; QT = QTb[sl]; V = Vb[sl]
                  nck = L // 128
                  for q0 in range(0, L, QW):
                      qs = qt_i[0] % 2
                      qt_i[0] += 1
                      trow = tok0 + q0
                      K.dma("sp", szat[qs], szaT[h * 128:(h + 1) * 128, trow:trow + QW], [R_sza[trow // 512]], [R_szat[qs]])

                      def qk(c):
                          b = 2 * (sb_i[0] % 2)
                          sb_i[0] += 1
                          for t in range(2):
                              K.mm(bank(b + t), KT[t * 64:(t + 1) * 64, c * 128:(c + 1) * 128],
                                   QT[t * 64:(t + 1) * 64, q0:q0 + QW], True, True, [R_KT[sl], R_QT[sl]], [R_bank[b + t]])
                          return b

                      if sb_i[0] % 2 == 1:
                          sb_i[0] += 1
                      slot_b = [qk(0), qk(1) if nck > 1 else 0]
                      for c in range(nck):
                          b = slot_b[c % 2]
                          pi = pt_i[0] % NPT
                          pt_i[0] += 1
                          K.act(PT[pi], PS[:, b * 512:b * 512 + 1024], AF.Exp, [R_bank[b], R_bank[b + 1]], [R_PT[pi]], scale=0.125)
                          for (cc, fn_) in pending:
                              if cc == c:
                                  fn_()
                          pending = [pf for pf in pending if pf[0] > c]
                          if c + 2 < nck:
                              slot_b[c % 2] = qk(c + 2)
                          for t in range(2):
                              K.mm(bank(4 + t), V[:, c, :], PT[pi][:, t * 512:(t + 1) * 512], c == 0, c == nck - 1,
                                   [R_PT[pi], R_V[sl]], [R_bank[4 + t]])
                          if sums_q:
                              sums_q.pop(0)()
                          if c % 2 == 1:
                              k2 = p2_i[0] % 2
                              p2_i[0] += 1
                              pprev = (pt_i[0] - 2) % NPT
                              K.tt("dve", PS2[k2], PT[pprev], PT[pi], ALU.add, [R_PT[pprev], R_PT[pi]], [R_PS2[k2]])
                              if c % 4 == 3:
                                  k4 = p4_i[0] % 2
                                  p4_i[0] += 1
                                  K.tt("dve", PS4[k4], PS2[1 - k2], PS2[k2], ALU.add, [R_PS2[0], R_PS2[1]], [R_PS4[k4]])

                                  def do_sums(k4=k4, first=(c == 3), last=(c == nck - 1)):
                                      for t in range(2):
                                          K.mm(bank(6 + t), onesb, PS4[k4][:, t * 512:(t + 1) * 512], first, last,
                                               [R_PS4[k4], R_cT], [R_bank[6 + t]])
                                  sums_q.append(do_sums)
                      while sums_q:
                          sums_q.pop(0)()
                      for (cc, fn_) in pending:
                          fn_()
                      pending = []
                      K.copy("dve", oc0, bank(4), [R_bank[4]], [R_fz])
                      K.copy("dve", oc1, bank(5), [R_bank[5]], [R_fz])
                      K.copy("act", lsb, bank(6), [R_bank[6]], [R_lc])
                      K.copy("act", lsb1, bank(7), [R_bank[7]], [R_lc])

                      def mk_stages(qs=qs, h=h, trow=trow):
                          def st0a():
                              K.recip(rl0, lsb, [R_lc], [R_fz])

                          def st0b():
                              K.recip(rl1, lsb1, [R_lc], [R_fz])

                          def st1():
                              K.tt("dve", o0, oc0, rl0, ALU.mult, [R_fz], [R_fz])
                              K.tt("dve", o1, oc1, rl1, ALU.mult, [R_fz], [R_fz])
                              K.stt(oo, o1, lamneg[:, 0:1], o0, ALU.mult, ALU.add, [R_fz, R_cT], [R_fz])
                              K.tt("dve", sqt, oo, oo, ALU.mult, [R_fz], [R_sq])

                          def st2():
                              b = 2 * (sb_i[0] % 2)
                              for s4 in range(4):
                                  K.mm(bank(b)[:, s4:s4 + 1], sqt[:, s4 * 128:(s4 + 1) * 128], ones32[:, 0:1], True, True,
                                       [R_sq, R_cT], [R_bank[b]])
                              K.ts("dve", sm[:, 0:4], bank(b)[:, 0:4], 1.0 / 128, 1e-5, ALU.mult, ALU.add, [R_bank[b]], [R_fz])
                              K.tt("pool", sm[:, 4:8], sm[:, 0:4], nh, ALU.pow, [R_fz], [R_fz])

                          def st3():
                              for s4 in range(4):
                                  di = dm_i[0] % 4
                                  dm_i[0] += 1
                                  K.ts("dve", Dm[di], ident32, sm[:, 4 + s4:5 + s4], None, ALU.mult, None, [R_const, R_fz], [R_Dm[di]])

                          def st4():
                              b = 2 * (sb_i[0] % 2)
                              for s4 in range(4):
                                  di = (dm_i[0] - 4 + s4) % 4
                                  K.mm(bank(b)[:, s4 * 128:(s4 + 1) * 128], ones32, Dm[di], True, True, [R_Dm[di], R_cT], [R_bank[b]])
                              K.tt("dve", o0, oo, bank(b), ALU.mult, [R_bank[b], R_fz], [R_fz])
                              K.stt(yst[qs], o0, sgcol[:, 0:1], szat[qs], ALU.mult, ALU.mult, [R_fz, R_cT, R_szat[qs]], [R_yst[qs]])
                              K.dma("sp", yaT_d[h * 128:(h + 1) * 128, trow:trow + QW], yst[qs], [R_yst[qs]], [R_ya[trow // 512]])
                          return [(1, st0a), (3, st0b), (5, st1), (9, st2), (10, st3), (13, st4)]

                      pending = mk_stages()
                      ckpt(25)
                      ckpt(100 + qt_i[0])
              for (cc, fn_) in pending:
                  fn_()
              pending = []

          if "C" in phases:
              P.barrier()
              A.off = base_off
              wglu = A.bf16(4 * 512).rearrange("p (a n) -> p a n", a=4)
              wbr = A.bf16(8 * 1024).rearrange("p (a n) -> p a n", a=8)
              wout = A.bf16(8 * 1024).rearrange("p (a n) -> p a n", a=8)
              bglu = A.f32(4)
              fgt = A.f32(1024)
              R_wC = Res("wC")
              K.dma("sp", bglu, bglu_in, (), [R_wC])
              K.dma("sp", fgt, fg_in, (), [R_wC])
              nhc = A.f32(4)
              K.memset("dve", nhc, -0.5, [R_wC])
              ytb = [A.bf16(2048).rearrange("p (a t) -> p a t", a=4) for _ in range(2)]
              szb = [A.bf16(2048).rearrange("p (a t) -> p a t", a=4) for _ in range(2)]
              sgsb = [A.bf16(4096).rearrange("p (a t) -> p a t", a=8) for _ in range(2)]
              sgab = [A.bf16(4096).rearrange("p (a t) -> p a t", a=8) for _ in range(2)]
              yab = [A.bf16(2048).rearrange("p (a t) -> p a t", a=4) for _ in range(2)]
              xtb = [A.f32(4096).rearrange("p (s d) -> p s d", s=4) for _ in range(2)]
              R_ld = [Res("ldC0"), Res("ldC1")]
              gbf = A.bf16(2048).rearrange("p (a t) -> p a t", a=4)
              ys3 = A.bf16(2048).rearrange("p (a t) -> p a t", a=4)
              yaT = A.bf16(2048).rearrange("p (a t) -> p a t", a=4)
              mT = A.bf16(4096).rearrange("p (a t) -> p a t", a=8)
              R_g = Res("g"); R_ys3 = Res("ys3"); R_yaT = Res("yaT"); R_mT = Res("mT")
              tA = [A.f32(512) for _ in range(2)]
              tB = [A.f32(512) for _ in range(2)]
              R_tA = [Res("tA0"), Res("tA1")]
              R_tB = [Res("tB0"), Res("tB1")]
              resb = [A.f32(1024) for _ in range(2)]
              outb = [A.f32(1024) for _ in range(2)]
              R_res = [Res("res0"), Res("res1")]
              R_outb = [Res("out0"), Res("out1")]
              junkc = A.f32(1024)
              R_jc = Res("junkc")
              smc = A.f32(8)
              R_smc = Res("smc")
              wstg = [xtb[1].rearrange("p s d -> p (s d)")[:, i * 2048:(i + 1) * 2048] for i in range(2)]
              R_wstg = [Res("wstgC0"), Res("wstgC1")]
              wi = [0]

              def wload(dst, src_ap):
                  n = dst.shape[-1]
                  sl = wi[0] % 2
                  wi[0] += 1
                  K.dma("sp", wstg[sl][:, 0:n], src_ap, (), [R_wstg[sl]])
                  K.copy(("dve", "pool")[sl], dst, wstg[sl][:, 0:n], [R_wstg[sl]], [R_wC])

              wg_v = wglu_in.rearrange("(a p) n -> p a n", p=128)
              for a in range(4):
                  wload(wglu[:, a, :], wg_v[:, a, :])
              wb_v = wbr_in.rearrange("b (a p) n -> p b a n", p=128)
              for br in range(2):
                  for a in range(4):
                      wload(wbr[:, br * 4 + a, :], wb_v[:, br, a, :])
              wo_v = wout_in.rearrange("(a p) n -> p a n", p=128)
              for a in range(8):
                  wload(wout[:, a, :], wo_v[:, a, :])
              for rw in R_wstg:
                  for kk_, vv_ in list(rw.r.items()) + ([rw.w] if rw.w else []):
                      if R_ld[1].r.get(kk_, 0) < vv_:
                          R_ld[1].r[kk_] = vv_

              def loadC(T):
                  sl = T % 2
                  ts_ = slice(T * 512, (T + 1) * 512)
                  W_ = [R_ld[sl]]
                  K.dma("sp", ytb[sl], yT[:, ts_].rearrange("(a p) t -> p a t", p=128), [R_yT[0], R_yT[1]], W_)
                  K.dma("sp", szb[sl], szsT[:, ts_].rearrange("(a p) t -> p a t", p=128), [R_szsT[T]], W_)
                  K.dma("sp", sgsb[sl], sgsT[:, ts_].rearrange("(a p) t -> p a t", p=128), [R_sgs[T]], W_)
                  K.dma("sp", sgab[sl], sgaT[:, ts_].rearrange("(a p) t -> p a t", p=128), [R_sga[T]], W_)
                  K.dma("sp", yab[sl], yaT_d[:, ts_].rearrange("(a p) t -> p a t", p=128), [R_ya[T]], W_)
                  K.dma("sp", xtb[sl], x_in[ts_, :].rearrange("(s p) d -> p s d", p=128), (), W_)

              rc_i = [0]

              def ringC():
                  b = rc_i[0] % 8
                  rc_i[0] += 1
                  return b

              loadC(0)
              for T in range(NT):
                  sl = T % 2
                  if T + 1 < NT:
                      loadC(T + 1)
                  RL = [R_ld[sl]]
                  yt = ytb[sl]
                  for a in range(4):
                      g_ = a % 2
                      K.tt("dve", tA[g_], yt[:, a, :], yt[:, a, :], ALU.mult, RL, [R_tA[g_]])
                      K.ts("dve", tA[g_], tA[g_], 0.044715, 1.0, ALU.mult, ALU.add, [R_tA[g_]], [R_tA[g_]])
                      K.tt("pool", tA[g_], tA[g_], yt[:, a, :], ALU.mult, RL + [R_tA[g_]], [R_tA[g_]])
                      K.act(tB[g_], tA[g_], AF.Sigmoid, [R_tA[g_]], [R_tB[g_]], scale=1.5957691216057308)
                      K.tt("dve", gbf[:, a, :], yt[:, a, :], tB[g_], ALU.mult, RL + [R_tB[g_]], [R_g])
                  for e_ in range(4):
                      b = ringC()
                      for a in range(4):
                          K.mm(bank(b), wglu[:, a, e_ * 128:(e_ + 1) * 128], gbf[:, a, :], a == 0, a == 3, [R_wC, R_g], [R_bank[b]])
                      g_ = e_ % 2
                      K.act(tB[g_], bank(b), AF.Sigmoid, [R_bank[b], R_wC], [R_tB[g_]], bias=bglu[:, e_:e_ + 1])
                      K.tt("dve", tA[g_], gbf[:, e_, :], tB[g_], ALU.mult, [R_g, R_tB[g_]], [R_tA[g_]])
                      K.tt("pool", ys3[:, e_, :], tA[g_], szb[sl][:, e_, :], ALU.mult, RL + [R_tA[g_]], [R_ys3])
                  for e_ in range(8):
                      bs = ringC()
                      for a in range(4):
                          K.mm(bank(bs), wbr[:, a, e_ * 128:(e_ + 1) * 128], ys3[:, a, :], a == 0, a == 3, [R_wC, R_ys3], [R_bank[bs]])
                      ba = ringC()
                      for a in range(4):
                          K.mm(bank(ba), wbr[:, 4 + a, e_ * 128:(e_ + 1) * 128], yab[sl][:, a, :], a == 0, a == 3, [R_wC] + RL, [R_bank[ba]])
                      g_ = e_ % 2
                      K.tt("dve", tA[g_], bank(bs), sgsb[sl][:, e_, :], ALU.mult, RL + [R_bank[bs]], [R_tA[g_]])
                      K.tt("dve", tB[g_], bank(ba), sgab[sl][:, e_, :], ALU.mult, RL + [R_bank[ba]], [R_tB[g_]])
                      K.tt("pool", mT[:, e_, :], tA[g_], tB[g_], ALU.add, [R_tA[g_], R_tB[g_]], [R_mT])
                  for s in range(4):
                      rs = s % 2
                      for hf in range(2):
                          b = ringC()
                          for a in range(8):
                              K.mm(bank(b), mT[:, a, s * 128:(s + 1) * 128], wout[:, a, hf * 512:(hf + 1) * 512], a == 0, a == 7,
                                   [R_wC, R_mT], [R_bank[b]])
                          K.tt("dve", resb[rs][:, hf * 512:(hf + 1) * 512], bank(b), xtb[sl][:, s, hf * 512:(hf + 1) * 512], ALU.add,
                               RL + [R_bank[b]], [R_res[rs]])
                      K.act(junkc, resb[rs], AF.Square, [R_res[rs]], [R_jc, R_smc], accum_out=smc[:, 0:1])
                      K.ts("dve", smc[:, 1:2], smc[:, 0:1], 1.0 / D, 1e-6, ALU.mult, ALU.add, [R_smc], [R_smc])
                      K.tt("pool", smc[:, 2:3], smc[:, 1:2], nhc[:, 0:1], ALU.pow, [R_smc, R_wC], [R_smc])
                      K.act(junkc, resb[rs], AF.Copy, [R_res[rs], R_smc], [R_jc], scale=smc[:, 2:3])
                      K.tt("pool", outb[rs], junkc, fgt, ALU.mult, [R_jc, R_wC], [R_outb[rs]])
                      rows = slice(T * 512 + s * 128, T * 512 + (s + 1) * 128)
                      K.dma("sp", y_out[rows, :], outb[rs], [R_outb[rs]], ())

        except _Stop:
            pass
        P.finish()
        P.emit(nc, st)
    return nc


def _rope_tables(lmax):
    half = 32
    inv = (1.0 / (10000.0 ** (np.arange(half, dtype=np.float32) * 2.0 / 64))).astype(np.float32)
    ang = (np.arange(lmax, dtype=np.float32)[:, None] * inv[None, :]).astype(np.float32)
    c = np.cos(ang.astype(np.float64)); s = np.sin(ang.astype(np.float64))
    C = np.zeros((128, lmax), np.float32); S = np.zeros((128, lmax), np.float32)
    for t in range(2):
        for d in range(64):
            C[t * 64 + d] = c[:, d % 32]
            S[t * 64 + d] = (-1.0 if d < 32 else 1.0) * s[:, d % 32]
    return C, S


def _host_consts(lmax):
    C, S = _rope_tables(lmax)
    ident = np.eye(128, dtype=np.float32)
    jswap = np.zeros((128, 128), np.float32)
    for k in range(128):
        jswap[k, (k + 64) % 128] = 1.0
    esel = np.zeros((8, 8, 128, 128), np.float32)
    for a in range(8):
        for b in range(8):
            for c in range(16):
                esel[a, b, a * 16 + c, b * 16 + c] = 1.0
    esel = esel.reshape(64, 128, 128).transpose(1, 0, 2).reshape(128, 64 * 128).astype(ml_dtypes.bfloat16)
    jj = np.arange(128) // 16
    maskf = (jj[:, None] <= jj[None, :]).astype(np.float32)
    maskb = (jj[:, None] >= jj[None, :]).astype(np.float32)
    pswap = np.zeros((128, 128), np.float32)
    for m in range(128):
        pswap[(m // 64) * 64 + ((m % 64) + 32) % 64, m] = 1.0
    pswap = pswap.astype(ml_dtypes.bfloat16)
    return dict(ropeC=C, ropeS=S, ident=ident, jswap=jswap, pswap=pswap, esel=np.ascontiguousarray(esel), maskf=maskf, maskb=maskb)


def _layout_weights(inp):
    w = np.asarray(inp["w_in"][0], np.float32)
    xs, zs, q, k, v, za = [w[:, i * 512:(i + 1) * 512] for i in range(6)]
    gs = w[:, 3072:4096]; ga = w[:, 4096:5120]
    f = np.arange(512)
    swap = (f // 64) * 64 + ((f % 64) + 32) % 64
    w_ext = np.concatenate([xs, zs, q, k, gs, ga, za, v], axis=1)
    o = {"w_in": np.ascontiguousarray(w_ext)}
    o["norm_g"] = np.ascontiguousarray(np.asarray(inp["norm_g"][0], np.float32).reshape(8, 128).T)

    def pdup(a):
        a = np.asarray(a, np.float32)
        rest = a.shape[3:]
        t = np.moveaxis(a, 2, 0).reshape(64, 64, *rest)
        t = np.concatenate([t, t], 0)
        return np.ascontiguousarray(t.reshape(128, -1))

    o["a_re"] = pdup(inp["ssm_a_re"][0])
    o["a_im"] = pdup(inp["ssm_a_im"][0])
    o["log_dt"] = np.ascontiguousarray(np.broadcast_to(np.asarray(inp["ssm_log_dt"][0], np.float32).reshape(1, 64), (128, 64)))
    o["b_re"] = pdup(inp["ssm_b_re"][0])
    o["b_im"] = pdup(inp["ssm_b_im"][0])
    o["c_re"] = pdup(np.swapaxes(np.asarray(inp["ssm_c_re"][0]), 2, 3))
    o["c_im"] = pdup(np.swapaxes(np.asarray(inp["ssm_c_im"][0]), 2, 3))
    dsk = np.asarray(inp["ssm_d"][0], np.float32).reshape(32, 16)
    o["dskip"] = np.ascontiguousarray(np.tile(dsk.T, (8, 1)))
    o["w_glu"] = np.ascontiguousarray(np.asarray(inp["w_glu"][0], np.float32))
    o["b_glu"] = np.ascontiguousarray(np.asarray(inp["b_glu"][0], np.float32).reshape(4, 128).T)
    o["w_branch"] = np.ascontiguousarray(np.asarray(inp["w_branch"][0], np.float32))
    o["w_out"] = np.ascontiguousarray(np.asarray(inp["w_out"][0], np.float32))
    o["final_g"] = np.ascontiguousarray(np.broadcast_to(np.asarray(inp["final_g"], np.float32).reshape(1, 1024), (128, 1024)))
    o["subln_g"] = np.ascontiguousarray(np.asarray(inp["subln_g"][0], np.float32).reshape(128, 1))
    lam = np.concatenate([np.asarray(inp[n][0], np.float32) for n in ("lambda_q1", "lambda_k1", "lambda_q2", "lambda_k2")])
    o["lam"] = np.ascontiguousarray(np.broadcast_to(lam.reshape(1, 256), (128, 256)))
    return o


def make_in_maps(inp, L0, L1, n_cores):
    shared = _host_consts(max(L0, L1))
    shared.update(_layout_weights(inp))
    maps = []
    for c in range(n_cores):
        m = dict(shared)
        m["x"] = np.ascontiguousarray(np.concatenate(
            [np.asarray(inp["x_prompt"][c, :L0], np.float32), np.asarray(inp["x_sample"][c, :L1], np.float32)], 0))
        maps.append(m)
    return maps


def kernel(**inputs):
    L0, L1 = 2048, 8192
    nc = build(L0, L1)
    maps = make_in_maps(inputs, L0, L1, N_CORES)
    res = run_bass_kernel_spmd(nc, maps, core_ids=list(range(N_CORES)))
    ys = [np.asarray(r["y"], np.float32) for r in res.results]
    y_prompt = np.stack([y[:L0] for y in ys], 0)
    y_sample = np.stack([y[L0:] for y in ys], 0)
    return (y_prompt, y_sample)
```

```python
from contextlib import ExitStack
import math
import numpy as np
import ml_dtypes
import concourse.bass as bass
import concourse.mybir as mybir
from concourse.bass_utils import run_bass_kernel_spmd

F32 = mybir.dt.float32
BF16 = mybir.dt.bfloat16
I32 = mybir.dt.int32
AF = mybir.ActivationFunctionType
ALU = mybir.AluOpType

NDMA = 24
ENGS = ("pe", "act", "dve", "pool", "sp")
N_CORES = 8
D = 1024
NEXT = 5120
ARENA_WORDS = 49152


class Res:
    __slots__ = ("name", "w", "r")

    def __init__(self, name=""):
        self.name = name
        self.w = None
        self.r = {}


class Prog:
    def __init__(self):
        self.ops = {e: [] for e in ENGS}
        self.cnt = {e: 0 for e in ENGS}
        self.waited = {e: {} for e in ENGS}
        self.dma_val = [0] * NDMA
        self.dma_next = 0

    def _deps(self, reads, writes):
        deps = {}
        for r in reads:
            if r.w is not None and deps.get(r.w[0], 0) < r.w[1]:
                deps[r.w[0]] = r.w[1]
        for w in writes:
            if w.w is not None and deps.get(w.w[0], 0) < w.w[1]:
                deps[w.w[0]] = w.w[1]
            for s, v in w.r.items():
                if deps.get(s, 0) < v:
                    deps[s] = v
        return deps

    def _waits(self, eng, deps):
        waits = []
        wd = self.waited[eng]
        for s, v in deps.items():
            if s == "pe" and eng == "pe":
                continue
            if wd.get(s, 0) >= v:
                continue
            wd[s] = v
            waits.append((s, v))
        return waits

    def _mark(self, tok, reads, writes):
        for r in reads:
            if r.r.get(tok[0], 0) < tok[1]:
                r.r[tok[0]] = tok[1]
        for w in writes:
            w.w = tok
            w.r = {}

    def op(self, eng, fn, reads=(), writes=()):
        waits = self._waits(eng, self._deps(reads, writes))
        n = self.cnt[eng] + 1
        self.cnt[eng] = n
        self._mark((eng, n), reads, writes)
        self.ops[eng].append((waits, fn, (eng, 1)))

    def dma(self, q, fn, reads=(), writes=()):
        s = self.dma_next
        self.dma_next = (s + 1) % NDMA
        key = ("dma", s)
        deps = self._deps(reads, writes)
        if self.dma_val[s] > 0:
            deps[key] = max(deps.get(key, 0), self.dma_val[s])
        waits = self._waits(q, deps)
        self.dma_val[s] += 16
        self._mark((key, self.dma_val[s]), reads, writes)
        self.ops[q].append((waits, fn, (key, 16)))

    def barrier(self):
        deps = {("dma", s): v for s, v in enumerate(self.dma_val) if v > 0}
        for e in ENGS:
            if e != "sp" and self.cnt[e] > 0:
                deps[e] = self.cnt[e]
        for q in ENGS:
            d = {k: v for k, v in deps.items() if k != q}
            waits = self._waits(q, d)
            if waits:
                self.ops[q].append((waits, None, None))

    def finish(self):
        for q in ("sp",):
            deps = {("dma", s): v for s, v in enumerate(self.dma_val) if v > 0}
            for e in ENGS:
                if e != q and e != "sp" and self.cnt[e] > 0:
                    deps[e] = self.cnt[e]
            waits = self._waits(q, deps)
            self.ops[q].append((waits, None, None))

    def emit(self, nc, stack):
        sems = {}
        for e in ENGS:
            sems[e] = stack.enter_context(nc.semaphore("s_" + e))
        for s in range(NDMA):
            sems[("dma", s)] = stack.enter_context(nc.semaphore("s_dma%d" % s))
        block = stack.enter_context(nc.Block())
        meth = {"pe": block.tensor, "act": block.scalar, "dve": block.vector,
                "pool": block.gpsimd, "sp": block.sync}
        for e in ENGS:
            ops = self.ops[e]

            def body(eng, ops=ops):
                for waits, fn, inc in ops:
                    for s, v in waits:
                        eng.wait_ge(sems[s], v)
                    if fn is not None:
                        ins = fn(eng)
                        if inc is not None:
                            ins.then_inc(sems[inc[0]], inc[1])
            meth[e](body)


class Arena:
    def __init__(self, ap, cap):
        self.ap = ap
        self.cap = cap
        self.off = 0

    def f32(self, n):
        off = self.off
        self.off += n
        assert self.off <= self.cap, ("arena overflow", self.off, self.cap)
        return self.ap[:, off:off + n]

    def bf16(self, n):
        w = (n + 1) // 2
        return self.f32(w).bitcast(BF16)[:, :n]


class KB:
    def __init__(self, P):
        self.P = P

    def mm(self, out, lhsT, rhs, start, stop, R, W):
        self.P.op("pe", lambda e: e.matmul(out, lhsT, rhs, start=start, stop=stop), R, W)

    def tr(self, out, in_, ident, R, W):
        self.P.op("pe", lambda e: e.transpose(out, in_, ident), R, W)

    def act(self, out, in_, func, R, W, bias=None, scale=None, accum_out=None):
        kw = {}
        if bias is not None:
            kw["bias"] = bias
        if scale is not None:
            kw["scale"] = scale
        if accum_out is not None:
            kw["accum_out"] = accum_out
        self.P.op("act", lambda e: e.activation(out=out, in_=in_, func=func, **kw), R, W)

    def ts(self, eng, out, in0, s1, s2, op0, op1, R, W):
        if op1 is None:
            self.P.op(eng, lambda e: e.tensor_scalar(out=out, in0=in0, scalar1=s1, scalar2=None, op0=op0), R, W)
        else:
            self.P.op(eng, lambda e: e.tensor_scalar(out=out, in0=in0, scalar1=s1, scalar2=s2, op0=op0, op1=op1), R, W)

    def tt(self, eng, out, in0, in1, op, R, W):
        self.P.op(eng, lambda e: e.tensor_tensor(out=out, in0=in0, in1=in1, op=op), R, W)

    def stt(self, out, in0, scalar, in1, op0, op1, R, W, accum_out=None):
        if accum_out is None:
            self.P.op("dve", lambda e: e.scalar_tensor_tensor(out=out, in0=in0, scalar=scalar, in1=in1, op0=op0, op1=op1), R, W)
        else:
            self.P.op("dve", lambda e: e.scalar_tensor_tensor(out=out, in0=in0, scalar=scalar, in1=in1, op0=op0, op1=op1,
                                                             accum_out=accum_out), R, W)

    def copy(self, eng, out, in_, R, W):
        if eng == "act":
            self.P.op("act", lambda e: e.activation(out=out, in_=in_, func=AF.Copy), R, W)
        else:
            self.P.op(eng, lambda e: e.tensor_copy(out=out, in_=in_), R, W)

    def memset(self, eng, ap, val, W):
        self.P.op(eng, lambda e: e.memset(ap, val), (), W)

    def recip(self, out, in_, R, W):
        self.P.op("dve", lambda e: e.reciprocal(out=out, in_=in_), R, W)

    def dma(self, q, out, in_, R, W, slow=False):
        q = "sp"
        if slow:
            self.P.dma(q, lambda e: e.dma_start(out=out, in_=in_, allow_slow_non_contiguous=True), R, W)
        else:
            self.P.dma(q, lambda e: e.dma_start(out=out, in_=in_), R, W)


class _Stop(Exception):
    pass


def build(L0, L1, debug=False, phases="ABTC", stop=0):
    Ls = [L0, L1]
    NTOK = L0 + L1
    LMAX = max(Ls)
    seq_off = [0, L0]
    nc = bass.Bass("TRN2", target_bir_lowering=False)

    def din(name, shape, dt=F32):
        return nc.dram_tensor(name, list(shape), dt, kind="ExternalInput").ap()

    skind = "ExternalOutput" if debug else "Internal"

    def dscr(name, shape, dt=BF16):
        return nc.dram_tensor(name, list(shape), dt, kind=skind).ap()

    x_in = din("x", [NTOK, D])
    w_in = din("w_in", [D, NEXT])
    ng_in = din("norm_g", [128, 8])
    ropeC = din("ropeC", [128, LMAX])
    ropeS = din("ropeS", [128, LMAX])
    ident_in = din("ident", [128, 128])
    jswap_in = din("jswap", [128, 128])
    pswap_in = din("pswap", [128, 128], BF16)
    esel_in = din("esel", [128, 64 * 128], BF16)
    maskf_in = din("maskf", [128, 128])
    maskb_in = din("maskb", [128, 128])
    a_re_in = din("a_re", [128, 64])
    a_im_in = din("a_im", [128, 64])
    ldt_in = din("log_dt", [128, 64])
    b_re_in = din("b_re", [128, 64 * 16])
    b_im_in = din("b_im", [128, 64 * 16])
    c_re_in = din("c_re", [128, 64 * 16])
    c_im_in = din("c_im", [128, 64 * 16])
    dsk_in = din("dskip", [128, 32])
    wglu_in = din("w_glu", [512, 512])
    bglu_in = din("b_glu", [128, 4])
    wbr_in = din("w_branch", [2, 512, 1024])
    wout_in = din("w_out", [1024, 1024])
    fg_in = din("final_g", [128, 1024])
    sg_in = din("subln_g", [128, 1])
    lam_in = din("lam", [128, 4 * 64])
    y_out = nc.dram_tensor("y", [NTOK, D], F32, kind="ExternalOutput").ap()

    xsT = dscr("xsT", [512, 8, NTOK // 8])
    szsT = dscr("szsT", [512, NTOK])
    qT = dscr("qT", [512, NTOK])
    kT = dscr("kT", [512, NTOK])
    vS = dscr("vS", [NTOK, 512])
    szaT = dscr("szaT", [512, NTOK])
    sgsT = dscr("sgsT", [1024, NTOK])
    sgaT = dscr("sgaT", [1024, NTOK])
    yT = dscr("yT", [512, NTOK])
    yaT_d = dscr("yaT", [512, NTOK])

    P = Prog()
    K = KB(P)
    NT = NTOK // 512
    R_xsT = [Res() for _ in range(NT)]
    R_szsT = [Res() for _ in range(NT)]
    R_qT = [Res() for _ in range(NT)]
    R_kT = [Res() for _ in range(NT)]
    R_v = [Res() for _ in range(NT)]
    R_sza = [Res() for _ in range(NT)]
    R_sgs = [Res() for _ in range(NT)]
    R_sga = [Res() for _ in range(NT)]
    R_yT = [Res() for _ in range(2)]
    R_ya = [Res() for _ in range(NT)]

    with ExitStack() as st:
        arena_t = st.enter_context(nc.sbuf_tensor("arena", [128, ARENA_WORDS], F32))
        psum_t = st.enter_context(nc.psum_tensor("psum", [128, 4096], F32))
        PS = psum_t[:, :]

        def bank(b, n=512):
            return PS[:, b * 512:b * 512 + n]

        R_bank = [Res("bank%d" % b) for b in range(8)]

        A = Arena(arena_t[:, :], ARENA_WORDS)
        ident32 = A.f32(128)
        identb = A.bf16(128)
        R_const = Res("const")
        K.dma("sp", ident32, ident_in, (), [R_const])
        K.copy("dve", identb, ident32, [R_const], [R_const])
        pswap = A.bf16(128)
        K.dma("sp", pswap, pswap_in, (), [R_const])
        base_off = A.off

        def ckpt(n):
            if stop == n:
                raise _Stop()

        try:
          if "A" in phases:
              A.off = base_off
              ng = A.f32(8)
              Wp = A.bf16(8 * NEXT).rearrange("p (a n) -> p a n", a=8)
              R_W = Res("W")
              R_ng = Res("ng")
              K.dma("sp", ng, ng_in, (), [R_ng])
              xt_off = A.off
              xt = A.f32(4096).rearrange("p (s d) -> p s d", s=4)
              R_xt = Res("xt")
              xn = A.bf16(4096).rearrange("p (s d) -> p s d", s=4)
              R_xn = Res("xn")
              junk = A.f32(1024)
              R_junk = Res("junk")
              xnT = [A.bf16(4096).rearrange("p (a t) -> p a t", a=8) for _ in range(2)]
              R_xnT = [Res("xnT0"), Res("xnT1")]
              cosT = [A.f32(512) for _ in range(2)]
              sinT = [A.f32(512) for _ in range(2)]
              R_tab = [Res("tab0"), Res("tab1")]
              t1 = [A.f32(512) for _ in range(2)]
              t2 = [A.f32(512) for _ in range(2)]
              R_t1 = [Res("t10"), Res("t11")]
              R_t2 = [Res("t20"), Res("t21")]
              sig = [A.f32(512) for _ in range(2)]
              qsb = [A.bf16(512) for _ in range(2)]
              R_qsb = [Res("qsb0"), Res("qsb1")]
              R_sig = [Res("sig0"), Res("sig1")]
              NSTG = 16
              stg = [A.bf16(512) for _ in range(NSTG)]
              R_stg = [Res("stg%d" % i) for i in range(NSTG)]
              ss = A.f32(4)
              ms = A.f32(4)
              rstd = A.f32(4)
              neghalf = A.f32(4)
              R_ss = Res("ss")
              R_rstd = Res("rstd")
              K.memset("dve", neghalf, -0.5, [R_rstd])
              wst = [arena_t[:, xt_off + h2 * 2048: xt_off + (h2 + 1) * 2048] for h2 in range(2)]
              R_wst = [Res("wst0"), Res("wst1")]
              w_view = w_in.rearrange("(a p) n -> p a n", p=128)
              i = 0
              for dc in range(8):
                  for cb in range(3):
                      sl = i % 2
                      c0_ = cb * 2048
                      cn_ = min(2048, NEXT - c0_)
                      K.dma("sp", wst[sl][:, 0:cn_], w_view[:, dc, c0_:c0_ + cn_], (), [R_wst[sl]])
                      eng = ("dve", "pool")[i % 2]
                      K.ts(eng, Wp[:, dc, c0_:c0_ + cn_], wst[sl][:, 0:cn_], ng[:, dc:dc + 1], None, ALU.mult, None,
                           [R_wst[sl], R_ng], [R_W])
                      i += 1

              for rw in R_wst:
                  for kk, vv in list(rw.r.items()) + ([rw.w] if rw.w else []):
                      if R_xt.r.get(kk, 0) < vv:
                          R_xt.r[kk] = vv
              ckpt(1)
              stg_i = [0]

              def stage_out(dst_ap, R_dst, produce, view3=False):
                  sl = stg_i[0] % NSTG
                  stg_i[0] += 1
                  produce(stg[sl], R_stg[sl])
                  src = stg[sl].rearrange("p (j k) -> p j k", j=8) if view3 else stg[sl]
                  K.dma("pool", dst_ap, src, [R_stg[sl]], [R_dst])

              pb_i = [0]

              def next_bank():
                  b = 4 + (pb_i[0] % 4)
                  pb_i[0] += 1
                  return b

              def load_tile(T):
                  K.dma("sp", xt, x_in[T * 512:(T + 1) * 512, :].rearrange("(s p) d -> p s d", p=128), (), [R_xt])

              def tile_pos(T):
                  tok0 = T * 512
                  sq = 0 if tok0 < L0 else 1
                  return sq, tok0 - seq_off[sq]

              def load_tabs(T):
                  sq, pos0 = tile_pos(T)
                  K.dma("sp", cosT[T % 2], ropeC[:, pos0:pos0 + 512], (), [R_tab[T % 2]])
                  K.dma("sp", sinT[T % 2], ropeS[:, pos0:pos0 + 512], (), [R_tab[T % 2]])

              load_tile(0)
              load_tabs(0)
              for T in range(NT):
                  sl = T % 2
                  for s in range(4):
                      K.stt(junk, xt[:, s, :], 1.0, xt[:, s, :], ALU.mult, ALU.mult, [R_xt], [R_junk, R_ss],
                            accum_out=ss[:, s:s + 1])
                  K.ts("dve", ms, ss, 1.0 / D, 1e-6, ALU.mult, ALU.add, [R_ss], [R_ss])
                  K.tt("pool", rstd, ms, neghalf, ALU.pow, [R_ss, R_rstd], [R_rstd])
                  for s in range(4):
                      K.act(xn[:, s, :], xt[:, s, :], AF.Copy, [R_xt, R_rstd], [R_xn], scale=rstd[:, s:s + 1])
                  if T + 1 < NT:
                      load_tile(T + 1)
                      load_tabs(T + 1)
                  for h in range(2):
                      pst = PS[:, h * 1024:(h + 1) * 1024].bitcast(BF16).rearrange("p (a t) -> p a t", a=4)
                      Rb = [R_bank[2 * h], R_bank[2 * h + 1]]
                      for a in range(4):
                          dc = h * 4 + a
                          for s in range(4):
                              K.tr(pst[:, a, s * 128:(s + 1) * 128], xn[:, s, dc * 128:(dc + 1) * 128], identb,
                                   [R_xn, R_const], Rb)
                      K.copy(("dve", "act")[h], xnT[sl][:, h * 4:(h + 1) * 4, :], pst, Rb, [R_xnT[sl]])
                  ckpt(3)
                  X = xnT[sl]
                  RX = R_xnT[sl]
                  tsl = slice(T * 512, (T + 1) * 512)

                  def fm_proj(blk):
                      b = next_bank()
                      for dc in range(8):
                          K.mm(bank(b), Wp[:, dc, blk * 128:(blk + 1) * 128], X[:, dc, :], dc == 0, dc == 7,
                               [R_W, RX], [R_bank[b]])
                      return b

                  for j in range(4):
                      b = fm_proj(j)
                      stage_out(xsT[j * 128:(j + 1) * 128, :, T * 64:(T + 1) * 64], R_xsT[T],
                                lambda s_ap, s_r, b=b: K.copy("dve", s_ap.rearrange("p (j k) -> p k j", j=8),
                                                              bank(b).rearrange("p (k j) -> p k j", j=8), [R_bank[b]], [s_r]),
                                view3=True)
                  ckpt(4)
                  for j in range(4):
                      b = fm_proj(4 + j)
                      g = j % 2
                      K.act(sig[g], bank(b), AF.Sigmoid, [R_bank[b]], [R_sig[g]])
                      stage_out(szsT[j * 128:(j + 1) * 128, tsl], R_szsT[T],
                                lambda s_ap, s_r, b=b, g=g: K.tt("dve", s_ap, bank(b), sig[g], ALU.mult,
                                                                 [R_bank[b], R_sig[g]], [s_r]))
                  ckpt(5)
                  for (base, dstT, Rd) in ((8, qT, R_qT), (12, kT, R_kT)):
                      for j in range(4):
                          b0 = fm_proj(base + j)
                          g = j % 2
                          K.copy("dve", qsb[g], bank(b0), [R_bank[b0]], [R_qsb[g]])
                          b1 = next_bank()
                          K.mm(bank(b1), pswap, qsb[g], True, True, [R_const, R_qsb[g]], [R_bank[b1]])
                          K.tt("dve", t1[g], bank(b0), cosT[sl], ALU.mult, [R_bank[b0], R_tab[sl]], [R_t1[g]])
                          K.tt("dve", t2[g], bank(b1), sinT[sl], ALU.mult, [R_bank[b1], R_tab[sl]], [R_t2[g]])
                          stage_out(dstT[j * 128:(j + 1) * 128, tsl], Rd[T],
                                    lambda s_ap, s_r, g=g: K.tt("pool", s_ap, t1[g], t2[g], ALU.add, [R_t1[g], R_t2[g]], [s_r]))
                  ckpt(6)
                  for (base, dstT, Rd) in ((16, sgsT, R_sgs), (24, sgaT, R_sga)):
                      for j in range(8):
                          b = fm_proj(base + j)
                          stage_out(dstT[j * 128:(j + 1) * 128, tsl], Rd[T],
                                    lambda s_ap, s_r, b=b: K.act(s_ap, bank(b), AF.Sigmoid, [R_bank[b]], [s_r]))
                  ckpt(7)
                  for j in range(4):
                      b = fm_proj(32 + j)
                      g = j % 2
                      K.act(sig[g], bank(b), AF.Sigmoid, [R_bank[b]], [R_sig[g]])
                      stage_out(szaT[j * 128:(j + 1) * 128, tsl], R_sza[T],
                                lambda s_ap, s_r, b=b, g=g: K.tt("dve", s_ap, bank(b), sig[g], ALU.mult,
                                                                 [R_bank[b], R_sig[g]], [s_r]))
                  for s in range(4):
                      rows = slice(T * 512 + s * 128, T * 512 + (s + 1) * 128)
                      b = next_bank()
                      for dc in range(8):
                          K.mm(bank(b), X[:, dc, s * 128:(s + 1) * 128], Wp[:, dc, 4608:5120], dc == 0, dc == 7,
                               [R_W, RX], [R_bank[b]])
                      stage_out(vS[rows, :], R_v[T],
                                lambda s_ap, s_r, b=b: K.copy("dve", s_ap, bank(b), [R_bank[b]], [s_r]))

          if "B" in phases:
              P.barrier()
              A.off = base_off
              TWO_PI = 2.0 * math.pi
              jsw = A.f32(128)
              mkf = A.f32(128)
              mkb = A.f32(128)
              dsk = A.f32(32)
              ES = A.bf16(64 * 128).rearrange("p (e n) -> p e n", e=64)
              R_cB = Res("constB")
              K.dma("sp", jsw, jswap_in, (), [R_cB])
              K.dma("sp", mkf, maskf_in, (), [R_cB])
              K.dma("sp", mkb, maskb_in, (), [R_cB])
              K.dma("sp", dsk, dsk_in, (), [R_cB])
              K.dma("sp", ES, esel_in.rearrange("p (e n) -> p e n", e=64), (), [R_cB])
              Cst_bf = A.bf16(64 * 128).rearrange("p (e n) -> p e n", e=64)
              Bst_bf = A.bf16(64 * 128).rearrange("p (e n) -> p e n", e=64)
              M_bf = A.bf16(32 * 128).rearrange("p (e n) -> p e n", e=32)
              S1tab = A.f32(10 * 64).rearrange("p (m e) -> p m e", m=10)
              S2tab = A.f32(10 * 64).rearrange("p (m e) -> p m e", m=10)
              R_fin = Res("finB")
              main_off = A.off
              R_s = Res("setupB")
              are = A.f32(64); aim = A.f32(64); ldt = A.f32(64)
              K.dma("sp", are, a_re_in, (), [R_s])
              K.dma("sp", aim, a_im_in, (), [R_s])
              K.dma("sp", ldt, ldt_in, (), [R_s])
              Bre = A.f32(1024).rearrange("p (e c) -> p e c", e=64)
              Bim = A.f32(1024).rearrange("p (e c) -> p e c", e=64)
              Cre = A.f32(1024).rearrange("p (e c) -> p e c", e=64)
              Cim = A.f32(1024).rearrange("p (e c) -> p e c", e=64)
              R_raw = Res("rawB")
              K.dma("sp", Bre, b_re_in.rearrange("p (e c) -> p e c", e=64), (), [R_raw])
              K.dma("sp", Bim, b_im_in.rearrange("p (e c) -> p e c", e=64), (), [R_raw])
              K.dma("sp", Cre, c_re_in.rearrange("p (e c) -> p e c", e=64), (), [R_raw])
              K.dma("sp", Cim, c_im_in.rearrange("p (e c) -> p e c", e=64), (), [R_raw])
              sc = lambda: A.f32(64)
              dtv = sc(); lr = sc(); ph = sc(); mag = sc(); kk = sc(); tmp = sc(); tmp2 = sc()
              r1 = sc(); r2 = sc(); sn = sc(); cs = sc(); minv = sc()
              PWre = A.f32(17 * 64).rearrange("p (n e) -> p n e", n=17)
              PWim = A.f32(17 * 64).rearrange("p (n e) -> p n e", n=17)
              w0re = sc(); w0im = sc(); den = sc()
              Rs, Ws = [R_s], [R_s]

              def v(out, in0, in1, op):
                  K.tt("dve", out, in0, in1, op, Rs, Ws)

              def vs(out, in0, s1, s2, op0, op1=None):
                  K.ts("dve", out, in0, s1, s2, op0, op1, Rs, Ws)

              K.act(dtv, ldt, AF.Exp, Rs, Ws)
              v(lr, are, dtv, ALU.mult)
              v(ph, aim, dtv, ALU.mult)
              K.act(mag, lr, AF.Exp, Rs, Ws)
              vs(kk, ph, math.pi, None, ALU.is_gt)
              for i in range(1, 9):
                  vs(tmp, ph, (2 * i + 1) * math.pi, None, ALU.is_gt)
                  v(kk, kk, tmp, ALU.add)
              vs(tmp, kk, -TWO_PI, None, ALU.mult)
              v(r1, ph, tmp, ALU.add)
              vs(r2, r1, math.pi / 2, None, ALU.add)
              vs(tmp, r2, math.pi, -TWO_PI, ALU.is_gt, ALU.mult)
              v(r2, r2, tmp, ALU.add)
              K.act(sn, r1, AF.Sin, Rs, Ws)
              K.act(cs, r2, AF.Sin, Rs, Ws)
              K.memset("dve", PWre[:, 8, :], 1.0, Ws)
              K.memset("dve", PWim[:, 8, :], 0.0, Ws)
              v(PWre[:, 9, :], mag, cs, ALU.mult)
              v(PWim[:, 9, :], mag, sn, ALU.mult)

              def cmul(ore, oim, are_, aim_, bre_, bim_):
                  v(tmp, are_, bre_, ALU.mult)
                  v(tmp2, aim_, bim_, ALU.mult)
                  v(kk, are_, bim_, ALU.mult)
                  v(den, aim_, bre_, ALU.mult)
                  v(ore, tmp, tmp2, ALU.subtract)
                  v(oim, kk, den, ALU.add)

              for n in range(2, 9):
                  cmul(PWre[:, 8 + n, :], PWim[:, 8 + n, :], PWre[:, 7 + n, :], PWim[:, 7 + n, :], PWre[:, 9, :], PWim[:, 9, :])
              for n in range(1, 9):
                  K.act(minv, lr, AF.Exp, Rs, Ws, scale=-2.0 * n)
                  v(PWre[:, 8 - n, :], PWre[:, 8 + n, :], minv, ALU.mult)
                  vs(tmp, PWim[:, 8 + n, :], -1.0, None, ALU.mult)
                  v(PWim[:, 8 - n, :], tmp, minv, ALU.mult)
              K.copy("dve", S1tab[:, 0, :], PWre[:, 16, :], Rs, Ws)
              K.copy("dve", S2tab[:, 0, :], PWim[:, 16, :], Rs, Ws)
              for m in range(1, 10):
                  v(tmp, S1tab[:, m - 1, :], S1tab[:, m - 1, :], ALU.mult)
                  v(tmp2, S2tab[:, m - 1, :], S2tab[:, m - 1, :], ALU.mult)
                  v(S1tab[:, m, :], tmp, tmp2, ALU.subtract)
                  K.stt(S2tab[:, m, :], S1tab[:, m - 1, :], 2.0, S2tab[:, m - 1, :], ALU.mult, ALU.mult, Rs, Ws)
              vs(tmp, PWre[:, 9, :], -1.0, None, ALU.add)
              v(den, are, are, ALU.mult)
              v(tmp2, aim, aim, ALU.mult)
              v(den, den, tmp2, ALU.add)
              K.recip(den, den, Rs, Ws)
              v(w0re, tmp, are, ALU.mult)
              v(tmp2, PWim[:, 9, :], aim, ALU.mult)
              v(w0re, w0re, tmp2, ALU.add)
              v(w0re, w0re, den, ALU.mult)
              v(w0im, PWim[:, 9, :], are, ALU.mult)
              v(tmp2, tmp, aim, ALU.mult)
              v(w0im, w0im, tmp2, ALU.subtract)
              v(w0im, w0im, den, ALU.mult)
              K.ts("dve", S2tab[64:128], S2tab[64:128], -1.0, None, ALU.mult, None, Rs, [R_s, R_fin])
              Tb = A.f32(1024).rearrange("p (e c) -> p e c", e=64)
              Tb2 = A.f32(1024).rearrange("p (e c) -> p e c", e=64)
              Bpre = A.f32(1024).rearrange("p (e c) -> p e c", e=64)
              Bpim = A.f32(1024).rearrange("p (e c) -> p e c", e=64)
              bc = lambda s_: s_.unsqueeze(2).to_broadcast([128, 64, 16])
              RW = [R_s, R_raw]
              K.tt("dve", Tb, Bre, bc(w0re), ALU.mult, RW, [R_raw])
              K.tt("dve", Tb2, Bim, bc(w0im), ALU.mult, RW, [R_raw])
              K.tt("dve", Bpre, Tb, Tb2, ALU.subtract, RW, [R_raw])
              K.tt("dve", Tb, Bim, bc(w0re), ALU.mult, RW, [R_raw])
              K.tt("dve", Tb2, Bre, bc(w0im), ALU.mult, RW, [R_raw])
              K.tt("dve", Bpim, Tb, Tb2, ALU.add, RW, [R_raw])
              Cst32 = A.f32(4096).rearrange("p (g j c) -> p g j c", g=32, j=8)
              Bneg32 = A.f32(4096).rearrange("p (g j c) -> p g j c", g=32, j=8)
              Bpos32 = A.f32(4096).rearrange("p (g j c) -> p g j c", g=32, j=8)
              M32 = A.f32(4096).rearrange("p (g n) -> p g n", g=32)
              T1 = A.f32(512).rearrange("p (g c) -> p g c", g=32)
              T2 = A.f32(512).rearrange("p (g c) -> p g c", g=32)
              R_bigh = [Res("bigB0"), Res("bigB1")]
              R_big = None
              R_Th = [Res("TB0"), Res("TB1")]
              R_M32 = Res("M32")
              R_T = Res("TB")

              def cscale(out4, j, MRE, MIM, n, d, ctype):
                  gs_ = slice(d * 32, (d + 1) * 32)
                  wre = PWre[:, 8 + n, gs_]
                  wim = PWim[:, 8 + n, gs_]
                  for half in range(2):
                      ps_ = slice(half * 64, (half + 1) * 64)
                      bcw = lambda w_: w_[ps_].unsqueeze(2).to_broadcast([64, 32, 16])
                      o = out4[ps_, :, j, :]
                      eng = "dve"
                      R_T = R_Th[half]
                      R_big = R_bigh[half]
                      if half == 0:
                          K.tt(eng, T1[ps_], MRE[ps_, gs_, :], bcw(wre), ALU.mult, [R_s, R_raw], [R_T])
                          K.tt(eng, T2[ps_], MIM[ps_, gs_, :], bcw(wim), ALU.mult, [R_s, R_raw], [R_T])
                          K.tt(eng, o, T1[ps_], T2[ps_], ALU.subtract, [R_T], [R_big])
                      else:
                          K.tt(eng, T1[ps_], MRE[ps_, gs_, :], bcw(wim), ALU.mult, [R_s, R_raw], [R_T])
                          K.tt(eng, T2[ps_], MIM[ps_, gs_, :], bcw(wre), ALU.mult, [R_s, R_raw], [R_T])
                          K.tt(eng, o, T1[ps_], T2[ps_], ALU.add, [R_T], [R_big])
                          if ctype:
                              K.ts(eng, o, o, -1.0, None, ALU.mult, None, [R_big], [R_big])

              rb_i = [0]

              def ringA():
                  b = rb_i[0] % 4
                  rb_i[0] += 1
                  return b

              for d in range(2):
                  for j in range(8):
                      cscale(Cst32, j, Cre, Cim, (j + 1) if d == 0 else (8 - j), d, True)
                      cscale(Bneg32, j, Bpre, Bpim, -(j + 1) if d == 0 else (j - 8), d, False)
                      cscale(Bpos32, j, Bpre, Bpim, (7 - j) if d == 0 else j, d, False)
                  K.copy("act", Cst_bf[:, d * 32:(d + 1) * 32, :], Cst32.rearrange("p g j c -> p g (j c)"), R_bigh, [R_fin])
                  Bpos_f = Bpos32.rearrange("p g j c -> p g (j c)")
                  Bneg_f = Bneg32.rearrange("p g j c -> p g (j c)")
                  Cst_f = Cst32.rearrange("p g j c -> p g (j c)")
                  for g4 in range(8):
                      b = ringA()
                      for q in range(4):
                          g = g4 * 4 + q
                          K.tr(bank(b)[:, q * 128:(q + 1) * 128], Bpos_f[:, g, :], ident32, R_bigh + [R_const], [R_bank[b]])
                      K.copy("dve", Bst_bf[:, d * 32 + g4 * 4:d * 32 + g4 * 4 + 4, :],
                             bank(b).rearrange("p (q n) -> p q n", q=4), [R_bank[b]], [R_fin])
                  for g4 in range(8):
                      b = ringA()
                      for q in range(4):
                          g = g4 * 4 + q
                          K.mm(bank(b)[:, q * 128:(q + 1) * 128], Bneg_f[:, g, :], Cst_f[:, g, :], True, True,
                               R_bigh, [R_bank[b]])
                      pv = bank(b).rearrange("p (q n) -> p q n", q=4)
                      mk = (mkf if d == 0 else mkb).unsqueeze(1).to_broadcast([128, 4, 128])
                      if d == 0:
                          K.tt("dve", M32[:, g4 * 4:g4 * 4 + 4, :], pv, mk, ALU.mult, [R_bank[b], R_cB], [R_M32])
                      else:
                          Tm = Tb.rearrange("p e c -> p (e c)")[:, 0:512].rearrange("p (q n) -> p q n", q=4)
                          K.tt("dve", Tm, pv, mk, ALU.mult, [R_bank[b], R_cB], [R_raw])
                          K.tt("dve", M32[:, g4 * 4:g4 * 4 + 4, :], M32[:, g4 * 4:g4 * 4 + 4, :], Tm, ALU.add,
                               [R_raw], [R_M32])
              for g in range(32):
                  K.stt(M32[:, g, :], ident32, dsk[:, g:g + 1], M32[:, g, :], ALU.mult, ALU.add, [R_const, R_cB], [R_M32])
              K.copy("act", M_bf, M32, [R_M32], [R_fin])

              ckpt(30)
              P.barrier()
              A.off = main_off
              KMAX = LMAX // 8
              XS = A.bf16(LMAX)
              R_XS = Res("XS")
              U = [A.bf16(KMAX) for _ in range(8)]
              R_U = [Res("U%d" % i) for i in range(8)]
              YU = [A.bf16(KMAX) for _ in range(8)]
              R_YU = [Res("YU%d" % i) for i in range(8)]
              YT = A.bf16(LMAX)
              R_YT = Res("YT")
              NS = 4
              Xs = [A.f32(KMAX) for _ in range(NS)]
              R_X = [Res("X%d" % i) for i in range(NS)]
              Hb = [A.bf16(KMAX + 2) for _ in range(NS)]
              R_H = [Res("H%d" % i) for i in range(NS)]
              NR = 8
              Rm = [A.f32(128) for _ in range(NR)]
              R_Rm = [Res("R%d" % i) for i in range(NR)]
              Rh = [A.bf16(128) for _ in range(NR)]
              Rl = [A.bf16(128) for _ in range(NR)]
              R_Rh = [Res("Rh%d" % i) for i in range(NR)]
              R_Rl = [Res("Rl%d" % i) for i in range(NR)]
              Xb = [A.bf16(KMAX) for _ in range(NS)]
              R_Xc = [[Res("X%d_%d" % (i, c_)) for c_ in range(2)] for i in range(NS)]
              R_Xbc = [[Res("Xb%d_%d" % (i, c_)) for c_ in range(2)] for i in range(NS)]
              rm_i = [0]
              rbB_i = [0]

              def ringB():
                  b = 4 + rbB_i[0] % 4
                  rbB_i[0] += 1
                  return b

              for sq in range(2):
                  L = Ls[sq]
                  Kc = L // 8
                  nstep = int(round(math.log2(Kc)))
                  tok0 = seq_off[sq]
                  CW = min(512, Kc)
                  nch = Kc // CW
                  for blk in range(4):
                      XSJ = XS[:, 0:L].rearrange("p (j k) -> p j k", j=8)
                      K.dma("sp", XSJ, xsT[blk * 128:(blk + 1) * 128, :, tok0 // 8:tok0 // 8 + Kc],
                            [R_xsT[(tok0 + t_) // 512] for t_ in range(0, L, 512)], [R_XS])
                      for gl in range(8):
                          for ch in range(nch):
                              b = ringA()
                              for j in range(8):
                                  K.mm(bank(b)[:, 0:CW], ES[:, gl * 8 + j, :], XSJ[:, j, ch * CW:(ch + 1) * CW], j == 0, j == 7,
                                       [R_cB, R_XS], [R_bank[b]])
                              K.copy(("dve", "act")[ch % 2], U[gl][:, ch * CW:(ch + 1) * CW], bank(b)[:, 0:CW], [R_bank[b]], [R_U[gl]])
                      for gp in range(4):
                          streams = [(gp * 2 + (s_ // 2), s_ % 2) for s_ in range(4)]
                          for s_, (gl, d) in enumerate(streams):
                              g = blk * 8 + gl
                              for ch in range(nch):
                                  b = ringA()
                                  K.mm(bank(b)[:, 0:CW], Bst_bf[:, d * 32 + g, :], U[gl][:, ch * CW:(ch + 1) * CW], True, True,
                                       [R_fin, R_U[gl]], [R_bank[b]])
                                  K.copy(("act", "dve")[ch % 2], Xs[s_][:, ch * CW:(ch + 1) * CW], bank(b)[:, 0:CW], [R_bank[b]], [R_Xc[s_][ch]])
                                  K.copy(("dve", "act")[ch % 2], Xb[s_][:, ch * CW:(ch + 1) * CW], Xs[s_][:, ch * CW:(ch + 1) * CW],
                                         [R_Xc[s_][ch]], [R_Xbc[s_][ch]])
                          ckpt(31)
                          def rbuild(m):
                              out = []
                              for s_, (gl, d) in enumerate(streams):
                                  g = blk * 8 + gl
                                  ri = rm_i[0] % NR
                                  rm_i[0] += 1
                                  K.ts("dve", Rm[ri], ident32, S1tab[:, m, d * 32 + g:d * 32 + g + 1], None, ALU.mult, None,
                                       [R_const, R_fin], [R_Rm[ri]])
                                  K.stt(Rm[ri], jsw, S2tab[:, m, d * 32 + g:d * 32 + g + 1], Rm[ri], ALU.mult, ALU.add,
                                        [R_cB, R_fin], [R_Rm[ri]])
                                  K.copy("act", Rh[ri], Rm[ri], [R_Rm[ri]], [R_Rh[ri]])
                                  K.tt("pool", Rl[ri], Rm[ri], Rh[ri], ALU.subtract, [R_Rm[ri], R_Rh[ri]], [R_Rl[ri]])
                                  out.append(ri)
                              return out

                          ris = rbuild(0)
                          for m in range(nstep):
                              sft = 1 << m
                              work = []
                              ris_next = rbuild(m + 1) if m + 1 < nstep else None
                              for s_, (gl, d) in enumerate(streams):
                                  ri = ris[s_]
                                  for ch in range(nch):
                                      if d == 0:
                                          lo = max(ch * CW, sft); hi = (ch + 1) * CW
                                          src = (lo - sft, hi - sft)
                                      else:
                                          lo = ch * CW; hi = min((ch + 1) * CW, Kc - sft)
                                          src = (lo + sft, hi + sft)
                                      n = hi - lo
                                      if n <= 0:
                                          continue
                                      b = ringB()
                                      srcR = [R_Xbc[s_][c_] for c_ in range(src[0] // CW, (src[1] - 1) // CW + 1)]
                                      K.mm(bank(b)[:, 0:n], Rh[ri], Xb[s_][:, src[0]:src[1]], True, False,
                                           [R_Rh[ri]] + srcR, [R_bank[b]])
                                      K.mm(bank(b)[:, 0:n], Rl[ri], Xb[s_][:, src[0]:src[1]], False, True,
                                           [R_Rl[ri]] + srcR, [R_bank[b]])
                                      work.append((s_, ch, lo, hi, n, b))
                                      if len(work) % 4 == 0:
                                          for (s2, ch2, lo2, hi2, n2, b2) in work[-4:]:
                                              K.tt("dve", Xs[s2][:, lo2:hi2], Xs[s2][:, lo2:hi2], bank(b2)[:, 0:n2], ALU.add,
                                                   [R_bank[b2]], [R_Xc[s2][ch2]])
                              rem = len(work) % 4
                              for (s2, ch2, lo2, hi2, n2, b2) in (work[-rem:] if rem else []):
                                  K.tt("dve", Xs[s2][:, lo2:hi2], Xs[s2][:, lo2:hi2], bank(b2)[:, 0:n2], ALU.add,
                                       [R_bank[b2]], [R_Xc[s2][ch2]])
                              if m + 1 < nstep:
                                  for (s2, ch2, lo2, hi2, n2, b2) in work:
                                      K.copy("act", Xb[s2][:, lo2:hi2], Xs[s2][:, lo2:hi2], [R_Xc[s2][ch2]], [R_Xbc[s2][ch2]])
                                  ris = ris_next
                          for s_, (gl, d) in enumerate(streams):
                              if d == 0:
                                  K.memset("pool", Hb[s_][:, 0:1], 0.0, [R_H[s_]])
                                  K.copy("act", Hb[s_][:, 1:Kc + 1], Xs[s_][:, 0:Kc], R_Xc[s_], [R_H[s_]])
                              else:
                                  K.memset("pool", Hb[s_][:, Kc:Kc + 1], 0.0, [R_H[s_]])
                                  K.copy("act", Hb[s_][:, 0:Kc], Xs[s_][:, 0:Kc], R_Xc[s_], [R_H[s_]])
                          for q in range(2):
                              gl = gp * 2 + q
                              g = blk * 8 + gl
                              sf, sb = q * 2, q * 2 + 1
                              for ch in range(nch):
                                  b = ringA()
                                  cs_ = slice(ch * CW, (ch + 1) * CW)
                                  K.mm(bank(b)[:, 0:CW], M_bf[:, g, :], U[gl][:, cs_], True, False, [R_fin, R_U[gl]], [R_bank[b]])
                                  K.mm(bank(b)[:, 0:CW], Cst_bf[:, g, :], Hb[sf][:, ch * CW:ch * CW + CW], False, False,
                                       [R_fin, R_H[sf]], [R_bank[b]])
                                  K.mm(bank(b)[:, 0:CW], Cst_bf[:, 32 + g, :], Hb[sb][:, ch * CW + 1:ch * CW + CW + 1], False, True,
                                       [R_fin, R_H[sb]], [R_bank[b]])
                                  K.copy(("dve", "act")[ch % 2], YU[gl][:, cs_], bank(b)[:, 0:CW], [R_bank[b]], [R_YU[gl]])
                      YT3 = YT[:, 0:L].rearrange("p (k j) -> p k j", j=8)
                      for j in range(8):
                          for ch in range(nch):
                              b = ringA()
                              for gl in range(8):
                                  K.mm(bank(b)[:, 0:CW], ES[:, j * 8 + gl, :], YU[gl][:, ch * CW:(ch + 1) * CW], gl == 0, gl == 7,
                                       [R_cB, R_YU[gl]], [R_bank[b]])
                              K.copy(("dve", "act")[(j + ch) % 2], YT3[:, ch * CW:(ch + 1) * CW, j], bank(b)[:, 0:CW], [R_bank[b]], [R_YT])
                      K.dma("sp", yT[blk * 128:(blk + 1) * 128, tok0:tok0 + L], YT[:, 0:L], [R_YT], [R_yT[sq]])

          if "T" in phases:
              P.barrier()
              A.off = base_off
              lamt = A.f32(256)
              sgcol = A.f32(1)
              R_cT = Res("constT")
              K.dma("sp", lamt, lam_in, (), [R_cT])
              K.dma("sp", sgcol, sg_in, (), [R_cT])
              jk = A.f32(64)
              s12 = A.f32(2)
              e12 = A.f32(2)
              lamneg = A.f32(1)
              ones32 = A.f32(128)
              onesb = A.bf16(128)
              K.memset("dve", ones32, 1.0, [R_cT])
              K.memset("dve", onesb, 1.0, [R_cT])
              selt = A.f32(128)
              K.memset("dve", selt, 0.0, [R_cT])
              K.memset("dve", selt[0:1, :], 1.0, [R_cT])
              K.memset("dve", selt[32:33, :], 1.0, [R_cT])
              lsb = A.f32(512)
              lsb1 = A.f32(512)
              R_lc = Res("lc")
              oc0 = A.f32(512)
              oc1 = A.f32(512)
              K.stt(jk, lamt[:, 0:64], 1.0, lamt[:, 64:128], ALU.mult, ALU.mult, [R_cT], [R_cT], accum_out=s12[:, 0:1])
              K.stt(jk, lamt[:, 128:192], 1.0, lamt[:, 192:256], ALU.mult, ALU.mult, [R_cT], [R_cT], accum_out=s12[:, 1:2])
              K.act(e12, s12, AF.Exp, [R_cT], [R_cT])
              K.tt("dve", lamneg, e12[:, 1:2], e12[:, 0:1], ALU.subtract, [R_cT], [R_cT])
              K.ts("dve", lamneg, lamneg, -0.2, None, ALU.add, None, [R_cT], [R_cT])
              K.ts("dve", sgcol, sgcol, 0.8, None, ALU.mult, None, [R_cT], [R_cT])
              NCK = LMAX // 128
              KTb = [A.bf16(LMAX) for _ in range(2)]
              QTb = [A.bf16(LMAX) for _ in range(2)]
              Vb = [A.bf16(NCK * 128).rearrange("p (c e) -> p c e", e=128) for _ in range(2)]
              R_KT = [Res("KT0"), Res("KT1")]
              R_QT = [Res("QT0"), Res("QT1")]
              R_V = [Res("V0"), Res("V1")]
              NPT = 4
              PT = [A.bf16(1024) for _ in range(NPT)]
              R_PT = [Res("PT%d" % i) for i in range(NPT)]
              PS2 = [A.bf16(1024) for _ in range(2)]
              R_PS2 = [Res("PS2_0"), Res("PS2_1")]
              p2_i = [0]
              sums_q = []
              PS4 = [A.bf16(1024) for _ in range(2)]
              R_PS4 = [Res("PS4_0"), Res("PS4_1")]
              p4_i = [0]
              szat = [A.bf16(512) for _ in range(2)]
              R_szat = [Res("szat0"), Res("szat1")]
              yst = [A.bf16(512) for _ in range(2)]
              R_yst = [Res("yst0"), Res("yst1")]
              rl0 = A.f32(512); rl1 = A.f32(512); o0 = A.f32(512); o1 = A.f32(512); oo = A.f32(512); sqt = A.f32(512)
              Dm = [A.f32(128) for _ in range(4)]
              R_Dm = [Res("Dm%d" % i_) for i_ in range(4)]
              sm = A.f32(16)
              nh = A.f32(4)
              R_fz = Res("finz")
              R_sq = Res("sq")
              K.memset("dve", nh, -0.5, [R_fz])
              QW = 512
              heads = [(sq, h) for sq in range(2) for h in range(4)]
              ckpt(20)

              def load_head(i):
                  sq, h = heads[i]
                  L = Ls[sq]; tok0 = seq_off[sq]; sl = i % 2
                  tiles = [(tok0 + t_) // 512 for t_ in range(0, L, 512)]
                  K.dma("sp", KTb[sl][:, 0:L], kT[h * 128:(h + 1) * 128, tok0:tok0 + L], [R_kT[t_] for t_ in tiles], [R_KT[sl]])
                  K.dma("sp", QTb[sl][:, 0:L], qT[h * 128:(h + 1) * 128, tok0:tok0 + L], [R_qT[t_] for t_ in tiles], [R_QT[sl]])
                  K.dma("sp", Vb[sl][:, 0:L // 128, :],
                        vS[tok0:tok0 + L, h * 128:(h + 1) * 128].rearrange("(c p) e -> p c e", p=128),
                        [R_v[t_] for t_ in tiles], [R_V[sl]])

              sb_i = [0]
              pt_i = [0]
              qt_i = [0]
              dm_i = [0]
              load_head(0)
              pending = []
              for hi, (sq, h) in enumerate(heads):
                  L = Ls[sq]; tok0 = seq_off[sq]; sl = hi % 2
                  if hi + 1 < len(heads):
                      load_head(hi + 1)
                  KT = KTb[sl]; QT = QTb[sl]; V = Vb[sl]
                  nck = L // 128
                  for q0 in range(0, L, QW):
                      qs = qt_i[0] % 2
                      qt_i[0] += 1
                      trow = tok0 + q0
                      K.dma("sp", szat[qs], szaT[h * 128:(h + 1) * 128, trow:trow + QW], [R_sza[trow // 512]], [R_szat[qs]])

                      def qk(c):
                          b = 2 * (sb_i[0] % 2)
                          sb_i[0] += 1
                          for t in range(2):
                              K.mm(bank(b + t), KT[t * 64:(t + 1) * 64, c * 128:(c + 1) * 128],
                                   QT[t * 64:(t + 1) * 64, q0:q0 + QW], True, True, [R_KT[sl], R_QT[sl]], [R_bank[b + t]])
                          return b

                      if sb_i[0] % 2 == 1:
                          sb_i[0] += 1
                      slot_b = [qk(0), qk(1) if nck > 1 else 0]
                      for c in range(nck):
                          b = slot_b[c % 2]
                          pi = pt_i[0] % NPT
                          pt_i[0] += 1
                          K.act(PT[pi], PS[:, b * 512:b * 512 + 1024], AF.Exp, [R_bank[b], R_bank[b + 1]], [R_PT[pi]], scale=0.125)
                          for (cc, fn_) in pending:
                              if cc == c:
                                  fn_()
                          pending = [pf for pf in pending if pf[0] > c]
                          if c + 2 < nck:
                              slot_b[c % 2] = qk(c + 2)
                          for t in range(2):
                              K.mm(bank(4 + t), V[:, c, :], PT[pi][:, t * 512:(t + 1) * 512], c == 0, c == nck - 1,
                                   [R_PT[pi], R_V[sl]], [R_bank[4 + t]])
                          if sums_q:
                              sums_q.pop(0)()
                          if c % 2 == 1:
                              k2 = p2_i[0] % 2
                              p2_i[0] += 1
                              pprev = (pt_i[0] - 2) % NPT
                              K.tt("dve", PS2[k2], PT[pprev], PT[pi], ALU.add, [R_PT[pprev], R_PT[pi]], [R_PS2[k2]])
                              if c % 4 == 3:
                                  k4 = p4_i[0] % 2
                                  p4_i[0] += 1
                                  K.tt("dve", PS4[k4], PS2[1 - k2], PS2[k2], ALU.add, [R_PS2[0], R_PS2[1]], [R_PS4[k4]])

                                  def do_sums(k4=k4, first=(c == 3), last=(c == nck - 1)):
                                      for t in range(2):
                                          K.mm(bank(6 + t), onesb, PS4[k4][:, t * 512:(t + 1) * 512], first, last,
                                               [R_PS4[k4], R_cT], [R_bank[6 + t]])
                                  sums_q.append(do_sums)
                      while sums_q:
                          sums_q.pop(0)()
                      for (cc, fn_) in pending:
                          fn_()
                      pending = []
                      K.copy("dve", oc0, bank(4), [R_bank[4]], [R_fz])
                      K.copy("dve", oc1, bank(5), [R_bank[5]], [R_fz])
                      K.copy("dve", lsb, bank(6), [R_bank[6]], [R_lc])
                      K.copy("dve", lsb1, bank(7), [R_bank[7]], [R_lc])

                      def mk_stages(qs=qs, h=h, trow=trow):
                          def st0a():
                              K.recip(rl0, lsb, [R_lc], [R_fz])

                          def st0b():
                              K.recip(rl1, lsb1, [R_lc], [R_fz])

                          def st1():
                              K.tt("dve", o0, oc0, rl0, ALU.mult, [R_fz], [R_fz])
                              K.tt("dve", o1, oc1, rl1, ALU.mult, [R_fz], [R_fz])
                              K.stt(oo, o1, lamneg[:, 0:1], o0, ALU.mult, ALU.add, [R_fz, R_cT], [R_fz])
                              K.tt("dve", sqt, oo, oo, ALU.mult, [R_fz], [R_sq])

                          def st2():
                              b = 2 * (sb_i[0] % 2)
                              for s4 in range(4):
                                  K.mm(bank(b)[:, s4:s4 + 1], sqt[:, s4 * 128:(s4 + 1) * 128], ones32[:, 0:1], True, True,
                                       [R_sq, R_cT], [R_bank[b]])
                              K.ts("dve", sm[:, 0:4], bank(b)[:, 0:4], 1.0 / 128, 1e-5, ALU.mult, ALU.add, [R_bank[b]], [R_fz])
                              K.tt("pool", sm[:, 4:8], sm[:, 0:4], nh, ALU.pow, [R_fz], [R_fz])

                          def st3():
                              for s4 in range(4):
                                  di = dm_i[0] % 4
                                  dm_i[0] += 1
                                  K.ts("dve", Dm[di], ident32, sm[:, 4 + s4:5 + s4], None, ALU.mult, None, [R_const, R_fz], [R_Dm[di]])

                          def st4():
                              b = 2 * (sb_i[0] % 2)
                              for s4 in range(4):
                                  di = (dm_i[0] - 4 + s4) % 4
                                  K.mm(bank(b)[:, s4 * 128:(s4 + 1) * 128], ones32, Dm[di], True, True, [R_Dm[di], R_cT], [R_bank[b]])
                              K.tt("dve", o0, oo, bank(b), ALU.mult, [R_bank[b], R_fz], [R_fz])
                              K.stt(yst[qs], o0, sgcol[:, 0:1], szat[qs], ALU.mult, ALU.mult, [R_fz, R_cT, R_szat[qs]], [R_yst[qs]])
                              K.dma("sp", yaT_d[h * 128:(h + 1) * 128, trow:trow + QW], yst[qs], [R_yst[qs]], [R_ya[trow // 512]])
                          return [(1, st0a), (3, st0b), (5, st1), (9, st2), (10, st3), (13, st4)]

                      pending = mk_stages()
                      ckpt(25)
                      ckpt(100 + qt_i[0])
              for (cc, fn_) in pending:
                  fn_()
              pending = []

          if "C" in phases:
              P.barrier()
              A.off = base_off
              wglu = A.bf16(4 * 512).rearrange("p (a n) -> p a n", a=4)
              wbr = A.bf16(8 * 1024).rearrange("p (a n) -> p a n", a=8)
              wout = A.bf16(8 * 1024).rearrange("p (a n) -> p a n", a=8)
              bglu = A.f32(4)
              fgt = A.f32(1024)
              R_wC = Res("wC")
              K.dma("sp", bglu, bglu_in, (), [R_wC])
              K.dma("sp", fgt, fg_in, (), [R_wC])
              nhc = A.f32(4)
              K.memset("dve", nhc, -0.5, [R_wC])
              ytb = [A.bf16(2048).rearrange("p (a t) -> p a t", a=4) for _ in range(2)]
              szb = [A.bf16(2048).rearrange("p (a t) -> p a t", a=4) for _ in range(2)]
              sgsb = [A.bf16(4096).rearrange("p (a t) -> p a t", a=8) for _ in range(2)]
              sgab = [A.bf16(4096).rearrange("p (a t) -> p a t", a=8) for _ in range(2)]
              yab = [A.bf16(2048).rearrange("p (a t) -> p a t", a=4) for _ in range(2)]
              xtb = [A.f32(4096).rearrange("p (s d) -> p s d", s=4) for _ in range(2)]
              R_ld = [Res("ldC0"), Res("ldC1")]
              gbf = A.bf16(2048).rearrange("p (a t) -> p a t", a=4)
              ys3 = A.bf16(2048).rearrange("p (a t) -> p a t", a=4)
              yaT = A.bf16(2048).rearrange("p (a t) -> p a t", a=4)
              mT = A.bf16(4096).rearrange("p (a t) -> p a t", a=8)
              R_g = Res("g"); R_ys3 = Res("ys3"); R_yaT = Res("yaT"); R_mT = Res("mT")
              tA = [A.f32(512) for _ in range(2)]
              tB = [A.f32(512) for _ in range(2)]
              R_tA = [Res("tA0"), Res("tA1")]
              R_tB = [Res("tB0"), Res("tB1")]
              resb = [A.f32(1024) for _ in range(2)]
              outb = [A.f32(1024) for _ in range(2)]
              R_res = [Res("res0"), Res("res1")]
              R_outb = [Res("out0"), Res("out1")]
              junkc = A.f32(1024)
              R_jc = Res("junkc")
              smc = A.f32(8)
              R_smc = Res("smc")
              wstg = [xtb[1].rearrange("p s d -> p (s d)")[:, i * 2048:(i + 1) * 2048] for i in range(2)]
              R_wstg = [Res("wstgC0"), Res("wstgC1")]
              wi = [0]

              def wload(dst, src_ap):
                  n = dst.shape[-1]
                  sl = wi[0] % 2
                  wi[0] += 1
                  K.dma("sp", wstg[sl][:, 0:n], src_ap, (), [R_wstg[sl]])
                  K.copy(("dve", "pool")[sl], dst, wstg[sl][:, 0:n], [R_wstg[sl]], [R_wC])

              wg_v = wglu_in.rearrange("(a p) n -> p a n", p=128)
              for a in range(4):
                  wload(wglu[:, a, :], wg_v[:, a, :])
              wb_v = wbr_in.rearrange("b (a p) n -> p b a n", p=128)
              for br in range(2):
                  for a in range(4):
                      wload(wbr[:, br * 4 + a, :], wb_v[:, br, a, :])
              wo_v = wout_in.rearrange("(a p) n -> p a n", p=128)
              for a in range(8):
                  wload(wout[:, a, :], wo_v[:, a, :])
              for rw in R_wstg:
                  for kk_, vv_ in list(rw.r.items()) + ([rw.w] if rw.w else []):
                      if R_ld[1].r.get(kk_, 0) < vv_:
                          R_ld[1].r[kk_] = vv_

              def loadC(T):
                  sl = T % 2
                  ts_ = slice(T * 512, (T + 1) * 512)
                  W_ = [R_ld[sl]]
                  K.dma("sp", ytb[sl], yT[:, ts_].rearrange("(a p) t -> p a t", p=128), [R_yT[0], R_yT[1]], W_)
                  K.dma("sp", szb[sl], szsT[:, ts_].rearrange("(a p) t -> p a t", p=128), [R_szsT[T]], W_)
                  K.dma("sp", sgsb[sl], sgsT[:, ts_].rearrange("(a p) t -> p a t", p=128), [R_sgs[T]], W_)
                  K.dma("sp", sgab[sl], sgaT[:, ts_].rearrange("(a p) t -> p a t", p=128), [R_sga[T]], W_)
                  K.dma("sp", yab[sl], yaT_d[:, ts_].rearrange("(a p) t -> p a t", p=128), [R_ya[T]], W_)
                  K.dma("sp", xtb[sl], x_in[ts_, :].rearrange("(s p) d -> p s d", p=128), (), W_)

              rc_i = [0]

              def ringC():
                  b = rc_i[0] % 8
                  rc_i[0] += 1
                  return b

              loadC(0)
              for T in range(NT):
                  sl = T % 2
                  if T + 1 < NT:
                      loadC(T + 1)
                  RL = [R_ld[sl]]
                  yt = ytb[sl]
                  for a in range(4):
                      g_ = a % 2
                      K.tt("dve", tA[g_], yt[:, a, :], yt[:, a, :], ALU.mult, RL, [R_tA[g_]])
                      K.ts("dve", tA[g_], tA[g_], 0.044715, 1.0, ALU.mult, ALU.add, [R_tA[g_]], [R_tA[g_]])
                      K.tt("pool", tA[g_], tA[g_], yt[:, a, :], ALU.mult, RL + [R_tA[g_]], [R_tA[g_]])
                      K.act(tB[g_], tA[g_], AF.Sigmoid, [R_tA[g_]], [R_tB[g_]], scale=1.5957691216057308)
                      K.tt("dve", gbf[:, a, :], yt[:, a, :], tB[g_], ALU.mult, RL + [R_tB[g_]], [R_g])
                  for e_ in range(4):
                      b = ringC()
                      for a in range(4):
                          K.mm(bank(b), wglu[:, a, e_ * 128:(e_ + 1) * 128], gbf[:, a, :], a == 0, a == 3, [R_wC, R_g], [R_bank[b]])
                      g_ = e_ % 2
                      K.act(tB[g_], bank(b), AF.Sigmoid, [R_bank[b], R_wC], [R_tB[g_]], bias=bglu[:, e_:e_ + 1])
                      K.tt("dve", tA[g_], gbf[:, e_, :], tB[g_], ALU.mult, [R_g, R_tB[g_]], [R_tA[g_]])
                      K.tt("pool", ys3[:, e_, :], tA[g_], szb[sl][:, e_, :], ALU.mult, RL + [R_tA[g_]], [R_ys3])
                  for e_ in range(8):
                      bs = ringC()
                      for a in range(4):
                          K.mm(bank(bs), wbr[:, a, e_ * 128:(e_ + 1) * 128], ys3[:, a, :], a == 0, a == 3, [R_wC, R_ys3], [R_bank[bs]])
                      ba = ringC()
                      for a in range(4):
                          K.mm(bank(ba), wbr[:, 4 + a, e_ * 128:(e_ + 1) * 128], yab[sl][:, a, :], a == 0, a == 3, [R_wC] + RL, [R_bank[ba]])
                      g_ = e_ % 2
                      K.tt("dve", tA[g_], bank(bs), sgsb[sl][:, e_, :], ALU.mult, RL + [R_bank[bs]], [R_tA[g_]])
                      K.tt("dve", tB[g_], bank(ba), sgab[sl][:, e_, :], ALU.mult, RL + [R_bank[ba]], [R_tB[g_]])
                      K.tt("pool", mT[:, e_, :], tA[g_], tB[g_], ALU.add, [R_tA[g_], R_tB[g_]], [R_mT])
                  for s in range(4):
                      rs = s % 2
                      for hf in range(2):
                          b = ringC()
                          for a in range(8):
                              K.mm(bank(b), mT[:, a, s * 128:(s + 1) * 128], wout[:, a, hf * 512:(hf + 1) * 512], a == 0, a == 7,
                                   [R_wC, R_mT], [R_bank[b]])
                          K.tt("dve", resb[rs][:, hf * 512:(hf + 1) * 512], bank(b), xtb[sl][:, s, hf * 512:(hf + 1) * 512], ALU.add,
                               RL + [R_bank[b]], [R_res[rs]])
                      K.act(junkc, resb[rs], AF.Square, [R_res[rs]], [R_jc, R_smc], accum_out=smc[:, 0:1])
                      K.ts("dve", smc[:, 1:2], smc[:, 0:1], 1.0 / D, 1e-6, ALU.mult, ALU.add, [R_smc], [R_smc])
                      K.tt("pool", smc[:, 2:3], smc[:, 1:2], nhc[:, 0:1], ALU.pow, [R_smc, R_wC], [R_smc])
                      K.act(junkc, resb[rs], AF.Copy, [R_res[rs], R_smc], [R_jc], scale=smc[:, 2:3])
                      K.tt("pool", outb[rs], junkc, fgt, ALU.mult, [R_jc, R_wC], [R_outb[rs]])
                      rows = slice(T * 512 + s * 128, T * 512 + (s + 1) * 128)
                      K.dma("sp", y_out[rows, :], outb[rs], [R_outb[rs]], ())

        except _Stop:
            pass
        P.finish()
        P.emit(nc, st)
    return nc


def _rope_tables(lmax):
    half = 32
    inv = (1.0 / (10000.0 ** (np.arange(half, dtype=np.float32) * 2.0 / 64))).astype(np.float32)
    ang = (np.arange(lmax, dtype=np.float32)[:, None] * inv[None, :]).astype(np.float32)
    c = np.cos(ang.astype(np.float64)); s = np.sin(ang.astype(np.float64))
    C = np.zeros((128, lmax), np.float32); S = np.zeros((128, lmax), np.float32)
    for t in range(2):
        for d in range(64):
            C[t * 64 + d] = c[:, d % 32]
            S[t * 64 + d] = (-1.0 if d < 32 else 1.0) * s[:, d % 32]
    return C, S


def _host_consts(lmax):
    C, S = _rope_tables(lmax)
    ident = np.eye(128, dtype=np.float32)
    jswap = np.zeros((128, 128), np.float32)
    for k in range(128):
        jswap[k, (k + 64) % 128] = 1.0
    esel = np.zeros((8, 8, 128, 128), np.float32)
    for a in range(8):
        for b in range(8):
            for c in range(16):
                esel[a, b, a * 16 + c, b * 16 + c] = 1.0
    esel = esel.reshape(64, 128, 128).transpose(1, 0, 2).reshape(128, 64 * 128).astype(ml_dtypes.bfloat16)
    jj = np.arange(128) // 16
    maskf = (jj[:, None] <= jj[None, :]).astype(np.float32)
    maskb = (jj[:, None] >= jj[None, :]).astype(np.float32)
    pswap = np.zeros((128, 128), np.float32)
    for m in range(128):
        pswap[(m // 64) * 64 + ((m % 64) + 32) % 64, m] = 1.0
    pswap = pswap.astype(ml_dtypes.bfloat16)
    return dict(ropeC=C, ropeS=S, ident=ident, jswap=jswap, pswap=pswap, esel=np.ascontiguousarray(esel), maskf=maskf, maskb=maskb)


def _layout_weights(inp):
    w = np.asarray(inp["w_in"][0], np.float32)
    xs, zs, q, k, v, za = [w[:, i * 512:(i + 1) * 512] for i in range(6)]
    gs = w[:, 3072:4096]; ga = w[:, 4096:5120]
    f = np.arange(512)
    swap = (f // 64) * 64 + ((f % 64) + 32) % 64
    w_ext = np.concatenate([xs, zs, q, k, gs, ga, za, v], axis=1)
    o = {"w_in": np.ascontiguousarray(w_ext)}
    o["norm_g"] = np.ascontiguousarray(np.asarray(inp["norm_g"][0], np.float32).reshape(8, 128).T)

    def pdup(a):
        a = np.asarray(a, np.float32)
        rest = a.shape[3:]
        t = np.moveaxis(a, 2, 0).reshape(64, 64, *rest)
        t = np.concatenate([t, t], 0)
        return np.ascontiguousarray(t.reshape(128, -1))

    o["a_re"] = pdup(inp["ssm_a_re"][0])
    o["a_im"] = pdup(inp["ssm_a_im"][0])
    o["log_dt"] = np.ascontiguousarray(np.broadcast_to(np.asarray(inp["ssm_log_dt"][0], np.float32).reshape(1, 64), (128, 64)))
    o["b_re"] = pdup(inp["ssm_b_re"][0])
    o["b_im"] = pdup(inp["ssm_b_im"][0])
    o["c_re"] = pdup(np.swapaxes(np.asarray(inp["ssm_c_re"][0]), 2, 3))
    o["c_im"] = pdup(np.swapaxes(np.asarray(inp["ssm_c_im"][0]), 2, 3))
    dsk = np.asarray(inp["ssm_d"][0], np.float32).reshape(32, 16)
    o["dskip"] = np.ascontiguousarray(np.tile(dsk.T, (8, 1)))
    o["w_glu"] = np.ascontiguousarray(np.asarray(inp["w_glu"][0], np.float32))
    o["b_glu"] = np.ascontiguousarray(np.asarray(inp["b_glu"][0], np.float32).reshape(4, 128).T)
    o["w_branch"] = np.ascontiguousarray(np.asarray(inp["w_branch"][0], np.float32))
    o["w_out"] = np.ascontiguousarray(np.asarray(inp["w_out"][0], np.float32))
    o["final_g"] = np.ascontiguousarray(np.broadcast_to(np.asarray(inp["final_g"], np.float32).reshape(1, 1024), (128, 1024)))
    o["subln_g"] = np.ascontiguousarray(np.asarray(inp["subln_g"][0], np.float32).reshape(128, 1))
    lam = np.concatenate([np.asarray(inp[n][0], np.float32) for n in ("lambda_q1", "lambda_k1", "lambda_q2", "lambda_k2")])
    o["lam"] = np.ascontiguousarray(np.broadcast_to(lam.reshape(1, 256), (128, 256)))
    return o


def make_in_maps(inp, L0, L1, n_cores):
    shared = _host_consts(max(L0, L1))
    shared.update(_layout_weights(inp))
    maps = []
    for c in range(n_cores):
        m = dict(shared)
        m["x"] = np.ascontiguousarray(np.concatenate(
            [np.asarray(inp["x_prompt"][c, :L0], np.float32), np.asarray(inp["x_sample"][c, :L1], np.float32)], 0))
        maps.append(m)
    return maps


def kernel(**inputs):
    L0, L1 = 2048, 8192
    nc = build(L0, L1)
    maps = make_in_maps(inputs, L0, L1, N_CORES)
    res = run_bass_kernel_spmd(nc, maps, core_ids=list(range(N_CORES)))
    ys = [np.asarray(r["y"], np.float32) for r in res.results]
    y_prompt = np.stack([y[:L0] for y in ys], 0)
    y_sample = np.stack([y[L0:] for y in ys], 0)
    return (y_prompt, y_sample)
```
